# Optimizing a Trainium2 kernel written in Bass

```python
import math
import jax
import jax.numpy as jnp
from jax import lax
import numpy as np

D_MODEL = 1024
BATCH = 8
SEQ = 2048
DEPTH = 2

S5_WIDTH = D_MODEL
S5_GROUP = 16
S5_GROUPS = S5_WIDTH // S5_GROUP
S5_STATE = 64
SSD_HEAD_DIM = 64
SSD_WIDTH = D_MODEL
SSD_HEADS = SSD_WIDTH // SSD_HEAD_DIM
SSD_GROUPS = 2
SSD_STATE = 128
SSD_CONV = 4
SSD_CHUNK = 128
SSD_CONV_DIM = SSD_WIDTH + 2 * SSD_GROUPS * SSD_STATE
MIX_WIDTH = S5_WIDTH + SSD_WIDTH
IN_PROJ = S5_WIDTH + SSD_WIDTH + SSD_CONV_DIM + SSD_HEADS
IN_SPLITS = (S5_WIDTH, S5_WIDTH + SSD_WIDTH, S5_WIDTH + SSD_WIDTH + SSD_CONV_DIM)
FFN_HIDDEN = ((8 * D_MODEL + 3 * 256 - 1) // (3 * 256)) * 256
EPS = 1e-6

kernel_name = 'hybrid_s5_ssd_parallel_heads'


def rmsnorm(x, g):
    xf = x.astype(jnp.float32)
    y = xf * lax.rsqrt(jnp.mean(xf * xf, axis=-1, keepdims=True) + EPS)
    return (y * g.astype(jnp.float32)).astype(x.dtype)


def s5_mixer(u, lam_re, lam_im, log_step, b_re, b_im, c_re, c_im, d_skip, w_glu, b_glu):
    bsz, L, _ = u.shape
    f32 = jnp.float32
    uf = u.astype(f32).reshape(bsz, L, S5_GROUPS, S5_GROUP)
    step = jnp.exp(log_step.astype(f32))[:, None]
    lr = lam_re.astype(f32)
    li = lam_im.astype(f32)
    mag = jnp.exp(lr * step)
    ang = li * step
    abar_re = mag * jnp.cos(ang)
    abar_im = mag * jnp.sin(ang)
    den = lr * lr + li * li
    nr = abar_re - 1.0
    ni = abar_im
    coef_re = (nr * lr + ni * li) / den
    coef_im = (ni * lr - nr * li) / den
    bre = b_re.astype(f32)
    bim = b_im.astype(f32)
    bbar_re = coef_re[..., None] * bre - coef_im[..., None] * bim
    bbar_im = coef_re[..., None] * bim + coef_im[..., None] * bre
    bu_re = jnp.einsum('blgh,gph->blgp', uf, bbar_re)
    bu_im = jnp.einsum('blgh,gph->blgp', uf, bbar_im)
    a_re = jnp.broadcast_to(abar_re, (1, L, S5_GROUPS, S5_STATE))
    a_im = jnp.broadcast_to(abar_im, (1, L, S5_GROUPS, S5_STATE))

    def combine(e1, e2):
        a1r, a1i, b1r, b1i = e1
        a2r, a2i, b2r, b2i = e2
        return (a2r * a1r - a2i * a1i,
                a2r * a1i + a2i * a1r,
                a2r * b1r - a2i * b1i + b2r,
                a2r * b1i + a2i * b1r + b2i)

    _, _, xr, xi = lax.associative_scan(combine, (a_re, a_im, bu_re, bu_im), axis=1)
    y = (jnp.einsum('blgp,ghp->blgh', xr, c_re.astype(f32))
         - jnp.einsum('blgp,ghp->blgh', xi, c_im.astype(f32)))
    y = y.reshape(bsz, L, S5_WIDTH) + d_skip.astype(f32) * uf.reshape(bsz, L, S5_WIDTH)
    g = jax.nn.gelu(y)
    out = g * jax.nn.sigmoid(g @ w_glu.astype(f32) + b_glu.astype(f32))
    return out.astype(u.dtype)


def segsum(a):
    T = a.shape[-1]
    rep = jnp.broadcast_to(a[..., None], a.shape + (T,))
    strict = jnp.tril(jnp.ones((T, T), dtype=bool), -1)
    cs = jnp.cumsum(jnp.where(strict, rep, 0.0), axis=-2)
    incl = jnp.tril(jnp.ones((T, T), dtype=bool))
    return jnp.where(incl, cs, -jnp.inf)


def ssd_mixer(z, xbc, dt, conv_w, conv_b, dt_bias, a_log, d_skip):
    bsz, L, _ = xbc.shape
    f32 = jnp.float32
    nc = L // SSD_CHUNK
    R = SSD_HEADS // SSD_GROUPS
    xbc = lax.conv_general_dilated(
        xbc, conv_w[:, None, :].astype(xbc.dtype), window_strides=(1,),
        padding=[(SSD_CONV - 1, 0)], dimension_numbers=('NWC', 'WIO', 'NWC'),
        feature_group_count=SSD_CONV_DIM) + conv_b
    xbc = jax.nn.silu(xbc).astype(f32)
    xs, bs, cs = jnp.split(xbc, (SSD_WIDTH, SSD_WIDTH + SSD_GROUPS * SSD_STATE), axis=-1)
    dtp = jax.nn.softplus(dt.astype(f32) + dt_bias.astype(f32))
    A = -jnp.exp(a_log.astype(f32))
    dta = (dtp * A).reshape(bsz, nc, SSD_CHUNK, SSD_GROUPS, R).transpose(0, 3, 4, 1, 2)
    xh = xs.reshape(bsz, L, SSD_HEADS, SSD_HEAD_DIM)
    xdt = (xh * dtp[..., None]).reshape(bsz, nc, SSD_CHUNK, SSD_GROUPS, R, SSD_HEAD_DIM)
    bmat = bs.reshape(bsz, nc, SSD_CHUNK, SSD_GROUPS, SSD_STATE)
    cmat = cs.reshape(bsz, nc, SSD_CHUNK, SSD_GROUPS, SSD_STATE)
    a_cum = jnp.cumsum(dta, axis=-1)
    lmat = jnp.exp(segsum(dta))
    y_diag = jnp.einsum('bclgn,bcsgn,bgrcls,bcsgrp->bclgrp', cmat, bmat, lmat, xdt)
    decay_states = jnp.exp(a_cum[..., -1:] - a_cum)
    states = jnp.einsum('bclgn,bgrcl,bclgrp->bcgrpn', bmat, decay_states, xdt)
    chunk_tot = jnp.pad(a_cum[..., -1], ((0, 0), (0, 0), (0, 0), (1, 0)))
    decay_chunk = jnp.exp(segsum(chunk_tot))
    states0 = jnp.concatenate([jnp.zeros_like(states[:, :1]), states], axis=1)
    new_states = jnp.einsum('bgrzc,bcgrpn->bzgrpn', decay_chunk, states0)
    prev = new_states[:, :-1]
    y_off = jnp.einsum('bclgn,bcgrpn,bgrcl->bclgrp', cmat, prev, jnp.exp(a_cum))
    y = (y_diag + y_off).reshape(bsz, L, SSD_HEADS, SSD_HEAD_DIM) + d_skip.astype(f32)[:, None] * xh
    y = y.reshape(bsz, L, SSD_WIDTH) * jax.nn.silu(z.astype(f32))
    return y.astype(z.dtype)


def setup_inputs(seed: int = 0) -> dict:
    key = jax.random.key(seed)
    ks = jax.random.split(key, 32)
    f32 = jnp.float32

    def nrm(k, shape, scale):
        return jax.random.normal(k, shape, f32) * scale

    def gain(k, n):
        return 1.0 + 0.01 * jax.random.normal(k, (DEPTH, n), f32)

    n_idx = jnp.arange(S5_STATE, dtype=f32)
    lam_re = -0.5 + 0.01 * jax.random.normal(ks[3], (DEPTH, S5_GROUPS, S5_STATE), f32)
    lam_im = (jnp.broadcast_to(math.pi * n_idx, (DEPTH, S5_GROUPS, S5_STATE))
              + 0.01 * jax.random.normal(ks[4], (DEPTH, S5_GROUPS, S5_STATE), f32))
    log_step = jax.random.uniform(ks[5], (DEPTH, S5_GROUPS), f32, math.log(1e-3), math.log(1e-1))
    dt0 = jnp.exp(jax.random.uniform(ks[17], (DEPTH, SSD_HEADS), f32, math.log(1e-3), math.log(1e-1)))
    dt_bias = dt0 + jnp.log(-jnp.expm1(-dt0))
    a_log = jnp.log(jax.random.uniform(ks[18], (DEPTH, SSD_HEADS), f32, 1.0, 16.0))
    return {
        'x': jax.random.normal(ks[0], (BATCH, SEQ, D_MODEL), f32),
        'norm_mix': gain(ks[1], D_MODEL),
        'w_in': nrm(ks[2], (DEPTH, D_MODEL, IN_PROJ), D_MODEL ** -0.5),
        's5_lam_re': lam_re,
        's5_lam_im': lam_im,
        's5_log_step': log_step,
        's5_b_re': nrm(ks[6], (DEPTH, S5_GROUPS, S5_STATE, S5_GROUP), (2 * S5_GROUP) ** -0.5),
        's5_b_im': nrm(ks[7], (DEPTH, S5_GROUPS, S5_STATE, S5_GROUP), (2 * S5_GROUP) ** -0.5),
        's5_c_re': nrm(ks[8], (DEPTH, S5_GROUPS, S5_GROUP, S5_STATE), (2 * S5_STATE) ** -0.5),
        's5_c_im': nrm(ks[9], (DEPTH, S5_GROUPS, S5_GROUP, S5_STATE), (2 * S5_STATE) ** -0.5),
        's5_d': nrm(ks[10], (DEPTH, S5_WIDTH), 1.0),
        's5_w_glu': nrm(ks[11], (DEPTH, S5_WIDTH, S5_WIDTH), S5_WIDTH ** -0.5),
        's5_b_glu': nrm(ks[12], (DEPTH, S5_WIDTH), 0.01),
        's5_norm': gain(ks[13], S5_WIDTH),
        'ssd_conv_w': nrm(ks[14], (DEPTH, SSD_CONV, SSD_CONV_DIM), SSD_CONV ** -0.5),
        'ssd_conv_b': nrm(ks[15], (DEPTH, SSD_CONV_DIM), 0.01),
        'ssd_dt_bias': dt_bias,
        'ssd_a_log': a_log,
        'ssd_d': 1.0 + 0.1 * jax.random.normal(ks[19], (DEPTH, SSD_HEADS), f32),
        'ssd_norm': gain(ks[20], SSD_WIDTH),
        'w_out': nrm(ks[21], (DEPTH, MIX_WIDTH, D_MODEL), MIX_WIDTH ** -0.5),
        'norm_ffn': gain(ks[22], D_MODEL),
        'w_gate': nrm(ks[23], (DEPTH, D_MODEL, FFN_HIDDEN), D_MODEL ** -0.5),
        'w_up': nrm(ks[24], (DEPTH, D_MODEL, FFN_HIDDEN), D_MODEL ** -0.5),
        'w_down': nrm(ks[25], (DEPTH, FFN_HIDDEN, D_MODEL), FFN_HIDDEN ** -0.5),
        'norm_final': 1.0 + 0.01 * jax.random.normal(ks[26], (D_MODEL,), f32),
    }


def reference(x, norm_mix, w_in, s5_lam_re, s5_lam_im, s5_log_step, s5_b_re, s5_b_im,
              s5_c_re, s5_c_im, s5_d, s5_w_glu, s5_b_glu, s5_norm, ssd_conv_w, ssd_conv_b,
              ssd_dt_bias, ssd_a_log, ssd_d, ssd_norm, w_out, norm_ffn, w_gate, w_up,
              w_down, norm_final):
    for i in range(DEPTH):
        h = rmsnorm(x, norm_mix[i])
        proj = h @ w_in[i]
        u_a, z_b, xbc_b, dt_b = jnp.split(proj, IN_SPLITS, axis=-1)
        y_a = s5_mixer(u_a, s5_lam_re[i], s5_lam_im[i], s5_log_step[i], s5_b_re[i],
                       s5_b_im[i], s5_c_re[i], s5_c_im[i], s5_d[i], s5_w_glu[i], s5_b_glu[i])
        y_a = rmsnorm(y_a, s5_norm[i])
        y_b = ssd_mixer(z_b, xbc_b, dt_b, ssd_conv_w[i], ssd_conv_b[i], ssd_dt_bias[i],
                        ssd_a_log[i], ssd_d[i])
        y_b = rmsnorm(y_b, ssd_norm[i])
        x = x + jnp.concatenate([y_a, y_b], axis=-1) @ w_out[i]
        h = rmsnorm(x, norm_ffn[i])
        x = x + (jax.nn.silu(h @ w_gate[i]) * (h @ w_up[i])) @ w_down[i]
    return rmsnorm(x, norm_final)
```

```python
import math
import os
from contextlib import ExitStack
import numpy as np
import ml_dtypes
import concourse.bass as bass
import concourse.mybir as mybir
from concourse.bass_utils import run_bass_kernel_spmd

F32 = mybir.dt.float32
BF16 = mybir.dt.bfloat16
AF = mybir.ActivationFunctionType
ALU = mybir.AluOpType

NT = 1024
NH = 2
DEPTH = 2
EPS = 1e-6
C_LRE, C_LIM, C_LST, C_BRE, C_BIM, C_CRE, C_CIM = 0, 32, 64, 96, 608, 1120, 1632
C_P0 = 2144
C_DCOL, C_GMIX, C_GS5, C_GSSD, C_GFFN, C_GFIN, C_BGLU, C_CW, C_CB, C_DTB, C_ALOG, C_SD = (
    0, 64, 72, 80, 88, 96, 104, 112, 160, 172, 173, 174)
NPERS = 184
NSP = C_P0 + NPERS
K_ID, K_TRI, K_NEG, K_BLK, K_ONE = 0, 128, 256, 384, 512
NCST = 640


class Sched:
    def __init__(self, nc, es):
        self.nc = nc
        self.ops = {e: [] for e in ('pe', 'act', 'dve', 'pool', 'sp')}
        self.cnt = {e: 0 for e in self.ops}
        self.sems = {e: [es.enter_context(nc.semaphore(f"s_{e}{i}")) for i in range(2)] for e in self.ops}
        self.dsems = {q: [es.enter_context(nc.semaphore(f"s_dma_{q}{i}")) for i in range(8)] for q in ('sp', 'pool')}
        self.dval = {q: [0] * 8 for q in ('sp', 'pool')}
        self.dnext = {q: 0 for q in ('sp', 'pool')}
        self.waited = {}
        self.lastw = {}
        self.readers = {}
        self.children = {}
        self.marks = []

    def _deps(self, b, is_write):
        out = []
        if '/' in b:
            par = b.split('/')[0]
            self.children.setdefault(par, set()).add(b)
            tags = [b, par]
        else:
            tags = [b] + list(self.children.get(b, ()))
        for t in tags:
            if t in self.lastw:
                out.append(self.lastw[t])
            if is_write:
                out.extend(self.readers.get(t, ()))
        return out

    def add(self, eng, fn, r=(), w=(), dma=False):
        deps = []
        for b in r:
            deps.extend(self._deps(b, False))
        for b in w:
            deps.extend(self._deps(b, True))
        if dma:
            i = self.dnext[eng]
            self.dnext[eng] = (i + 1) % 8
            if self.dval[eng][i] > 0:
                deps.append(('dma', eng, i, self.dval[eng][i]))
            self.dval[eng][i] += 16
            tok = ('dma', eng, i, self.dval[eng][i])
        else:
            k = self.cnt[eng]
            self.cnt[eng] += 1
            tok = ('op', eng, k)
        need = []
        for d in deps:
            if d[0] == 'op' and d[1] == eng and not dma and eng == 'pe':
                continue
            need.append(d)
        self.ops[eng].append([need, fn, tok])
        for b in w:
            self.lastw[b] = tok
            self.readers[b] = []
        for b in r:
            self.readers.setdefault(b, []).append(tok)
        return tok

    def mark(self, name):
        self.marks.append((name, dict(self.cnt)))

    def emit(self, final_tokens):
        nc = self.nc
        if os.environ.get('KMARKS'):
            import json
            json.dump(self.marks, open(os.environ['KMARKS'], 'w'))
        targets = {e: set() for e in self.ops}
        for e, lst in self.ops.items():
            for need, fn, tok in lst:
                best = {}
                for d in need:
                    if d[0] == 'op':
                        best[d[1]] = max(best.get(d[1], -1), d[2])
                for pe_, k in best.items():
                    targets[pe_].add(k)
        for d in final_tokens:
            if d[0] == 'op':
                targets[d[1]].add(d[2])
        for e in self.ops:
            if e != 'pe':
                targets[e] = set(range(self.cnt[e]))
        rank = {}
        for e, ks in targets.items():
            for r_, k in enumerate(sorted(ks)):
                rank[(e, k)] = r_

        def semval(d):
            if d[0] == 'dma':
                return self.dsems[d[1]][d[2]], d[3]
            r_ = rank[(d[1], d[2])]
            return self.sems[d[1]][r_ % 2], r_ // 2 + 1
        plans = {}
        for e, lst in self.ops.items():
            waited = {}
            plan = []
            for need, fn, tok in lst:
                best = {}
                for d in need:
                    if d[0] == 'op':
                        key = ('op', d[1])
                        if key not in best or best[key][2] < d[2]:
                            best[key] = d
                    else:
                        key = ('dma', d[1], d[2])
                        if key not in best or best[key][3] < d[3]:
                            best[key] = d
                waits = []
                for d in best.values():
                    sem, val = semval(d)
                    if waited.get(id(sem), 0) >= val:
                        continue
                    waited[id(sem)] = val
                    waits.append((sem, val))
                if tok[0] == 'dma':
                    inc = (self.dsems[tok[1]][tok[2]], 16)
                elif (tok[1], tok[2]) in rank:
                    r_ = rank[(tok[1], tok[2])]
                    inc = (self.sems[tok[1]][r_ % 2], 1)
                else:
                    inc = None
                plan.append((waits, fn, inc))
            plans[e] = plan
        fin = [semval(d) for d in final_tokens]
        with nc.Block() as block:
            def runner(name):
                def body(e):
                    for waits, fn, inc in plans[name]:
                        for (s_, v) in waits:
                            e.wait_ge(s_, v)
                        ins = fn(e)
                        if inc is not None:
                            ins.then_inc(inc[0], inc[1])
                    if name == 'pool':
                        for (s_, v) in fin:
                            e.wait_ge(s_, v)
                return body
            block.tensor(runner('pe'))
            block.scalar(runner('act'))
            block.vector(runner('dve'))
            block.gpsimd(runner('pool'))
            block.sync(runner('sp'))


import os
STAGE = int(os.environ.get('KSTAGE', '99'))
SUB = int(os.environ.get('KSUB', '99'))


def build_nc(dbg_spec=None):
    if dbg_spec is None and STAGE < 99:
        dbg_spec = ('dbg', [128, 32768])
    nc = bass.Bass("TRN2", target_bir_lowering=False)
    es = ExitStack()
    with es:
        _build(nc, es, dbg_spec)
    return nc


def _build(nc, es, dbg_spec):
    def dram(name, shape, dt=F32, kind="ExternalInput"):
        return nc.dram_tensor(name, shape, dt, kind=kind).ap()

    xT_d = dram("xT", [1024, NH * NT])
    out_d = dram("outT", [1024, NH * NT], kind="ExternalOutput")
    w_in_d = dram("w_in", [DEPTH, 1024, 3600])
    w_glu_d = dram("w_glu", [DEPTH, 1024, 1024])
    w_out_d = dram("w_out", [DEPTH, 2048, 1024])
    w_gate_d = dram("w_gate", [DEPTH, 1024, 2816])
    w_up_d = dram("w_up", [DEPTH, 1024, 2816])
    w_down_d = dram("w_down", [DEPTH, 2816, 1024])
    sp_d = dram("sp", [DEPTH, 128, NSP])
    cst_d = dram("cst", [128, NCST])
    tabM_d = dram("tabM", [DEPTH, 128, 8192], BF16, kind="Internal")
    tabG1T_d = dram("tabG1T", [DEPTH, 128, 8192], BF16, kind="Internal")
    tabG2_d = dram("tabG2", [DEPTH, 2, 128, 8192], BF16, kind="Internal")
    tabW_d = dram("tabW", [DEPTH, 4, 128, 4096], BF16, kind="Internal")
    dbg_d = None
    if dbg_spec is not None:
        dbg_d = dram("dbg", list(dbg_spec[1]), F32, kind="ExternalOutput")

    def sb(name, shape, dt=F32):
        return es.enter_context(nc.sbuf_tensor(name, shape, dt))

    S = Sched(nc, es)
    x = sb("x", [128, 8, NT])
    hT = sb("hT", [128, 8, NT], BF16)
    BIG = sb("BIG", [128, 24576], BF16)
    PQ = sb("PQ", [128, 8448])
    BCT = sb("BCT", [128, 4, NT], BF16)
    rs = sb("rs", [128, NT])
    rstd = sb("rstd", [128, NT])
    NWB = 4
    wbuf = [sb(f"wbuf{i}", [128, 4096], BF16) for i in range(NWB)]
    cst = sb("cst_sb", [128, NCST])
    identb = sb("identb", [128, 128], BF16)
    onesb = sb("onesb", [128, 128], BF16)
    pers = sb("pers", [128, DEPTH, NPERS])
    T1 = sb("T1", [128, DEPTH, 64])
    T2 = sb("T2", [128, DEPTH, 64])
    qcar = sb("qcar", [128, DEPTH, 64])
    RHO = sb("RHO", [128, DEPTH, 32])
    W128 = sb("W128", [128, DEPTH, 64])
    sstate = sb("sstate", [128, DEPTH, 1024])
    ctail = sb("ctail", [128, DEPTH, 12, 3], BF16)
    cdiag = sb("cdiag", [128, 48, 128], BF16)
    smalls = sb("smalls", [128, 64])
    sgt = sb("sgt", [128, NT], BF16)
    pp = [es.enter_context(nc.psum_tensor(f"pp{i}", [128, 1024], F32)) for i in range(4)]
    ppn = [f"pp{i}" for i in range(4)]

    R1 = BIG[:, 0:8192]
    R2 = BIG[:, 8192:16384]
    R5 = BIG[:, 16384:24576]
    identf = cst[:, K_ID:K_ID + 128]
    trif = cst[:, K_TRI:K_TRI + 128]
    negf = cst[:, K_NEG:K_NEG + 128]
    blkf = cst[:, K_BLK:K_BLK + 128]

    def TT(eng, out, a, b, op, r, w):
        S.add(eng, lambda e: e.tensor_tensor(out, a, b, op), r, w)

    def TS(eng, out, a, s1, s2, op0, op1, r, w):
        if op1 is None:
            S.add(eng, lambda e: e.tensor_scalar(out, a, s1, None, op0), r, w)
        else:
            S.add(eng, lambda e: e.tensor_scalar(out, a, s1, s2, op0, op1), r, w)

    def STT(out, a, sc, b, op0, op1, r, w):
        S.add('dve', lambda e: e.scalar_tensor_tensor(out, a, sc, b, op0, op1), r, w)

    def ACT(out, a, func, r, w, bias=None, scale=None):
        kw = {}
        if bias is not None:
            kw['bias'] = bias
        if scale is not None:
            kw['scale'] = scale
        S.add('act', lambda e: e.activation(out, a, func, **kw), r, w)

    def CP(eng, out, a, r, w):
        if eng == 'act':
            S.add('act', lambda e: e.copy(out, a), r, w)
        else:
            S.add(eng, lambda e: e.tensor_copy(out, a), r, w)

    def MM(out, lhsT, rhs, start, stop, r, w):
        S.add('pe', lambda e: e.matmul(out, lhsT, rhs, start=start, stop=stop), r, w)

    def TR(out, in_, ident, r, w):
        S.add('pe', lambda e: e.transpose(out, in_, ident), r, w)

    def DMA(eng, out, in_, r, w):
        return S.add(eng, lambda e: e.dma_start(out=out, in_=in_), r, w, dma=True)

    wb_i = [0]

    def next_wb():
        i = wb_i[0] % NWB
        wb_i[0] += 1
        return i

    DMA('sp', cst[:], cst_d[:, :], [], ['cst'])
    CP('dve', identb[:], identf, ['cst'], ['identb'])
    CP('dve', onesb[:], cst[:, K_ONE:K_ONE + 128], ['cst'], ['onesb'])
    for L in range(DEPTH):
        DMA('sp', pers[:, L, :], sp_d[L, :, C_P0:C_P0 + NPERS], [], ['pers'])
    S.add('dve', lambda e: e.memset(qcar[:], 0.0), [], ['qcar'])
    S.add('dve', lambda e: e.memset(sstate[:], 0.0), [], ['sstate'])
    S.add('dve', lambda e: e.memset(ctail[:], 0.0), [], ['ctail'])

    held = {}

    def gen_tables(L):
        prm = PQ[:, 0:C_P0]
        DMA('sp', prm, sp_d[L, :, 0:C_P0], ['PQ'], ['PQ'])
        base = [C_P0]

        def tmp(n):
            a = PQ[:, base[0]:base[0] + n]
            base[0] += n
            return a
        R, W = ['PQ', 'cst'], ['PQ']
        lr = prm[:, C_LRE:C_LRE + 32]
        li = prm[:, C_LIM:C_LIM + 32]
        step = tmp(32); th = tmp(32); aa = tmp(32)
        ACT(step, prm[:, C_LST:C_LST + 32], AF.Exp, R, W)
        TT('dve', th, li, step, ALU.mult, R, W)
        TT('dve', aa, lr, step, ALU.mult, R, W)
        mag = tmp(32); sn = tmp(32); cs = tmp(32)
        ACT(mag, aa, AF.Exp, R, W, scale=1.0 / 16)
        ACT(sn, th, AF.Sin, R, W, scale=1.0 / 16)
        hp = tmp(1)
        S.add('dve', lambda e: e.memset(hp, math.pi / 2), R, W)
        ACT(cs, th, AF.Sin, R, W, scale=1.0 / 16, bias=hp)
        zr = tmp(32); zi = tmp(32); t0 = tmp(32); t1 = tmp(32)
        TT('dve', zr, mag, cs, ALU.mult, R, W)
        TT('dve', zi, mag, sn, ALU.mult, R, W)
        for _ in range(4):
            TT('dve', t0, zr, zr, ALU.mult, R, W)
            TT('dve', t1, zi, zi, ALU.mult, R, W)
            TT('dve', zi, zr, zi, ALU.mult, R, W)
            TS('dve', zi, zi, 2.0, None, ALU.mult, None, R, W)
            TT('dve', zr, t0, t1, ALU.subtract, R, W)
        ar, ai = zr, zi
        ir = tmp(32); ii = tmp(32)
        TT('dve', t0, ar, ar, ALU.mult, R, W)
        TT('dve', t1, ai, ai, ALU.mult, R, W)
        TT('dve', t0, t0, t1, ALU.add, R, W)
        S.add('dve', lambda e: e.reciprocal(t0, t0), R, W)
        TT('dve', ir, ar, t0, ALU.mult, R, W)
        TT('dve', ii, ai, t0, ALU.mult, R, W)
        TS('dve', ii, ii, -1.0, None, ALU.mult, None, R, W)
        PWr = tmp(9 * 32); PWi = tmp(9 * 32); NPr = tmp(8 * 32); NPi = tmp(8 * 32)

        def powers(Pr, Pi, n, br_, bi_):
            S.add('dve', lambda e: e.memset(Pr[:, 0:32], 1.0), R, W)
            S.add('dve', lambda e: e.memset(Pi[:, 0:32], 0.0), R, W)
            for j in range(1, n):
                pr, pi = Pr[:, (j - 1) * 32:j * 32], Pi[:, (j - 1) * 32:j * 32]
                qr, qi = Pr[:, j * 32:(j + 1) * 32], Pi[:, j * 32:(j + 1) * 32]
                TT('dve', t0, pr, br_, ALU.mult, R, W)
                TT('dve', t1, pi, bi_, ALU.mult, R, W)
                TT('dve', qr, t0, t1, ALU.subtract, R, W)
                TT('dve', t0, pr, bi_, ALU.mult, R, W)
                TT('dve', t1, pi, br_, ALU.mult, R, W)
                TT('dve', qi, t0, t1, ALU.add, R, W)
        powers(PWr, PWi, 9, ar, ai)
        powers(NPr, NPi, 8, ir, ii)
        if SUB == 1:
            return
        nr = tmp(32); den = tmp(32); cr = tmp(32); ci = tmp(32)
        TS('dve', nr, ar, -1.0, None, ALU.add, None, R, W)
        TT('dve', t0, lr, lr, ALU.mult, R, W)
        TT('dve', t1, li, li, ALU.mult, R, W)
        TT('dve', den, t0, t1, ALU.add, R, W)
        S.add('dve', lambda e: e.reciprocal(den, den), R, W)
        TT('dve', t0, nr, lr, ALU.mult, R, W)
        TT('dve', t1, ai, li, ALU.mult, R, W)
        TT('dve', t0, t0, t1, ALU.add, R, W)
        TT('dve', cr, t0, den, ALU.mult, R, W)
        TT('dve', t0, ai, lr, ALU.mult, R, W)
        TT('dve', t1, nr, li, ALU.mult, R, W)
        TT('dve', t0, t0, t1, ALU.subtract, R, W)
        TT('dve', ci, t0, den, ALU.mult, R, W)
        v3 = lambda a: a.rearrange("p (g h) -> p g h", h=16)
        bc = lambda a: a.unsqueeze(2).broadcast_to([128, 32, 16])
        bre = v3(prm[:, C_BRE:C_BRE + 512]); bim = v3(prm[:, C_BIM:C_BIM + 512])
        cre = v3(prm[:, C_CRE:C_CRE + 512]); cim = v3(prm[:, C_CIM:C_CIM + 512])
        bbr = v3(tmp(512)); bbi = v3(tmp(512)); u0 = v3(tmp(512)); u1 = v3(tmp(512))
        TT('dve', u0, bre, bc(cr), ALU.mult, R, W)
        TT('dve', u1, bim, bc(ci), ALU.mult, R, W)
        TT('dve', bbr, u0, u1, ALU.subtract, R, W)
        TT('dve', u0, bim, bc(cr), ALU.mult, R, W)
        TT('dve', u1, bre, bc(ci), ALU.mult, R, W)
        TT('dve', bbi, u0, u1, ALU.add, R, W)
        G1 = R1.rearrange("p (r g s h) -> p r g s h", r=2, g=32, s=8)
        G2 = R2.rearrange("p (r g s h) -> p r g s h", r=2, g=32, s=8)
        t5w = R5.bitcast(F32)
        w0 = t5w[:, 0:2048].rearrange("p (g s h) -> p g s h", g=32, s=4)
        w1 = t5w[:, 2048:4096].rearrange("p (g s h) -> p g s h", g=32, s=4)
        bs = lambda a_: a_.unsqueeze(2).broadcast_to([128, 32, 4, 16])
        ps_ = lambda tab, s0: tab.rearrange("p (s g) -> p g s", g=32)[:, :, s0:s0 + 4].unsqueeze(3).broadcast_to([128, 32, 4, 16])
        RW5 = R + ['R5']
        for s0 in (0, 4):
            nr_, ni_ = ps_(NPr[:, 0:256], s0), ps_(NPi[:, 0:256], s0)
            pr_, pi_ = ps_(PWr[:, 0:256], s0), ps_(PWi[:, 0:256], s0)
            TT('dve', w0, bs(bbr), nr_, ALU.mult, RW5, ['R5'])
            TT('dve', w1, bs(bbi), ni_, ALU.mult, RW5, ['R5'])
            TT('dve', G1[:, 0, :, s0:s0 + 4, :], w0, w1, ALU.subtract, RW5, ['R1'])
            TT('dve', w0, bs(bbr), ni_, ALU.mult, RW5 + ['R1'], ['R5'])
            TT('dve', w1, bs(bbi), nr_, ALU.mult, RW5 + ['R1'], ['R5'])
            TT('dve', G1[:, 1, :, s0:s0 + 4, :], w0, w1, ALU.add, RW5, ['R1'])
            TT('dve', w0, bs(cre), pr_, ALU.mult, RW5 + ['R1'], ['R5'])
            TT('dve', w1, bs(cim), pi_, ALU.mult, RW5 + ['R1'], ['R5'])
            TT('dve', G2[:, 0, :, s0:s0 + 4, :], w0, w1, ALU.subtract, RW5, ['R2'])
            TT('dve', w0, bs(cre), pi_, ALU.mult, RW5 + ['R2'], ['R5'])
            TT('dve', w1, bs(cim), pr_, ALU.mult, RW5 + ['R2'], ['R5'])
            TT('dve', w0, w0, w1, ALU.add, RW5, ['R5'])
            TS('dve', G2[:, 1, :, s0:s0 + 4, :], w0, -1.0, None, ALU.mult, None, RW5, ['R2'])
        if SUB == 2:
            return
        G1f = R1.rearrange("p (r g a) -> p r g a", r=2, g=32)
        G2f = R2.rearrange("p (r g a) -> p r g a", r=2, g=32)
        G2z = [hT[:].rearrange("p k t -> p (k t)"), x[:].rearrange("p k t -> p (k t)").bitcast(BF16)[:, 0:8192]]
        G2ztag = ['hT', 'x']
        for g2 in range(2):
            S.add('dve', lambda e, g2=g2: e.memset(G2z[g2], 0.0), [], [G2ztag[g2]])
            CP('dve', G2z[g2][64 * g2:64 * g2 + 64, :], R2[64 * g2:64 * g2 + 64, :], ['R2'], [G2ztag[g2]])
            DMA('sp', tabG2_d[L, g2, :, :], G2z[g2], [G2ztag[g2]], ['tabG2'])
        G2zf = [g.rearrange("p (r g a) -> p r g a", r=2, g=32) for g in G2z]
        if SUB == 3:
            return
        Mst = R5.rearrange("p (g a) -> p g a", g=64)
        dcol = pers[:, L, C_DCOL:C_DCOL + 64]
        dd = PQ[:, 6144:6144 + 1024].rearrange("p (g a) -> p g a", g=8)
        mt = PQ[:, 7168:7168 + 1024].rearrange("p (g a) -> p g a", g=8)
        for b8 in range(8):
            pt = pp[b8 % 2]
            ptn = ppn[b8 % 2]
            for j in range(8):
                g = b8 * 8 + j
                gp, g2 = g // 2, g % 2
                for ri in range(2):
                    MM(pt[:, j * 128:(j + 1) * 128], G1f[:, ri, gp, :], G2zf[g2][:, ri, gp, :], ri == 0, ri == 1,
                       ['R1', G2ztag[g2]], [ptn])
            TT('dve', dd, identf.unsqueeze(1).broadcast_to([128, 8, 128]),
               dcol[:, b8 * 8:(b8 + 1) * 8].unsqueeze(2).broadcast_to([128, 8, 128]), ALU.mult, ['cst', 'pers'], ['PQd'])
            TT('dve', mt, pt[:].rearrange("p (g a) -> p g a", g=8),
               blkf.unsqueeze(1).broadcast_to([128, 8, 128]), ALU.mult, [ptn, 'cst'], ['PQm'])
            TT('dve', Mst[:, b8 * 8:(b8 + 1) * 8, :], mt, dd, ALU.add, ['PQm', 'PQd'], ['R5'])
        DMA('sp', tabM_d[L, :, :], R5, ['R5'], ['tabM'])
        if L == DEPTH - 1 and STAGE >= 99:
            for k in range(8):
                DMA('sp', x[:, k, :], xT_d[k * 128:(k + 1) * 128, 0:NT], ['x'], ['x'])
            held['x0'] = True
        if SUB == 4:
            return
        G1Tst = R5.rearrange("p (r g a) -> p r g a", r=2, g=32)
        for b4 in range(4):
            pt = pp[2 + b4 % 2]
            ptn = ppn[2 + b4 % 2]
            ptb = pt[:].bitcast(BF16)
            for j in range(16):
                idx = b4 * 16 + j
                ri, gp = idx // 32, idx % 32
                TR(ptb[:, j * 128:(j + 1) * 128], G1f[:, ri, gp, :], identb[:], ['R1', 'identb'], [ptn])
            CP('act', R5[:, b4 * 2048:(b4 + 1) * 2048], ptb, [ptn], ['R5'])
        DMA('sp', tabG1T_d[L, :, :], R5, ['R5'], ['tabG1T'])
        irho = tmp(32); Vr = tmp(32); Vi = tmp(32); v0 = tmp(32); v1 = tmp(32)
        RW = ['PQ', 'RHO']
        ACT(RHO[:, L, :], aa, AF.Exp, R, RW, scale=8.0)
        S.add('dve', lambda e: e.reciprocal(irho, RHO[:, L, :]), RW, W)
        TT('dve', Vr, PWr[:, 256:288], irho, ALU.mult, R, W)
        TT('dve', Vi, PWi[:, 256:288], irho, ALU.mult, R, W)
        Wc = R1.bitcast(F32).rearrange("p (g c) -> p g c", g=32)
        Ws = R2.bitcast(F32).rearrange("p (g c) -> p g c", g=32)
        t5 = R5.bitcast(F32)
        S.add('dve', lambda e: e.memset(Wc[:, :, 0:1], 1.0), ['R1'], ['R1'])
        S.add('dve', lambda e: e.memset(Ws[:, :, 0:1], 0.0), ['R2'], ['R2'])
        k = 1
        while k < 128:
            ta = t5[:, 0:32 * k].rearrange("p (g c) -> p g c", g=32)
            tb_ = t5[:, 2048:2048 + 32 * k].rearrange("p (g c) -> p g c", g=32)
            vrb = Vr.unsqueeze(2).broadcast_to([128, 32, k])
            vib = Vi.unsqueeze(2).broadcast_to([128, 32, k])
            RR_ = ['R1', 'R2', 'R5', 'PQ']
            TT('dve', ta, Wc[:, :, 0:k], vrb, ALU.mult, RR_, ['R5'])
            TT('dve', tb_, Ws[:, :, 0:k], vib, ALU.mult, RR_, ['R5'])
            TT('dve', Wc[:, :, k:2 * k], ta, tb_, ALU.subtract, RR_, ['R1'])
            TT('dve', ta, Wc[:, :, 0:k], vib, ALU.mult, RR_, ['R5'])
            TT('dve', tb_, Ws[:, :, 0:k], vrb, ALU.mult, RR_, ['R5'])
            TT('dve', Ws[:, :, k:2 * k], ta, tb_, ALU.add, RR_, ['R2'])
            TT('dve', v0, Vr, Vr, ALU.mult, R, W)
            TT('dve', v1, Vi, Vi, ALU.mult, R, W)
            TT('dve', Vi, Vr, Vi, ALU.mult, R, W)
            TS('dve', Vi, Vi, 2.0, None, ALU.mult, None, R, W)
            TT('dve', Vr, v0, v1, ALU.subtract, R, W)
            k *= 2
        CP('dve', W128[:, L, 0:32], Vr, R, ['W128'])
        CP('dve', W128[:, L, 32:64], Vi, R, ['W128'])
        st5 = R5.rearrange("p (t g c) -> p t g c", t=2, g=32)
        rb = RHO[:, L, :].unsqueeze(2).broadcast_to([128, 32, 128])
        TT('dve', st5[:, 0], Wc, rb, ALU.mult, ['R1', 'RHO'], ['R5'])
        TT('dve', st5[:, 1], Ws, rb, ALU.mult, ['R2', 'RHO'], ['R5'])
        DMA('sp', tabW_d[L, 0:2, :, :].rearrange("t p n -> p t n"), R5.rearrange("p (t n) -> p t n", t=2), ['R5'], ['tabW'])
        CP('dve', st5[:, 0], Wc, ['R1'], ['R5'])
        CP('act', st5[:, 1], Ws, ['R2'], ['R5'])
        DMA('sp', tabW_d[L, 2:4, :, :].rearrange("t p n -> p t n"), R5.rearrange("p (t n) -> p t n", t=2), ['R5'], ['tabW'])

    final = []

    dbgst = sb("dbgst", [128, 256]) if dbg_d is not None else None

    def dump(tile_ap, col0, ncols, tags):
        for c in range(0, ncols, 256):
            n = min(256, ncols - c)
            CP('dve', dbgst[:, 0:n], tile_ap[:, c:c + n], tags, ['dbgst'])
            final.append(DMA('sp', dbg_d[:, col0 + c:col0 + c + n], dbgst[:, 0:n], ['dbgst'], ['dbg']))

    if STAGE >= 1:
        for L in range(DEPTH if STAGE > 1 else 1):
            S.mark(f'tables{L}')
            gen_tables(L)
    if STAGE == 1 and SUB != 5:
        DMA('sp', R1, tabM_d[0, :, :], ['tabM', 'R1'], ['R1'])
        dump(R1, 0, 8192, ['R1'])
        DMA('sp', R2, tabG1T_d[0, :, :], ['tabG1T', 'R2'], ['R2'])
        dump(R2, 8192, 8192, ['R2'])
        DMA('sp', R5, tabG2_d[0, 0, :, :], ['tabG2', 'R5'], ['R5'])
        dump(R5, 16384, 8192, ['R5'])
        dump(T1[:, 0, :], 24576, 64, ['T12'])
        dump(T2[:, 0, :], 24640, 64, ['T12'])

    epsT = sb("epsT", [128, 1])
    S.add('dve', lambda e: e.memset(epsT[:], EPS), [], ['epsT'])

    def rms_generic(src, dst, gain_col0, L, sq, src_tags, dst_tags, sq_tags, extra_w=(), src_tile=None):
        hs = [slice(0, 512), slice(512, 1024)]
        sqp = sq_tags[0].split('/')[0]
        dstp = dst_tags[0].split('/')[0]
        inplace = (dst_tags == src_tags)
        first = [True]

        def sqw(k, nh):
            t = [f'{sqp}/sq{k}n{nh}']
            if first[0]:
                first[0] = False
                t = [sqp] + t
            return t
        for nh in range(2):
            for k in range(8):
                rt = src_tile(k) if src_tile else src_tags
                if k % 2 == 0:
                    ACT(sq[:, k, hs[nh]], src[:, k, hs[nh]], AF.Square, rt, sqw(k, nh))
                else:
                    TT('dve', sq[:, k, hs[nh]], src[:, k, hs[nh]], src[:, k, hs[nh]], ALU.mult, rt, sqw(k, nh))
        for nh in range(2):
            for k in range(8):
                MM(pp[0][:, hs[nh]], onesb[:], sq[:, k, hs[nh]], k == 0, k == 7, [f'{sqp}/sq{k}n{nh}', 'onesb'], [f'pp0/n{nh}'])
        for nh in range(2):
            ACT(rs[:, hs[nh]], pp[0][:, hs[nh]], AF.Ln, [f'pp0/n{nh}', 'epsT'], [f'rs/n{nh}'], bias=epsT[:], scale=1.0 / 1024)
        for nh in range(2):
            ACT(rstd[:, hs[nh]], rs[:, hs[nh]], AF.Exp, [f'rs/n{nh}'], [f'rstd/n{nh}'], scale=-0.5)
        firstd = [True]
        for nh in range(2):
            for k in range(8):
                g = pers[:, L, gain_col0 + k:gain_col0 + k + 1]
                wt = [f'{dstp}/o{k}n{nh}']
                if firstd[0]:
                    firstd[0] = False
                    wt = [dstp] + wt
                STT(dst[:, k, hs[nh]], src[:, k, hs[nh]], g, rstd[:, hs[nh]], ALU.mult, ALU.mult,
                    (src_tile(k) if src_tile else src_tags) + ['pers', f'rstd/n{nh}'], wt + list(extra_w))

    def rmsnorm_to(dst_bf, gain_col0, L, src=x, tagw='hT'):
        rms_generic(src, dst_bf, gain_col0, L, R1.rearrange("p (k t) -> p k t", k=8), ['x'], [tagw], ['R1'],
                    src_tile=lambda k: [f'x/t{k}'])

    def load_w(src_ap, kt, ncols, eng='pool'):
        i = next_wb()
        view = wbuf[i][:, 0:kt * ncols].rearrange("p (k c) -> p k c", k=kt)
        DMA(eng, view, src_ap.rearrange("(k p) c -> p k c", p=128), [f'wb{i}'], [f'wb{i}'])
        return view, f'wb{i}'

    def load_tab(src_ap, ncols):
        i = next_wb()
        view = wbuf[i][:, 0:ncols]
        DMA('sp', view, src_ap, [f'wb{i}', 'tabM', 'tabG1T', 'tabW'], [f'wb{i}'])
        return view, f'wb{i}'

    pp_rr = [0]

    def next_pp():
        i = pp_rr[0] % 4
        pp_rr[0] += 1
        return pp[i], ppn[i]

    ev_rr = [0]

    def evac_eng():
        ev_rr[0] += 1
        return 'act' if ev_rr[0] % 2 else 'dve'

    def prefetch_w(w_d, L, col0, ncols_total, kt, blk=512, row0=0, nblocks=1):
        out = []
        c = 0
        while c < ncols_total and len(out) < nblocks:
            nb = min(blk, ncols_total - c)
            out.append(load_w(w_d[L, row0:row0 + kt * 128, col0 + c:col0 + c + nb], kt, nb))
            c += nb
        return out

    def proj_fm(w_d, L, col0, ncols_total, kt, rhs_tile, rhs_tag, consume, blk=512, row0=0, pre=None):
        c = 0
        pre = list(pre or [])
        while c < ncols_total:
            nb = min(blk, ncols_total - c)
            if pre:
                wv, wtag = pre.pop(0)
            else:
                wv, wtag = load_w(w_d[L, row0:row0 + kt * 128, col0 + c:col0 + c + nb], kt, nb)
            for m0 in range(0, nb, 128):
                ms = min(128, nb - m0)
                pt, ptn = next_pp()
                for n in range(2):
                    for k in range(kt):
                        MM(pt[0:ms, n * 512:(n + 1) * 512], wv[:, k, m0:m0 + ms],
                           rhs_tile[:, k, n * 512:(n + 1) * 512], k == 0, k == kt - 1,
                           [wtag] + (rhs_tag if isinstance(rhs_tag, list) else [rhs_tag]), [ptn])
                consume((c + m0) // 128, pt, ptn, ms)
            c += nb

    claimed = {}
    ph_no = [0]

    def ct(parent, name, phase):
        key = (parent, phase)
        if key not in claimed:
            claimed[key] = True
            return [parent, f'{parent}/{name}']
        return [f'{parent}/{name}']

    def s5_block(L, wvs):
        ph_no[0] += 1
        Utm2 = R1.rearrange("p (g s h) -> p g s h", g=64, s=8)
        hs = hT[:].rearrange("p k (c s) -> p k s c", s=8)
        for s in range(8):
            pt, ptn = next_pp()
            for nb in range(2):
                wv, wtag = wvs[nb]
                for k in range(8):
                    MM(pt[:, nb * 512:(nb + 1) * 512], hs[:, k, s, :], wv[:, k, :], k == 0, k == 7,
                       ['hT', wtag], [ptn])
            CP(evac_eng(), Utm2[:, :, s, :], pt[:].rearrange("p (g h) -> p g h", h=16), [ptn], ct('R1', f'u{s}', ('u', L, ph_no[0])))
        S.mark('s5.tr')
        for b4 in range(4):
            pt, ptn = next_pp()
            ptb = pt[:].bitcast(BF16)
            for j in range(16):
                g = b4 * 16 + j
                TR(ptb[:, j * 128:(j + 1) * 128], R1[:, g * 128:(g + 1) * 128], identb[:], ['R1', 'identb'], [ptn])
            CP(evac_eng(), R2[:, b4 * 2048:(b4 + 1) * 2048], ptb, [ptn], ct('R2', f'ug{b4}', ('ug', L, ph_no[0])))
        Ug = R2.rearrange("p (g c) -> p g c", g=64)
        S.mark('s5.yintra')
        Ytm = R1.rearrange("p (l g h) -> p l g h", l=8, g=64)
        for q in range(4):
            tv, ttag = load_tab(tabM_d[L, :, q * 2048:(q + 1) * 2048], 2048)
            for half8 in range(2):
                pt, ptn = next_pp()
                for j in range(8):
                    gl = half8 * 8 + j
                    g = q * 16 + gl
                    MM(pt[:, j * 128:(j + 1) * 128], Ug[:, g, :], tv[:, gl * 128:(gl + 1) * 128], True, True,
                       ['R2', ttag], [ptn])
                g0 = q * 16 + half8 * 8
                CP(evac_eng(), Ytm[:, :, g0:g0 + 8, :].rearrange("p l g h -> p g l h"),
                   pt[:].rearrange("p (g l h) -> p g l h", g=8, l=8), [ptn], ct('R1', f'y{g0}', ('yi', L, ph_no[0])))
        S.mark('s5.P')
        Pq = PQ[:, 0:2 * 32 * 129].rearrange("p (r g c) -> p r g c", r=2, g=32)
        CP('dve', Pq[:, :, :, 0], qcar[:, L, :].rearrange("p (r g) -> p r g", r=2), ['qcar'], ['PQ'])
        for ri in range(2):
            for q in range(2):
                tv, ttag = load_tab(tabG1T_d[L, :, ri * 4096 + q * 2048:ri * 4096 + (q + 1) * 2048], 2048)
                for b in range(4):
                    pt, ptn = next_pp()
                    for j in range(4):
                        gpl = b * 4 + j
                        gp = q * 16 + gpl
                        for g2 in range(2):
                            g = gp * 2 + g2
                            MM(pt[64 * g2:64 * g2 + 64, j * 128:(j + 1) * 128],
                               tv[:, gpl * 128 + g2 * 64:gpl * 128 + g2 * 64 + 64], Ug[:, g, :], True, True,
                               ['R2', ttag], [ptn])
                    gp0 = q * 16 + b * 4
                    CP(evac_eng(), Pq[:, ri, gp0:gp0 + 4, 1:129],
                       pt[:, 0:512].rearrange("p (g c) -> p g c", g=4), [ptn], [f'PQ/P{ri}_{gp0}'])
        S.mark('s5.scan')
        Tc_v, Tc_t = load_tab(tabW_d[L, 0, :, :], 4096)
        Ts_v, Ts_t = load_tab(tabW_d[L, 1, :, :], 4096)
        g3 = lambda a_: a_.rearrange("p (g c) -> p g c", g=32)
        R2f = R2.bitcast(F32)
        Ra = R2f[:, 0:2048].rearrange("p (g c) -> p g c", g=16)
        Rb = R2f[:, 2048:4096].rearrange("p (g c) -> p g c", g=16)
        R5f_ = R5.bitcast(F32)
        Ra1 = R5f_[:, 0:2048].rearrange("p (g c) -> p g c", g=16)
        Rb1 = R5f_[:, 2048:4096].rearrange("p (g c) -> p g c", g=16)
        for hg in (1, 0):
            gs = slice(16 * hg, 16 * hg + 16)
            Pr = Pq[:, 0, gs, 1:129]
            Pi = Pq[:, 1, gs, 1:129]
            Tc = g3(Tc_v)[:, gs, :]
            Ts = g3(Ts_v)[:, gs, :]
            en = 'dve'
            ra, rb, rt, pt_ = (Ra1, Rb1, 'R5', 'PQ') if hg == 1 else (Ra, Rb, 'R2', 'PQ')
            TT(en, ra, Pr, Tc, ALU.mult, [pt_, Tc_t], [rt])
            TT(en, rb, Pi, Ts, ALU.mult, [pt_, Ts_t], [rt])
            TT(en, ra, ra, rb, ALU.add, [rt], [rt])
            TT(en, rb, Pi, Tc, ALU.mult, [pt_, Tc_t, rt], [rt])
            TT(en, Pr, Pr, Ts, ALU.mult, [pt_, Ts_t], [pt_])
            TT(en, rb, rb, Pr, ALU.subtract, [rt, pt_], [rt])
        for hg in range(2):
            Ra, Rb = (Ra, Rb) if hg == 0 else (Ra1, Rb1)
            for j in range(16):
                gp = 16 * hg + j
                for ri, src in ((0, Ra), (1, Rb)):
                    S.add('dve', lambda e, ri=ri, gp=gp, src=src, j=j: e.tensor_tensor_scan(
                        Pq[:, ri, gp, 1:129], RHO[:, L, gp:gp + 1].broadcast_to([128, 128]), src[:, j, :],
                        Pq[:, ri, gp, 0:1], ALU.mult, ALU.add), ['R2', 'R5', 'PQ', 'RHO'], ['PQ'])
        r128 = Pq[:, :, :, 128]
        w128 = W128[:, L, :].rearrange("p (r g) -> p r g", r=2)
        sm = smalls[:, 0:64].rearrange("p (r g) -> p r g", r=2)
        qc = qcar[:, L, :].rearrange("p (r g) -> p r g", r=2)
        RS = ['PQ', 'W128', 'smalls']
        TT('dve', sm[:, 0, :], r128[:, 0, :], w128[:, 0, :], ALU.mult, RS, ['smalls'])
        TT('dve', sm[:, 1, :], r128[:, 1, :], w128[:, 1, :], ALU.mult, RS, ['smalls'])
        TT('dve', qc[:, 0, :], sm[:, 0, :], sm[:, 1, :], ALU.subtract, RS, ['qcar'])
        TT('dve', sm[:, 0, :], r128[:, 0, :], w128[:, 1, :], ALU.mult, RS + ['qcar'], ['smalls'])
        TT('dve', sm[:, 1, :], r128[:, 1, :], w128[:, 0, :], ALU.mult, RS + ['qcar'], ['smalls'])
        TT('dve', qc[:, 1, :], sm[:, 0, :], sm[:, 1, :], ALU.add, RS, ['qcar'])
        Wc_v, Wc_t = load_tab(tabW_d[L, 2, :, :], 4096)
        Ws_v, Ws_t = load_tab(tabW_d[L, 3, :, :], 4096)
        Qb = R2.rearrange("p (r g c) -> p r g c", r=2, g=32)
        R5f = R5.bitcast(F32)
        tA = R5f[:, 0:2048].rearrange("p (g c) -> p g c", g=16)[:, :, 0:127]
        tB = R5f[:, 2048:4096].rearrange("p (g c) -> p g c", g=16)[:, :, 0:127]
        CP('act', Qb[:, :, :, 0], Pq[:, :, :, 0], ['PQ'], ['R2'])
        for hg in range(2):
            gs = slice(16 * hg, 16 * hg + 16)
            rr = Pq[:, 0, gs, 1:128]
            rim = Pq[:, 1, gs, 1:128]
            wc = g3(Wc_v)[:, gs, 1:128]
            ws = g3(Ws_v)[:, gs, 1:128]
            TT('dve', tA, rim, ws, ALU.mult, ['PQ', Ws_t, 'R5'], ['R5'])
            TT('dve', tB, rr, wc, ALU.mult, ['PQ', Wc_t, 'R5'], ['R5'])
            TT('dve', Qb[:, 0, gs, 1:128], tB, tA, ALU.subtract, ['R5'], ['R2'])
            TT('dve', tA, rr, ws, ALU.mult, ['PQ', Ws_t, 'R5'], ['R5'])
            TT('dve', tB, rim, wc, ALU.mult, ['PQ', Wc_t, 'R5'], ['R5'])
            TT('dve', Qb[:, 1, gs, 1:128], tA, tB, ALU.add, ['R5'], ['R2'])
        for q in range(4):
            tvs = []
            for g2 in range(2):
                i = next_wb()
                view = wbuf[i][:, 0:2048].rearrange("p (r g a) -> p r g a", r=2, g=8)
                DMA('sp', view, tabG2_d[L, g2, :, :].rearrange("p (r g a) -> p r g a", r=2, g=32)[:, :, q * 8:(q + 1) * 8, :],
                    [f'wb{i}', 'tabG2'], [f'wb{i}'])
                tvs.append((view, f'wb{i}'))
            for half8 in range(2):
                pt, ptn = next_pp()
                for j in range(8):
                    gl = half8 * 8 + j
                    g = q * 16 + gl
                    gp, g2 = g // 2, g % 2
                    gpl = gp - q * 8
                    tv, ttag = tvs[g2]
                    for ri in range(2):
                        MM(pt[:, j * 128:(j + 1) * 128], Qb[:, ri, gp, :], tv[:, ri, gpl, :], ri == 0, ri == 1,
                           ['R2', ttag], [ptn])
                g0 = q * 16 + half8 * 8
                yv = Ytm[:, :, g0:g0 + 8, :].rearrange("p l g h -> p g l h")
                TT('dve', yv, yv, pt[:].rearrange("p (g l h) -> p g l h", g=8, l=8), ALU.add, [ptn, f'R1/y{g0}'], [f'R1/y{g0}'])
        pre_glu = prefetch_w(w_glu_d, L, 0, 1024, 8, nblocks=1)
        ACT(R1, R1, AF.Gelu_apprx_tanh, ['R1', 'PQ'], ['R1'])
        S.mark('s5.gelu_tr')
        gT = PQ[:, 4096:8192].bitcast(BF16).rearrange("p (k t) -> p k t", k=8)
        Yf = R1.rearrange("p (l k c) -> p l k c", l=8, k=8)
        for k in range(8):
            pt, ptn = next_pp()
            ptb = pt[:].bitcast(BF16)
            for l in range(8):
                TR(ptb[:, l * 128:(l + 1) * 128], Yf[:, l, k, :], identb[:], ['R1', 'identb'], [ptn])
            CP(evac_eng(), gT[:, k, :].rearrange("p (c s) -> p s c", s=8),
               ptb[:, 0:1024].rearrange("p (s c) -> p s c", s=8), [ptn], ct('PQ', f'gT{k}', ('gT', L, ph_no[0])))
        S.mark('s5.glu')
        yaT = R2.rearrange("p (k t) -> p k t", k=8)
        sg = sgt[:]

        def glu_consume(m, pt, ptn, ms):
            ACT(sg, pt[:], AF.Sigmoid, [ptn, 'pers'], ['sgt'],
                bias=pers[:, L, C_BGLU + m:C_BGLU + m + 1])
            TT('dve', yaT[:, m, :], gT[:, m, :], sg, ALU.mult, ['PQ', 'sgt'], ['R2'])
        proj_fm(w_glu_d, L, 0, 1024, 8, gT, 'PQ', glu_consume, pre=pre_glu)
        held['xbc'] = prefetch_w(w_in_d, L, 2048, 1536, 8, nblocks=1)

    def norm_inplace(yT, tag, gain_col0, L, sq=None, sq_tag='PQ'):
        if sq is None:
            sq = PQ[:, 4096:8192].bitcast(BF16).rearrange("p (k t) -> p k t", k=8)
        rms_generic(yT, yT, gain_col0, L, sq, [tag], [tag], [sq_tag])

    def ssd_block(L):
        for i in range(48):
            TS('pool', cdiag[:, i, :], identb[:], pers[:, L, C_CW + i:C_CW + i + 1], None, ALU.mult, None,
               ['identb', 'pers'], ['cdiag'])
        xraw = PQ[:, 0:6168].bitcast(BF16)[:, 0:12 * 1027].rearrange("p (m t) -> p m t", m=12)
        CP('dve', xraw[:, :, 0:3], ctail[:, L, :, :], ['ctail', 'PQ'], ['PQ'])

        def xbc_consume(m, pt, ptn, ms):
            CP(evac_eng(), xraw[:, m, 3:3 + NT], pt[:], [ptn], [f'PQ/xr{m}'])
        proj_fm(w_in_d, L, 2048, 1536, 8, hT, 'hT', xbc_consume, pre=held.pop('xbc', None))
        CP('dve', ctail[:, L, :, :], xraw[:, :, NT:NT + 3], ['PQ'], ['ctail'])
        norm_inplace(R2.rearrange("p (k t) -> p k t", k=8), 'R2', C_GS5, L,
                     sq=R1.rearrange("p (k t) -> p k t", k=8), sq_tag='R1')
        S.mark('ssd.conv')
        xsT = R1.rearrange("p (k t) -> p k t", k=8)
        for m in range(12):
            pt, ptn = next_pp()
            for n in range(2):
                for tap in range(4):
                    MM(pt[:, n * 512:(n + 1) * 512], cdiag[:, m * 4 + tap, :],
                       xraw[:, m, tap + n * 512:tap + n * 512 + 512], tap == 0, tap == 3, [f'PQ/xr{m}', 'cdiag'], [ptn])
            dst = xsT[:, m, :] if m < 8 else BCT[:, m - 8, :]
            ACT(dst, pt[:], AF.Silu, [ptn, 'pers'], ['R1' if m < 8 else 'BCT'],
                bias=pers[:, L, C_CB + m:C_CB + m + 1])
        S.mark('ssd.dt')
        wv, wtag = load_w(w_in_d[L, :, 3584:3600], 8, 16)
        pt, ptn = next_pp()
        for n in range(2):
            for k in range(8):
                MM(pt[0:16, n * 512:(n + 1) * 512], wv[:, k, :], hT[:, k, n * 512:(n + 1) * 512], k == 0, k == 7,
                   [wtag, 'hT'], [ptn])
        o = [0]

        def tq(n, name):
            a_ = PQ[:, o[0]:o[0] + n]
            o[0] += n
            return a_, 'PQ/' + name
        e1 = rs
        dtp = rstd
        dta = rs
        nA, nA_t = tq(1, 'nA')
        S.add('dve', lambda e: e.memset(nA[:, 0:1], 0.0), [], ['PQ'])
        ACT(e1[0:16, :], pt[0:16, :], AF.Exp, [ptn, 'pers'], ['rs'], bias=pers[0:16, L, C_DTB:C_DTB + 1])
        ACT(dtp[0:16, :], e1[0:16, :], AF.Ln, ['rs'], ['rstd'], bias=1.0)
        ACT(nA[0:16, :], pers[0:16, L, C_ALOG:C_ALOG + 1], AF.Exp, ['pers'], [nA_t])
        TS('dve', dta[0:16, :], dtp[0:16, :], nA[0:16, 0:1], -1.0, ALU.mult, ALU.mult, ['rstd', nA_t], ['rs'])
        dtp_tm, dtp_tm_t = tq(128, 'dtp_tm')
        dta_tm, dta_tm_t = tq(128, 'dta_tm')
        pt2, ptn2 = next_pp()
        for c in range(8):
            TR(pt2[:, c * 16:(c + 1) * 16], dtp[0:16, c * 128:(c + 1) * 128], identf[0:16, 0:16], ['rstd', 'cst'], [ptn2])
            TR(pt2[:, 128 + c * 16:128 + (c + 1) * 16], dta[0:16, c * 128:(c + 1) * 128], identf[0:16, 0:16],
               ['rs', 'cst'], [ptn2])
        CP('dve', dtp_tm, pt2[:, 0:128], [ptn2], [dtp_tm_t])
        CP('dve', dta_tm, pt2[:, 128:256], [ptn2], [dta_tm_t])
        acum_tm, acum_t = tq(128, 'acum')
        nacum, nacum_t = tq(128, 'nacum')
        dec_tm, dec_t = tq(128, 'dec')
        dtd_tm, dtd_t = tq(128, 'dtd')
        alast, alast_t = tq(128, 'alast')
        edec, edec_t = tq(128, 'edec')
        pt3, ptn3 = next_pp()
        MM(pt3[:, 0:128], trif, dta_tm, True, True, [dta_tm_t, 'cst'], [ptn3])
        MM(pt3[:, 128:256], cst[:, K_ONE:K_ONE + 128], dta_tm, True, True, [dta_tm_t, 'cst'], [ptn3])
        CP('dve', acum_tm, pt3[:, 0:128], [ptn3], [acum_t])
        TS('dve', nacum, acum_tm, -1.0, None, ALU.mult, None, [acum_t], [nacum_t])
        CP('dve', alast, pt3[:, 128:256], [ptn3], [alast_t])
        TT('dve', dec_tm, alast, acum_tm, ALU.subtract, [alast_t, acum_t], [dec_t])
        ACT(dec_tm, dec_tm, AF.Exp, [dec_t], [dec_t])
        TT('dve', dtd_tm, dec_tm, dtp_tm, ALU.mult, [dec_t, dtp_tm_t], [dtd_t])
        ACT(edec, alast, AF.Exp, [alast_t], [edec_t])
        dtri_l, dtri_tl = [], []
        for g in range(2):
            a_, t_ = tq(1024, f'dtri{g}')
            dtri_l.append(a_); dtri_tl.append(t_)
        bfb = PQ[:, o[0]:8448].bitcast(BF16)
        ob = [0]

        def tb(n, name):
            a_ = bfb[:, ob[0]:ob[0] + n]
            ob[0] += n
            return a_, 'PQ/' + name
        xdt, xdt_t = tb(1024, 'xdt')
        xdtd, xdtd_t = tb(1024, 'xdtd')
        Btm, Btm_t = tb(256, 'Btm')
        stb, stb_t = tb(1024, 'stb')
        LTl = [tb(1024, f'LT{g}') for g in range(2)]
        EEl = [tb(1024, f'EE{g}') for g in range(2)]
        CBl = [tb(128, f'CBm{g}') for g in range(2)]
        pre_z = prefetch_w(w_in_d, L, 1024, 1024, 8, nblocks=2)
        S.mark('ssd.chunks')
        ybT = R5.rearrange("p (k t) -> p k t", k=8)
        ones_f = cst[:, K_ONE:K_ONE + 128]
        v8 = lambda a_: a_.rearrange("p (h l) -> p h l", h=8)
        for c in range(8):
            cs_ = slice(c * 128, (c + 1) * 128)
            ptx, ptxn = next_pp()
            ptxb = ptx[:].bitcast(BF16)
            for k in range(8):
                TR(ptxb[:, k * 128:(k + 1) * 128], xsT[:, k, cs_], identb[:], ['R1', 'identb'], [ptxn])
            for g in range(2):
                TR(ptxb[:, 1024 + g * 128:1024 + (g + 1) * 128], BCT[:, g, cs_], identb[:], ['BCT', 'identb'], [ptxn])
            bch = lambda a_: a_.unsqueeze(2).broadcast_to([128, 16, 64])
            for g in range(2):
                for h8 in range(8):
                    hcol = c * 16 + g * 8 + h8
                    S.add('act', lambda e, g=g, h8=h8, hcol=hcol: e.activation(
                        dtri_l[g][:, h8 * 128:(h8 + 1) * 128], trif, AF.Copy, scale=dta_tm[:, hcol:hcol + 1]),
                        [dta_tm_t, 'cst'], [dtri_tl[g]])
            TT('dve', xdt.rearrange("p (h q) -> p h q", h=16), ptxb[:, 0:1024].rearrange("p (h q) -> p h q", h=16),
               bch(dtp_tm[:, c * 16:(c + 1) * 16]), ALU.mult, [ptxn, dtp_tm_t], [xdt_t])
            TT('dve', xdtd.rearrange("p (h q) -> p h q", h=16), xdt.rearrange("p (h q) -> p h q", h=16),
               bch(dec_tm[:, c * 16:(c + 1) * 16]), ALU.mult, [xdt_t, dec_t], [xdtd_t])
            CP('act', Btm, ptxb[:, 1024:1280], [ptxn], [Btm_t])
            CP('act', stb, sstate[:, L, :], ['sstate'], [stb_t])
            pls = []
            pc, pcn = next_pp()
            for g in range(2):
                pl, pln = next_pp()
                pls.append((pl, pln))
                for hb in range(2):
                    MM(pl[:, hb * 512:(hb + 1) * 512], ones_f, dtri_l[g][:, hb * 512:(hb + 1) * 512], True, True,
                       [dtri_tl[g], 'cst'], [pln])
                MM(pc[:, g * 128:(g + 1) * 128], BCT[:, g, cs_], BCT[:, 2 + g, cs_], True, True, ['BCT'], [pcn])
            for g in range(2):
                ACT(EEl[g][0], pls[g][0][:], AF.Exp, [pls[g][1]], [EEl[g][1]])
            for g in range(2):
                pl, pln = pls[g]
                TT('dve', v8(pl[:]), v8(pl[:]),
                   acum_tm[:, c * 16 + g * 8:c * 16 + g * 8 + 8].unsqueeze(2).broadcast_to([128, 8, 128]), ALU.min,
                   [pln, acum_t], [pln])
                TT('dve', v8(pl[:]), v8(pl[:]),
                   nacum[:, c * 16 + g * 8:c * 16 + g * 8 + 8].unsqueeze(2).broadcast_to([128, 8, 128]), ALU.add,
                   [pln, nacum_t], [pln])
                TT('dve', CBl[g][0], pc[:, g * 128:(g + 1) * 128], trif, ALU.mult, [pcn, 'cst'], [CBl[g][1]])
            for g in range(2):
                ACT(LTl[g][0], pls[g][0][:], AF.Exp, [pls[g][1]], [LTl[g][1]])
            for g in range(2):
                TT('dve', v8(EEl[g][0]), v8(EEl[g][0]), BCT[:, 2 + g, cs_].unsqueeze(1).broadcast_to([128, 8, 128]), ALU.mult,
                   [EEl[g][1], 'BCT'], [EEl[g][1]])
                TT('dve', v8(LTl[g][0]), v8(LTl[g][0]), CBl[g][0].unsqueeze(1).broadcast_to([128, 8, 128]), ALU.mult,
                   [LTl[g][1], CBl[g][1]], [LTl[g][1]])
            for g in range(2):
                py, pyn = next_pp()
                MT, MT_t = LTl[g]
                CE, CE_t = EEl[g]
                for kk in range(4):
                    k = g * 4 + kk
                    for hh in range(2):
                        h = 2 * k + hh
                        h8 = h % 8
                        o_ = py[64 * hh:64 * hh + 64, kk * 128:(kk + 1) * 128]
                        MM(o_, xdt[:, h * 64:(h + 1) * 64], MT[:, h8 * 128:(h8 + 1) * 128], True, False, [xdt_t, MT_t], [pyn])
                        MM(o_, stb[:, h * 64:(h + 1) * 64], CE[:, h8 * 128:(h8 + 1) * 128], False, True, [stb_t, CE_t], [pyn])
                for kk in range(4):
                    k = g * 4 + kk
                    STT(ybT[:, k, cs_], xsT[:, k, cs_], pers[:, L, C_SD + k:C_SD + k + 1], py[:, kk * 128:(kk + 1) * 128],
                        ALU.mult, ALU.add, ['R1', 'pers', pyn], ['R5'])
            ps_, psn = next_pp()
            for g in range(2):
                MM(ps_[:, g * 512:(g + 1) * 512], Btm[:, g * 128:(g + 1) * 128], xdtd[:, g * 512:(g + 1) * 512], True, True,
                   [Btm_t, xdtd_t], [psn])
            sv = sstate[:, L, :].rearrange("p (h q) -> p h q", h=16)
            TT('pool', sv, sv, edec[:, c * 16:(c + 1) * 16].unsqueeze(2).broadcast_to([128, 16, 64]),
               ALU.mult, ['sstate', edec_t], ['sstate'])
            TT('dve', sstate[:, L, :], sstate[:, L, :], ps_[:, :], ALU.add, ['sstate', psn], ['sstate'])
        zs = sgt[:]

        def z_consume(m, pt, ptn, ms):
            ACT(zs, pt[:], AF.Silu, [ptn], ['sgt'])
            TT('dve', ybT[:, m, :], ybT[:, m, :], zs, ALU.mult, ['sgt', 'R5'], ['R5'])
        proj_fm(w_in_d, L, 1024, 1024, 8, hT, 'hT', z_consume, pre=pre_z)

    negones = sb("negones", [128, 128])
    S.add('dve', lambda e: e.memset(negones[:], -1.0), [], ['negones'])

    def out_proj(L):
        def consume(m, pt, ptn, ms):
            TT('dve', x[:, m, :], x[:, m, :], pt[:], ALU.add, [ptn, f'x/t{m}'], [f'x/t{m}'])
        ybT_ = R5.rearrange("p (k t) -> p k t", k=8)
        yaT_ = R2.rearrange("p (k t) -> p k t", k=8)
        norm_inplace(ybT_, 'R5', C_GSSD, L)
        proj_fm(w_out_d, L, 0, 1024, 8, yaT_, 'R2', consume, blk=512, row0=0)
        proj_fm(w_out_d, L, 0, 1024, 8, ybT_, 'R5', consume, blk=512, row0=1024)

    def proj_fm_dep2(*a, **k):
        return proj_fm(*a, **k)

    def ffn(L):
        pre_g = [load_w(w_gate_d[L, :, 0:128], 8, 128), load_w(w_up_d[L, :, 0:128], 8, 128)]
        rmsnorm_to(hT, C_GFFN, L)
        aT = BIG[:, 0:22 * NT].rearrange("p (k t) -> p k t", k=22)
        sg = sgt[:]
        for m in range(22):
            if m == 0:
                (wg, wgt), (wu, wut) = pre_g
            else:
                wg, wgt = load_w(w_gate_d[L, :, m * 128:(m + 1) * 128], 8, 128)
                wu, wut = load_w(w_up_d[L, :, m * 128:(m + 1) * 128], 8, 128)
            pg, pgn = next_pp()
            pu, pun = next_pp()
            for n in range(2):
                for k in range(8):
                    MM(pg[:, n * 512:(n + 1) * 512], wg[:, k, :], hT[:, k, n * 512:(n + 1) * 512], k == 0, k == 7,
                       [wgt, 'hT'], [pgn])
            for n in range(2):
                for k in range(8):
                    MM(pu[:, n * 512:(n + 1) * 512], wu[:, k, :], hT[:, k, n * 512:(n + 1) * 512], k == 0, k == 7,
                       [wut, 'hT'], [pun])
            ACT(sg, pg[:], AF.Silu, [pgn], ['sgt'])
            TT('dve', aT[:, m, :], sg, pu[:], ALU.mult, ['sgt', pun], ['R1', 'R2', 'R5'])

        def consume(m, pt, ptn, ms):
            TT('dve', x[:, m, :], x[:, m, :], pt[:], ALU.add, [ptn, f'x/t{m}'], [f'x/t{m}'])
        proj_fm(w_down_d, L, 0, 1024, 22, aT, ['R1', 'R2', 'R5'], consume, blk=128)

    for hf in range(NH):
        if not (hf == 0 and held.pop('x0', False)):
            for k in range(8):
                DMA('sp', x[:, k, :], xT_d[k * 128:(k + 1) * 128, hf * NT:(hf + 1) * NT], ['x'], ['x'])
        for L in range(DEPTH):
            if STAGE < 2:
                break
            S.mark(f'h{hf}L{L}.norm')
            wvs_u = [load_w(w_in_d[L, :, nb * 512:(nb + 1) * 512], 8, 512) for nb in range(2)]
            rmsnorm_to(hT, C_GMIX, L)
            if STAGE == 2:
                for k in range(8):
                    dump(hT[:, k, :], k * 1024, 1024, ['hT'])
                break
            S.mark(f'h{hf}L{L}.s5')
            s5_block(L, wvs_u)
            if STAGE == 3:
                for k in range(8):
                    dump(R2[:, k * 1024:(k + 1) * 1024], k * 1024, 1024, ['R2'])
                break
            S.mark(f'h{hf}L{L}.ssd')
            ssd_block(L)
            if STAGE == 4:
                for k in range(8):
                    dump(R5[:, k * 1024:(k + 1) * 1024], k * 1024, 1024, ['R5'])
                break
            S.mark(f'h{hf}L{L}.outproj')
            out_proj(L)
            if STAGE == 5:
                break
            S.mark(f'h{hf}L{L}.ffn')
            ffn(L)
            if STAGE == 6:
                break
        if STAGE < 99:
            for k in range(8):
                final.append(DMA('sp', out_d[k * 128:(k + 1) * 128, hf * NT:(hf + 1) * NT], x[:, k, :], ['x'], ['out']))
            break
        S.mark(f'h{hf}.final')
        if hf < NH - 1:
            ostage = BIG[:, 8192:24576].bitcast(F32).rearrange("p (k t) -> p k t", k=8)
            rms_generic(x, ostage, C_GFIN, 0, R1.rearrange("p (k t) -> p k t", k=8), ['x'], ['R2'], ['R1'], extra_w=['R5'],
                        src_tile=lambda k: [f'x/t{k}'])
            for k in range(8):
                final.append(DMA('sp', out_d[k * 128:(k + 1) * 128, hf * NT:(hf + 1) * NT], ostage[:, k, :],
                                 [f'R2/o{k}n0', f'R2/o{k}n1', 'R5'], ['out']))
        else:
            rms_generic(x, x, C_GFIN, 0, R1.rearrange("p (k t) -> p k t", k=8), ['x'], ['x'], ['R1'])
            for k in range(8):
                final.append(DMA('sp', out_d[k * 128:(k + 1) * 128, hf * NT:(hf + 1) * NT], x[:, k, :],
                                 [f'x/o{k}n0', f'x/o{k}n1'], ['out']))
    S.emit(final)


def _s5_lay(a):
    return a.reshape(32, 2, 64).transpose(1, 2, 0).reshape(128, 32)


def _consts():
    c = np.zeros((128, NCST), np.float32)
    i = np.arange(128)
    c[:, K_ID:K_ID + 128] = np.eye(128)
    c[:, K_TRI:K_TRI + 128] = (i[:, None] <= i[None, :])
    c[:, K_NEG:K_NEG + 128] = np.where(i[None, :] < i[:, None], -30000.0, 0.0)
    c[:, K_BLK:K_BLK + 128] = ((i[None, :] // 16) >= (i[:, None] // 16))
    c[:, K_ONE:K_ONE + 128] = 1.0
    return c


def _small_params(inp):
    sp = np.zeros((DEPTH, 128, NSP), np.float32)
    fm = lambda v: np.asarray(v, np.float32).reshape(8, 128).T
    for L in range(DEPTH):
        sp[L, :, C_LRE:C_LRE + 32] = _s5_lay(inp['s5_lam_re'][L])
        sp[L, :, C_LIM:C_LIM + 32] = _s5_lay(inp['s5_lam_im'][L])
        sp[L, :, C_LST:C_LST + 32] = _s5_lay(np.broadcast_to(inp['s5_log_step'][L][:, None], (64, 64)))
        for nm, c0 in (('s5_b_re', C_BRE), ('s5_b_im', C_BIM)):
            sp[L, :, c0:c0 + 512] = inp[nm][L].reshape(32, 2, 64, 16).transpose(1, 2, 0, 3).reshape(128, 512)
        for nm, c0 in (('s5_c_re', C_CRE), ('s5_c_im', C_CIM)):
            sp[L, :, c0:c0 + 512] = inp[nm][L].reshape(32, 2, 16, 64).transpose(1, 3, 0, 2).reshape(128, 512)
        P = C_P0
        sp[L, :, P + C_DCOL:P + C_DCOL + 64] = np.tile(inp['s5_d'][L].reshape(64, 16).T, (8, 1))
        sp[L, :, P + C_GMIX:P + C_GMIX + 8] = fm(inp['norm_mix'][L])
        sp[L, :, P + C_GS5:P + C_GS5 + 8] = fm(inp['s5_norm'][L])
        sp[L, :, P + C_GSSD:P + C_GSSD + 8] = fm(inp['ssd_norm'][L])
        sp[L, :, P + C_GFFN:P + C_GFFN + 8] = fm(inp['norm_ffn'][L])
        sp[L, :, P + C_GFIN:P + C_GFIN + 8] = fm(inp['norm_final'])
        sp[L, :, P + C_BGLU:P + C_BGLU + 8] = fm(inp['s5_b_glu'][L])
        sp[L, :, P + C_CW:P + C_CW + 48] = inp['ssd_conv_w'][L].T.reshape(12, 128, 4).transpose(1, 0, 2).reshape(128, 48)
        sp[L, :, P + C_CB:P + C_CB + 12] = inp['ssd_conv_b'][L].reshape(12, 128).T
        sp[L, 0:16, P + C_DTB] = inp['ssd_dt_bias'][L]
        sp[L, 0:16, P + C_ALOG] = inp['ssd_a_log'][L]
        sp[L, :, P + C_SD:P + C_SD + 8] = np.repeat(inp['ssd_d'][L], 64).reshape(8, 128).T
    return sp


def make_in_maps(inp, cores):
    inp = {k: np.asarray(v) for k, v in inp.items()}
    sp = _small_params(inp)
    cst = _consts()
    shared = dict(w_in=np.ascontiguousarray(inp['w_in'], np.float32), w_glu=np.ascontiguousarray(inp['s5_w_glu'], np.float32),
                  w_out=np.ascontiguousarray(inp['w_out'], np.float32), w_gate=np.ascontiguousarray(inp['w_gate'], np.float32),
                  w_up=np.ascontiguousarray(inp['w_up'], np.float32), w_down=np.ascontiguousarray(inp['w_down'], np.float32),
                  sp=sp, cst=cst)
    maps = []
    for b in cores:
        m = dict(shared)
        m['xT'] = np.ascontiguousarray(inp['x'][b].T, np.float32)
        maps.append(m)
    return maps


_NC = [None]


def kernel(**inputs):
    if _NC[0] is None:
        _NC[0] = build_nc()
    nc = _NC[0]
    maps = make_in_maps(inputs, list(range(8)))
    res = run_bass_kernel_spmd(nc, maps, core_ids=list(range(8)))
    out = np.stack([np.asarray(r["outT"]).T for r in res.results], 0)
    return np.ascontiguousarray(out, np.float32)
```

```python
import math
import os
from contextlib import ExitStack
import numpy as np
import ml_dtypes
import concourse.bass as bass
import concourse.mybir as mybir
from concourse.bass_utils import run_bass_kernel_spmd

F32 = mybir.dt.float32
BF16 = mybir.dt.bfloat16
AF = mybir.ActivationFunctionType
ALU = mybir.AluOpType

NT = 1024
NH = 2
DEPTH = 2
EPS = 1e-6
C_LRE, C_LIM, C_LST, C_BRE, C_BIM, C_CRE, C_CIM = 0, 32, 64, 96, 608, 1120, 1632
C_P0 = 2144
C_DCOL, C_GMIX, C_GS5, C_GSSD, C_GFFN, C_GFIN, C_BGLU, C_CW, C_CB, C_DTB, C_ALOG, C_SD = (
    0, 64, 72, 80, 88, 96, 104, 112, 160, 172, 173, 174)
NPERS = 184
NSP = C_P0 + NPERS
K_ID, K_TRI, K_NEG, K_BLK, K_ONE = 0, 128, 256, 384, 512
NCST = 640


class Sched:
    def __init__(self, nc, es):
        self.nc = nc
        self.ops = {e: [] for e in ('pe', 'act', 'dve', 'pool', 'sp')}
        self.cnt = {e: 0 for e in self.ops}
        self.sems = {e: [es.enter_context(nc.semaphore(f"s_{e}{i}")) for i in range(2)] for e in self.ops}
        self.dsems = {q: [es.enter_context(nc.semaphore(f"s_dma_{q}{i}")) for i in range(8)] for q in ('sp', 'pool')}
        self.dval = {q: [0] * 8 for q in ('sp', 'pool')}
        self.dnext = {q: 0 for q in ('sp', 'pool')}
        self.waited = {}
        self.lastw = {}
        self.readers = {}
        self.children = {}
        self.marks = []

    def _deps(self, b, is_write):
        out = []
        if '/' in b:
            par = b.split('/')[0]
            self.children.setdefault(par, set()).add(b)
            tags = [b, par]
        else:
            tags = [b] + list(self.children.get(b, ()))
        for t in tags:
            if t in self.lastw:
                out.append(self.lastw[t])
            if is_write:
                out.extend(self.readers.get(t, ()))
        return out

    def add(self, eng, fn, r=(), w=(), dma=False):
        deps = []
        for b in r:
            deps.extend(self._deps(b, False))
        for b in w:
            deps.extend(self._deps(b, True))
        if dma:
            i = self.dnext[eng]
            self.dnext[eng] = (i + 1) % 8
            if self.dval[eng][i] > 0:
                deps.append(('dma', eng, i, self.dval[eng][i]))
            self.dval[eng][i] += 16
            tok = ('dma', eng, i, self.dval[eng][i])
        else:
            k = self.cnt[eng]
            self.cnt[eng] += 1
            tok = ('op', eng, k)
        need = []
        for d in deps:
            if d[0] == 'op' and d[1] == eng and not dma and eng == 'pe':
                continue
            need.append(d)
        self.ops[eng].append([need, fn, tok])
        for b in w:
            self.lastw[b] = tok
            self.readers[b] = []
        for b in r:
            self.readers.setdefault(b, []).append(tok)
        return tok

    def mark(self, name):
        self.marks.append((name, dict(self.cnt)))

    def emit(self, final_tokens):
        nc = self.nc
        if os.environ.get('KMARKS'):
            import json
            json.dump(self.marks, open(os.environ['KMARKS'], 'w'))
        targets = {e: set() for e in self.ops}
        for e, lst in self.ops.items():
            for need, fn, tok in lst:
                best = {}
                for d in need:
                    if d[0] == 'op':
                        best[d[1]] = max(best.get(d[1], -1), d[2])
                for pe_, k in best.items():
                    targets[pe_].add(k)
        for d in final_tokens:
            if d[0] == 'op':
                targets[d[1]].add(d[2])
        for e in self.ops:
            if e != 'pe':
                targets[e] = set(range(self.cnt[e]))
        rank = {}
        for e, ks in targets.items():
            for r_, k in enumerate(sorted(ks)):
                rank[(e, k)] = r_

        def semval(d):
            if d[0] == 'dma':
                return self.dsems[d[1]][d[2]], d[3]
            r_ = rank[(d[1], d[2])]
            return self.sems[d[1]][r_ % 2], r_ // 2 + 1
        plans = {}
        for e, lst in self.ops.items():
            waited = {}
            plan = []
            for need, fn, tok in lst:
                best = {}
                for d in need:
                    if d[0] == 'op':
                        key = ('op', d[1])
                        if key not in best or best[key][2] < d[2]:
                            best[key] = d
                    else:
                        key = ('dma', d[1], d[2])
                        if key not in best or best[key][3] < d[3]:
                            best[key] = d
                waits = []
                for d in best.values():
                    sem, val = semval(d)
                    if waited.get(id(sem), 0) >= val:
                        continue
                    waited[id(sem)] = val
                    waits.append((sem, val))
                if tok[0] == 'dma':
                    inc = (self.dsems[tok[1]][tok[2]], 16)
                elif (tok[1], tok[2]) in rank:
                    r_ = rank[(tok[1], tok[2])]
                    inc = (self.sems[tok[1]][r_ % 2], 1)
                else:
                    inc = None
                plan.append((waits, fn, inc))
            plans[e] = plan
        fin = [semval(d) for d in final_tokens]
        with nc.Block() as block:
            def runner(name):
                def body(e):
                    for waits, fn, inc in plans[name]:
                        for (s_, v) in waits:
                            e.wait_ge(s_, v)
                        ins = fn(e)
                        if inc is not None:
                            ins.then_inc(inc[0], inc[1])
                    if name == 'pool':
                        for (s_, v) in fin:
                            e.wait_ge(s_, v)
                return body
            block.tensor(runner('pe'))
            block.scalar(runner('act'))
            block.vector(runner('dve'))
            block.gpsimd(runner('pool'))
            block.sync(runner('sp'))


import os
STAGE = int(os.environ.get('KSTAGE', '99'))
SUB = int(os.environ.get('KSUB', '99'))


def build_nc(dbg_spec=None):
    if dbg_spec is None and STAGE < 99:
        dbg_spec = ('dbg', [128, 32768])
    nc = bass.Bass("TRN2", target_bir_lowering=False)
    es = ExitStack()
    with es:
        _build(nc, es, dbg_spec)
    return nc


def _build(nc, es, dbg_spec):
    def dram(name, shape, dt=F32, kind="ExternalInput"):
        return nc.dram_tensor(name, shape, dt, kind=kind).ap()

    xT_d = dram("xT", [1024, NH * NT])
    out_d = dram("outT", [1024, NH * NT], kind="ExternalOutput")
    w_in_d = dram("w_in", [DEPTH, 1024, 3600])
    w_glu_d = dram("w_glu", [DEPTH, 1024, 1024])
    w_out_d = dram("w_out", [DEPTH, 2048, 1024])
    w_gate_d = dram("w_gate", [DEPTH, 1024, 2816])
    w_up_d = dram("w_up", [DEPTH, 1024, 2816])
    w_down_d = dram("w_down", [DEPTH, 2816, 1024])
    sp_d = dram("sp", [DEPTH, 128, NSP])
    cst_d = dram("cst", [128, NCST])
    tabM_d = dram("tabM", [DEPTH, 128, 8192], BF16, kind="Internal")
    tabG1T_d = dram("tabG1T", [DEPTH, 128, 8192], BF16, kind="Internal")
    tabG2_d = dram("tabG2", [DEPTH, 2, 128, 8192], BF16, kind="Internal")
    tabW_d = dram("tabW", [DEPTH, 4, 128, 4096], BF16, kind="Internal")
    dbg_d = None
    if dbg_spec is not None:
        dbg_d = dram("dbg", list(dbg_spec[1]), F32, kind="ExternalOutput")

    def sb(name, shape, dt=F32):
        return es.enter_context(nc.sbuf_tensor(name, shape, dt))

    S = Sched(nc, es)
    x = sb("x", [128, 8, NT])
    hT = sb("hT", [128, 8, NT], BF16)
    BIG = sb("BIG", [128, 24576], BF16)
    PQ = sb("PQ", [128, 8448])
    BCT = sb("BCT", [128, 4, NT], BF16)
    rs = sb("rs", [128, NT])
    rstd = sb("rstd", [128, NT])
    NWB = 4
    wbuf = [sb(f"wbuf{i}", [128, 4096], BF16) for i in range(NWB)]
    cst = sb("cst_sb", [128, NCST])
    identb = sb("identb", [128, 128], BF16)
    onesb = sb("onesb", [128, 128], BF16)
    pers = sb("pers", [128, DEPTH, NPERS])
    T1 = sb("T1", [128, DEPTH, 64])
    T2 = sb("T2", [128, DEPTH, 64])
    qcar = sb("qcar", [128, DEPTH, 64])
    RHO = sb("RHO", [128, DEPTH, 32])
    W128 = sb("W128", [128, DEPTH, 64])
    sstate = sb("sstate", [128, DEPTH, 1024])
    ctail = sb("ctail", [128, DEPTH, 12, 3], BF16)
    cdiag = sb("cdiag", [128, 48, 128], BF16)
    smalls = sb("smalls", [128, 64])
    sgt = sb("sgt", [128, NT], BF16)
    pp = [es.enter_context(nc.psum_tensor(f"pp{i}", [128, 1024], F32)) for i in range(4)]
    ppn = [f"pp{i}" for i in range(4)]

    R1 = BIG[:, 0:8192]
    R2 = BIG[:, 8192:16384]
    R5 = BIG[:, 16384:24576]
    identf = cst[:, K_ID:K_ID + 128]
    trif = cst[:, K_TRI:K_TRI + 128]
    negf = cst[:, K_NEG:K_NEG + 128]
    blkf = cst[:, K_BLK:K_BLK + 128]

    def TT(eng, out, a, b, op, r, w):
        S.add(eng, lambda e: e.tensor_tensor(out, a, b, op), r, w)

    def TS(eng, out, a, s1, s2, op0, op1, r, w):
        if op1 is None:
            S.add(eng, lambda e: e.tensor_scalar(out, a, s1, None, op0), r, w)
        else:
            S.add(eng, lambda e: e.tensor_scalar(out, a, s1, s2, op0, op1), r, w)

    def STT(out, a, sc, b, op0, op1, r, w):
        S.add('dve', lambda e: e.scalar_tensor_tensor(out, a, sc, b, op0, op1), r, w)

    def ACT(out, a, func, r, w, bias=None, scale=None):
        kw = {}
        if bias is not None:
            kw['bias'] = bias
        if scale is not None:
            kw['scale'] = scale
        S.add('act', lambda e: e.activation(out, a, func, **kw), r, w)

    def CP(eng, out, a, r, w):
        if eng == 'act':
            S.add('act', lambda e: e.copy(out, a), r, w)
        else:
            S.add(eng, lambda e: e.tensor_copy(out, a), r, w)

    def MM(out, lhsT, rhs, start, stop, r, w):
        S.add('pe', lambda e: e.matmul(out, lhsT, rhs, start=start, stop=stop), r, w)

    def TR(out, in_, ident, r, w):
        S.add('pe', lambda e: e.transpose(out, in_, ident), r, w)

    def DMA(eng, out, in_, r, w):
        return S.add(eng, lambda e: e.dma_start(out=out, in_=in_), r, w, dma=True)

    wb_i = [0]

    def next_wb():
        i = wb_i[0] % NWB
        wb_i[0] += 1
        return i

    DMA('sp', cst[:], cst_d[:, :], [], ['cst'])
    CP('dve', identb[:], identf, ['cst'], ['identb'])
    CP('dve', onesb[:], cst[:, K_ONE:K_ONE + 128], ['cst'], ['onesb'])
    for L in range(DEPTH):
        DMA('sp', pers[:, L, :], sp_d[L, :, C_P0:C_P0 + NPERS], [], ['pers'])
    S.add('dve', lambda e: e.memset(qcar[:], 0.0), [], ['qcar'])
    S.add('dve', lambda e: e.memset(sstate[:], 0.0), [], ['sstate'])
    S.add('dve', lambda e: e.memset(ctail[:], 0.0), [], ['ctail'])

    held = {}

    def gen_tables(L):
        prm = PQ[:, 0:C_P0]
        DMA('sp', prm, sp_d[L, :, 0:C_P0], ['PQ'], ['PQ'])
        base = [C_P0]

        def tmp(n):
            a = PQ[:, base[0]:base[0] + n]
            base[0] += n
            return a
        R, W = ['PQ', 'cst'], ['PQ']
        lr = prm[:, C_LRE:C_LRE + 32]
        li = prm[:, C_LIM:C_LIM + 32]
        step = tmp(32); th = tmp(32); aa = tmp(32)
        ACT(step, prm[:, C_LST:C_LST + 32], AF.Exp, R, W)
        TT('dve', th, li, step, ALU.mult, R, W)
        TT('dve', aa, lr, step, ALU.mult, R, W)
        mag = tmp(32); sn = tmp(32); cs = tmp(32)
        ACT(mag, aa, AF.Exp, R, W, scale=1.0 / 16)
        ACT(sn, th, AF.Sin, R, W, scale=1.0 / 16)
        hp = tmp(1)
        S.add('dve', lambda e: e.memset(hp, math.pi / 2), R, W)
        ACT(cs, th, AF.Sin, R, W, scale=1.0 / 16, bias=hp)
        zr = tmp(32); zi = tmp(32); t0 = tmp(32); t1 = tmp(32)
        TT('dve', zr, mag, cs, ALU.mult, R, W)
        TT('dve', zi, mag, sn, ALU.mult, R, W)
        for _ in range(4):
            TT('dve', t0, zr, zr, ALU.mult, R, W)
            TT('dve', t1, zi, zi, ALU.mult, R, W)
            TT('dve', zi, zr, zi, ALU.mult, R, W)
            TS('dve', zi, zi, 2.0, None, ALU.mult, None, R, W)
            TT('dve', zr, t0, t1, ALU.subtract, R, W)
        ar, ai = zr, zi
        ir = tmp(32); ii = tmp(32)
        TT('dve', t0, ar, ar, ALU.mult, R, W)
        TT('dve', t1, ai, ai, ALU.mult, R, W)
        TT('dve', t0, t0, t1, ALU.add, R, W)
        S.add('dve', lambda e: e.reciprocal(t0, t0), R, W)
        TT('dve', ir, ar, t0, ALU.mult, R, W)
        TT('dve', ii, ai, t0, ALU.mult, R, W)
        TS('dve', ii, ii, -1.0, None, ALU.mult, None, R, W)
        PWr = tmp(9 * 32); PWi = tmp(9 * 32); NPr = tmp(8 * 32); NPi = tmp(8 * 32)

        t2 = tmp(32); t3 = tmp(32)
        for (Pr, Pi, tg) in ((PWr, PWi, 'PQ/pa'), (NPr, NPi, 'PQ/pb')):
            S.add('dve', lambda e, Pr=Pr: e.memset(Pr[:, 0:32], 1.0), [tg], [tg])
            S.add('dve', lambda e, Pi=Pi: e.memset(Pi[:, 0:32], 0.0), [tg], [tg])
        chains = [(PWr, PWi, 9, ar, ai, t0, t1, 'PQ/pa'), (NPr, NPi, 8, ir, ii, t2, t3, 'PQ/pb')]
        for j in range(1, 9):
            steps = []
            for (Pr, Pi, n, br_, bi_, ta_, tb2, tg) in chains:
                if j >= n:
                    continue
                pr, pi = Pr[:, (j - 1) * 32:j * 32], Pi[:, (j - 1) * 32:j * 32]
                qr, qi = Pr[:, j * 32:(j + 1) * 32], Pi[:, j * 32:(j + 1) * 32]
                steps.append([(ta_, pr, br_, ALU.mult, tg), (tb2, pi, bi_, ALU.mult, tg), (qr, ta_, tb2, ALU.subtract, tg),
                              (ta_, pr, bi_, ALU.mult, tg), (tb2, pi, br_, ALU.mult, tg), (qi, ta_, tb2, ALU.add, tg)])
            for i6 in range(6):
                for st in steps:
                    o_, a_, b_, op_, tg = st[i6]
                    TT('dve', o_, a_, b_, op_, [tg], [tg])
        if SUB == 1:
            return
        nr = tmp(32); den = tmp(32); cr = tmp(32); ci = tmp(32)
        TS('dve', nr, ar, -1.0, None, ALU.add, None, R, W)
        TT('dve', t0, lr, lr, ALU.mult, R, W)
        TT('dve', t1, li, li, ALU.mult, R, W)
        TT('dve', den, t0, t1, ALU.add, R, W)
        S.add('dve', lambda e: e.reciprocal(den, den), R, W)
        TT('dve', t0, nr, lr, ALU.mult, R, W)
        TT('dve', t1, ai, li, ALU.mult, R, W)
        TT('dve', t0, t0, t1, ALU.add, R, W)
        TT('dve', cr, t0, den, ALU.mult, R, W)
        TT('dve', t0, ai, lr, ALU.mult, R, W)
        TT('dve', t1, nr, li, ALU.mult, R, W)
        TT('dve', t0, t0, t1, ALU.subtract, R, W)
        TT('dve', ci, t0, den, ALU.mult, R, W)
        v3 = lambda a: a.rearrange("p (g h) -> p g h", h=16)
        bc = lambda a: a.unsqueeze(2).broadcast_to([128, 32, 16])
        bre = v3(prm[:, C_BRE:C_BRE + 512]); bim = v3(prm[:, C_BIM:C_BIM + 512])
        cre = v3(prm[:, C_CRE:C_CRE + 512]); cim = v3(prm[:, C_CIM:C_CIM + 512])
        bbr = v3(tmp(512)); bbi = v3(tmp(512)); u0 = v3(tmp(512)); u1 = v3(tmp(512))
        TT('dve', u0, bre, bc(cr), ALU.mult, R, W)
        TT('dve', u1, bim, bc(ci), ALU.mult, R, W)
        TT('dve', bbr, u0, u1, ALU.subtract, R, W)
        TT('dve', u0, bim, bc(cr), ALU.mult, R, W)
        TT('dve', u1, bre, bc(ci), ALU.mult, R, W)
        TT('dve', bbi, u0, u1, ALU.add, R, W)
        G1 = R1.rearrange("p (r g s h) -> p r g s h", r=2, g=32, s=8)
        G2 = R2.rearrange("p (r g s h) -> p r g s h", r=2, g=32, s=8)
        t5w = R5.bitcast(F32)
        w0 = t5w[:, 0:2048].rearrange("p (g s h) -> p g s h", g=32, s=4)
        w1 = t5w[:, 2048:4096].rearrange("p (g s h) -> p g s h", g=32, s=4)
        bs = lambda a_: a_.unsqueeze(2).broadcast_to([128, 32, 4, 16])
        ps_ = lambda tab, s0: tab.rearrange("p (s g) -> p g s", g=32)[:, :, s0:s0 + 4].unsqueeze(3).broadcast_to([128, 32, 4, 16])
        RW5 = R + ['R5']
        for s0 in (0, 4):
            nr_, ni_ = ps_(NPr[:, 0:256], s0), ps_(NPi[:, 0:256], s0)
            pr_, pi_ = ps_(PWr[:, 0:256], s0), ps_(PWi[:, 0:256], s0)
            TT('dve', w0, bs(bbr), nr_, ALU.mult, RW5, ['R5'])
            TT('dve', w1, bs(bbi), ni_, ALU.mult, RW5, ['R5'])
            TT('dve', G1[:, 0, :, s0:s0 + 4, :], w0, w1, ALU.subtract, RW5, ['R1'])
            TT('dve', w0, bs(bbr), ni_, ALU.mult, RW5 + ['R1'], ['R5'])
            TT('dve', w1, bs(bbi), nr_, ALU.mult, RW5 + ['R1'], ['R5'])
            TT('dve', G1[:, 1, :, s0:s0 + 4, :], w0, w1, ALU.add, RW5, ['R1'])
            TT('dve', w0, bs(cre), pr_, ALU.mult, RW5 + ['R1'], ['R5'])
            TT('dve', w1, bs(cim), pi_, ALU.mult, RW5 + ['R1'], ['R5'])
            TT('dve', G2[:, 0, :, s0:s0 + 4, :], w0, w1, ALU.subtract, RW5, ['R2'])
            TT('dve', w0, bs(cre), pi_, ALU.mult, RW5 + ['R2'], ['R5'])
            TT('dve', w1, bs(cim), pr_, ALU.mult, RW5 + ['R2'], ['R5'])
            TT('dve', w0, w0, w1, ALU.add, RW5, ['R5'])
            TS('dve', G2[:, 1, :, s0:s0 + 4, :], w0, -1.0, None, ALU.mult, None, RW5, ['R2'])
        if SUB == 2:
            return
        G1f = R1.rearrange("p (r g a) -> p r g a", r=2, g=32)
        G2f = R2.rearrange("p (r g a) -> p r g a", r=2, g=32)
        G2z = [hT[:].rearrange("p k t -> p (k t)"), x[:].rearrange("p k t -> p (k t)").bitcast(BF16)[:, 0:8192]]
        G2ztag = ['hT', 'x']
        for g2 in range(2):
            S.add('dve', lambda e, g2=g2: e.memset(G2z[g2], 0.0), [], [G2ztag[g2]])
            CP('dve', G2z[g2][64 * g2:64 * g2 + 64, :], R2[64 * g2:64 * g2 + 64, :], ['R2'], [G2ztag[g2]])
            DMA('sp', tabG2_d[L, g2, :, :], G2z[g2], [G2ztag[g2]], ['tabG2'])
        G2zf = [g.rearrange("p (r g a) -> p r g a", r=2, g=32) for g in G2z]
        if SUB == 3:
            return
        Mst = R5.rearrange("p (g a) -> p g a", g=64)
        dcol = pers[:, L, C_DCOL:C_DCOL + 64]
        dd = PQ[:, 6144:6144 + 1024].rearrange("p (g a) -> p g a", g=8)
        mt = PQ[:, 7168:7168 + 1024].rearrange("p (g a) -> p g a", g=8)
        ddb = PQ[:, 6144:6656].bitcast(BF16).rearrange("p (g a) -> p g a", g=8)
        for b8 in range(8):
            pt = pp[b8 % 2]
            ptn = ppn[b8 % 2]
            TT('dve', ddb, identf.unsqueeze(1).broadcast_to([128, 8, 128]),
               dcol[:, b8 * 8:(b8 + 1) * 8].unsqueeze(2).broadcast_to([128, 8, 128]), ALU.mult, ['cst', 'pers'], ['PQd'])
            for j in range(8):
                g = b8 * 8 + j
                gp, g2 = g // 2, g % 2
                for ri in range(2):
                    MM(pt[:, j * 128:(j + 1) * 128], G1f[:, ri, gp, :], G2zf[g2][:, ri, gp, :], ri == 0, False,
                       ['R1', G2ztag[g2]], [ptn])
                MM(pt[:, j * 128:(j + 1) * 128], identb[:], ddb[:, j, :], False, True, ['identb', 'PQd'], [ptn])
            TT('dve', Mst[:, b8 * 8:(b8 + 1) * 8, :], pt[:].rearrange("p (g a) -> p g a", g=8),
               blkf.unsqueeze(1).broadcast_to([128, 8, 128]), ALU.mult, [ptn, 'cst'], ['R5'])
        DMA('sp', tabM_d[L, :, :], R5, ['R5'], ['tabM'])
        if L == DEPTH - 1 and STAGE >= 99:
            for k in range(8):
                DMA('sp', x[:, k, :], xT_d[k * 128:(k + 1) * 128, 0:NT], ['x'], ['x'])
            held['x0'] = True
        if SUB == 4:
            return
        G1Tst = R5.rearrange("p (r g a) -> p r g a", r=2, g=32)
        for b4 in range(4):
            pt = pp[2 + b4 % 2]
            ptn = ppn[2 + b4 % 2]
            ptb = pt[:].bitcast(BF16)
            for j in range(16):
                idx = b4 * 16 + j
                ri, gp = idx // 32, idx % 32
                TR(ptb[:, j * 128:(j + 1) * 128], G1f[:, ri, gp, :], identb[:], ['R1', 'identb'], [ptn])
            CP('act', R5[:, b4 * 2048:(b4 + 1) * 2048], ptb, [ptn], ['R5'])
        DMA('sp', tabG1T_d[L, :, :], R5, ['R5'], ['tabG1T'])
        irho = tmp(32); Vr = tmp(32); Vi = tmp(32); v0 = tmp(32); v1 = tmp(32)
        RW = ['PQ', 'RHO']
        ACT(RHO[:, L, :], aa, AF.Exp, R, RW, scale=8.0)
        S.add('dve', lambda e: e.reciprocal(irho, RHO[:, L, :]), RW, W)
        TT('dve', Vr, PWr[:, 256:288], irho, ALU.mult, R, W)
        TT('dve', Vi, PWi[:, 256:288], irho, ALU.mult, R, W)
        Wc = R1.bitcast(F32).rearrange("p (g c) -> p g c", g=32)
        Ws = R2.bitcast(F32).rearrange("p (g c) -> p g c", g=32)
        t5 = R5.bitcast(F32)
        S.add('dve', lambda e: e.memset(Wc[:, :, 0:1], 1.0), ['R1'], ['R1'])
        S.add('dve', lambda e: e.memset(Ws[:, :, 0:1], 0.0), ['R2'], ['R2'])
        k = 1
        while k < 128:
            ta = t5[:, 0:32 * k].rearrange("p (g c) -> p g c", g=32)
            tb_ = t5[:, 2048:2048 + 32 * k].rearrange("p (g c) -> p g c", g=32)
            vrb = Vr.unsqueeze(2).broadcast_to([128, 32, k])
            vib = Vi.unsqueeze(2).broadcast_to([128, 32, k])
            RR_ = ['R1', 'R2', 'R5', 'PQ']
            TT('dve', ta, Wc[:, :, 0:k], vrb, ALU.mult, RR_, ['R5'])
            TT('dve', tb_, Ws[:, :, 0:k], vib, ALU.mult, RR_, ['R5'])
            TT('dve', Wc[:, :, k:2 * k], ta, tb_, ALU.subtract, RR_, ['R1'])
            TT('dve', ta, Wc[:, :, 0:k], vib, ALU.mult, RR_, ['R5'])
            TT('dve', tb_, Ws[:, :, 0:k], vrb, ALU.mult, RR_, ['R5'])
            TT('dve', Ws[:, :, k:2 * k], ta, tb_, ALU.add, RR_, ['R2'])
            TT('dve', v0, Vr, Vr, ALU.mult, R, W)
            TT('dve', v1, Vi, Vi, ALU.mult, R, W)
            TT('dve', Vi, Vr, Vi, ALU.mult, R, W)
            TS('dve', Vi, Vi, 2.0, None, ALU.mult, None, R, W)
            TT('dve', Vr, v0, v1, ALU.subtract, R, W)
            k *= 2
        CP('dve', W128[:, L, 0:32], Vr, R, ['W128'])
        CP('dve', W128[:, L, 32:64], Vi, R, ['W128'])
        st5 = R5.rearrange("p (t g c) -> p t g c", t=2, g=32)
        rb = RHO[:, L, :].unsqueeze(2).broadcast_to([128, 32, 128])
        TT('dve', st5[:, 0], Wc, rb, ALU.mult, ['R1', 'RHO'], ['R5'])
        TT('dve', st5[:, 1], Ws, rb, ALU.mult, ['R2', 'RHO'], ['R5'])
        DMA('sp', tabW_d[L, 0:2, :, :].rearrange("t p n -> p t n"), R5.rearrange("p (t n) -> p t n", t=2), ['R5'], ['tabW'])
        CP('dve', st5[:, 0], Wc, ['R1'], ['R5'])
        CP('act', st5[:, 1], Ws, ['R2'], ['R5'])
        DMA('sp', tabW_d[L, 2:4, :, :].rearrange("t p n -> p t n"), R5.rearrange("p (t n) -> p t n", t=2), ['R5'], ['tabW'])

    final = []

    dbgst = sb("dbgst", [128, 256]) if dbg_d is not None else None

    def dump(tile_ap, col0, ncols, tags):
        for c in range(0, ncols, 256):
            n = min(256, ncols - c)
            CP('dve', dbgst[:, 0:n], tile_ap[:, c:c + n], tags, ['dbgst'])
            final.append(DMA('sp', dbg_d[:, col0 + c:col0 + c + n], dbgst[:, 0:n], ['dbgst'], ['dbg']))

    if STAGE >= 1:
        for L in range(DEPTH if STAGE > 1 else 1):
            S.mark(f'tables{L}')
            gen_tables(L)
    if STAGE == 1 and SUB != 5:
        DMA('sp', R1, tabM_d[0, :, :], ['tabM', 'R1'], ['R1'])
        dump(R1, 0, 8192, ['R1'])
        DMA('sp', R2, tabG1T_d[0, :, :], ['tabG1T', 'R2'], ['R2'])
        dump(R2, 8192, 8192, ['R2'])
        DMA('sp', R5, tabG2_d[0, 0, :, :], ['tabG2', 'R5'], ['R5'])
        dump(R5, 16384, 8192, ['R5'])
        dump(T1[:, 0, :], 24576, 64, ['T12'])
        dump(T2[:, 0, :], 24640, 64, ['T12'])

    epsT = sb("epsT", [128, 1])
    S.add('dve', lambda e: e.memset(epsT[:], EPS), [], ['epsT'])

    def rms_generic(src, dst, gain_col0, L, sq, src_tags, dst_tags, sq_tags, extra_w=()):
        hs = [slice(0, 512), slice(512, 1024)]
        sqp = sq_tags[0].split('/')[0]
        dstp = dst_tags[0].split('/')[0]
        inplace = (dst_tags == src_tags)
        first = [True]

        def sqw(k, nh):
            t = [f'{sqp}/sq{k}n{nh}']
            if first[0]:
                first[0] = False
                t = [sqp] + t
            return t
        for nh in range(2):
            for k in range(8):
                if k % 2 == 0:
                    ACT(sq[:, k, hs[nh]], src[:, k, hs[nh]], AF.Square, src_tags, sqw(k, nh))
                else:
                    TT('dve', sq[:, k, hs[nh]], src[:, k, hs[nh]], src[:, k, hs[nh]], ALU.mult, src_tags, sqw(k, nh))
        for nh in range(2):
            for k in range(8):
                MM(pp[0][:, hs[nh]], onesb[:], sq[:, k, hs[nh]], k == 0, k == 7, [f'{sqp}/sq{k}n{nh}', 'onesb'], [f'pp0/n{nh}'])
        for nh in range(2):
            ACT(rs[:, hs[nh]], pp[0][:, hs[nh]], AF.Ln, [f'pp0/n{nh}', 'epsT'], [f'rs/n{nh}'], bias=epsT[:], scale=1.0 / 1024)
        for nh in range(2):
            ACT(rstd[:, hs[nh]], rs[:, hs[nh]], AF.Exp, [f'rs/n{nh}'], [f'rstd/n{nh}'], scale=-0.5)
        firstd = [True]
        for nh in range(2):
            for k in range(8):
                g = pers[:, L, gain_col0 + k:gain_col0 + k + 1]
                wt = [f'{dstp}/o{k}n{nh}']
                if firstd[0]:
                    firstd[0] = False
                    wt = [dstp] + wt
                STT(dst[:, k, hs[nh]], src[:, k, hs[nh]], g, rstd[:, hs[nh]], ALU.mult, ALU.mult,
                    src_tags + ['pers', f'rstd/n{nh}'], wt + list(extra_w))

    def rmsnorm_to(dst_bf, gain_col0, L, src=x, tagw='hT'):
        rms_generic(src, dst_bf, gain_col0, L, R1.rearrange("p (k t) -> p k t", k=8), ['x'], [tagw], ['R1'])

    def load_w(src_ap, kt, ncols, eng='pool'):
        i = next_wb()
        view = wbuf[i][:, 0:kt * ncols].rearrange("p (k c) -> p k c", k=kt)
        DMA(eng, view, src_ap.rearrange("(k p) c -> p k c", p=128), [f'wb{i}'], [f'wb{i}'])
        return view, f'wb{i}'

    def load_tab(src_ap, ncols):
        i = next_wb()
        view = wbuf[i][:, 0:ncols]
        DMA('sp', view, src_ap, [f'wb{i}', 'tabM', 'tabG1T', 'tabW'], [f'wb{i}'])
        return view, f'wb{i}'

    pp_rr = [0]

    def next_pp():
        i = pp_rr[0] % 4
        pp_rr[0] += 1
        return pp[i], ppn[i]

    ev_rr = [0]

    def evac_eng():
        ev_rr[0] += 1
        return 'act' if ev_rr[0] % 2 else 'dve'

    def prefetch_w(w_d, L, col0, ncols_total, kt, blk=512, row0=0, nblocks=1):
        out = []
        c = 0
        while c < ncols_total and len(out) < nblocks:
            nb = min(blk, ncols_total - c)
            out.append(load_w(w_d[L, row0:row0 + kt * 128, col0 + c:col0 + c + nb], kt, nb))
            c += nb
        return out

    def proj_fm(w_d, L, col0, ncols_total, kt, rhs_tile, rhs_tag, consume, blk=512, row0=0, pre=None):
        c = 0
        pre = list(pre or [])
        while c < ncols_total:
            nb = min(blk, ncols_total - c)
            if pre:
                wv, wtag = pre.pop(0)
            else:
                wv, wtag = load_w(w_d[L, row0:row0 + kt * 128, col0 + c:col0 + c + nb], kt, nb)
            for m0 in range(0, nb, 128):
                ms = min(128, nb - m0)
                pt, ptn = next_pp()
                for n in range(2):
                    for k in range(kt):
                        MM(pt[0:ms, n * 512:(n + 1) * 512], wv[:, k, m0:m0 + ms],
                           rhs_tile[:, k, n * 512:(n + 1) * 512], k == 0, k == kt - 1,
                           [wtag] + (rhs_tag if isinstance(rhs_tag, list) else [rhs_tag]), [ptn])
                consume((c + m0) // 128, pt, ptn, ms)
            c += nb

    claimed = {}
    ph_no = [0]

    def ct(parent, name, phase):
        key = (parent, phase)
        if key not in claimed:
            claimed[key] = True
            return [parent, f'{parent}/{name}']
        return [f'{parent}/{name}']

    def s5_block(L, wvs):
        ph_no[0] += 1
        Utm2 = R1.rearrange("p (g s h) -> p g s h", g=64, s=8)
        hs = hT[:].rearrange("p k (c s) -> p k s c", s=8)
        for s in range(8):
            pt, ptn = next_pp()
            for nb in range(2):
                wv, wtag = wvs[nb]
                for k in range(8):
                    MM(pt[:, nb * 512:(nb + 1) * 512], hs[:, k, s, :], wv[:, k, :], k == 0, k == 7,
                       ['hT', wtag], [ptn])
            CP(evac_eng(), Utm2[:, :, s, :], pt[:].rearrange("p (g h) -> p g h", h=16), [ptn], ct('R1', f'u{s}', ('u', L, ph_no[0])))
        S.mark('s5.tr')
        for b4 in range(4):
            pt, ptn = next_pp()
            ptb = pt[:].bitcast(BF16)
            for j in range(16):
                g = b4 * 16 + j
                TR(ptb[:, j * 128:(j + 1) * 128], R1[:, g * 128:(g + 1) * 128], identb[:], ['R1', 'identb'], [ptn])
            CP(evac_eng(), R2[:, b4 * 2048:(b4 + 1) * 2048], ptb, [ptn], ct('R2', f'ug{b4}', ('ug', L, ph_no[0])))
        Ug = R2.rearrange("p (g c) -> p g c", g=64)
        S.mark('s5.yintra')
        Ytm = R1.rearrange("p (l g h) -> p l g h", l=8, g=64)
        for q in range(4):
            tv, ttag = load_tab(tabM_d[L, :, q * 2048:(q + 1) * 2048], 2048)
            for half8 in range(2):
                pt, ptn = next_pp()
                for j in range(8):
                    gl = half8 * 8 + j
                    g = q * 16 + gl
                    MM(pt[:, j * 128:(j + 1) * 128], Ug[:, g, :], tv[:, gl * 128:(gl + 1) * 128], True, True,
                       ['R2', ttag], [ptn])
                g0 = q * 16 + half8 * 8
                CP(evac_eng(), Ytm[:, :, g0:g0 + 8, :].rearrange("p l g h -> p g l h"),
                   pt[:].rearrange("p (g l h) -> p g l h", g=8, l=8), [ptn], ct('R1', f'y{g0}', ('yi', L, ph_no[0])))
        S.mark('s5.P')
        Pq = PQ[:, 0:2 * 32 * 129].rearrange("p (r g c) -> p r g c", r=2, g=32)
        CP('dve', Pq[:, :, :, 0], qcar[:, L, :].rearrange("p (r g) -> p r g", r=2), ['qcar'], ['PQ'])
        for ri in range(2):
            for q in range(2):
                tv, ttag = load_tab(tabG1T_d[L, :, ri * 4096 + q * 2048:ri * 4096 + (q + 1) * 2048], 2048)
                for b in range(4):
                    pt, ptn = next_pp()
                    for j in range(4):
                        gpl = b * 4 + j
                        gp = q * 16 + gpl
                        for g2 in range(2):
                            g = gp * 2 + g2
                            MM(pt[64 * g2:64 * g2 + 64, j * 128:(j + 1) * 128],
                               tv[:, gpl * 128 + g2 * 64:gpl * 128 + g2 * 64 + 64], Ug[:, g, :], True, True,
                               ['R2', ttag], [ptn])
                    gp0 = q * 16 + b * 4
                    CP(evac_eng(), Pq[:, ri, gp0:gp0 + 4, 1:129],
                       pt[:, 0:512].rearrange("p (g c) -> p g c", g=4), [ptn], [f'PQ/P{ri}_{gp0}'])
        S.mark('s5.scan')
        Tc_v, Tc_t = load_tab(tabW_d[L, 0, :, :], 4096)
        Ts_v, Ts_t = load_tab(tabW_d[L, 1, :, :], 4096)
        g3 = lambda a_: a_.rearrange("p (g c) -> p g c", g=32)
        R2f = R2.bitcast(F32)
        Ra = R2f[:, 0:2048].rearrange("p (g c) -> p g c", g=16)
        Rb = R2f[:, 2048:4096].rearrange("p (g c) -> p g c", g=16)
        R5f_ = R5.bitcast(F32)
        Ra1 = R5f_[:, 0:2048].rearrange("p (g c) -> p g c", g=16)
        Rb1 = R5f_[:, 2048:4096].rearrange("p (g c) -> p g c", g=16)
        for hg in (1, 0):
            gs = slice(16 * hg, 16 * hg + 16)
            Pr = Pq[:, 0, gs, 1:129]
            Pi = Pq[:, 1, gs, 1:129]
            Tc = g3(Tc_v)[:, gs, :]
            Ts = g3(Ts_v)[:, gs, :]
            en = 'dve'
            ra, rb, rt, pt_ = (Ra1, Rb1, 'R5', 'PQ') if hg == 1 else (Ra, Rb, 'R2', 'PQ')
            TT(en, ra, Pr, Tc, ALU.mult, [pt_, Tc_t], [rt])
            TT(en, rb, Pi, Ts, ALU.mult, [pt_, Ts_t], [rt])
            TT(en, ra, ra, rb, ALU.add, [rt], [rt])
            TT(en, rb, Pi, Tc, ALU.mult, [pt_, Tc_t, rt], [rt])
            TT(en, Pr, Pr, Ts, ALU.mult, [pt_, Ts_t], [pt_])
            TT(en, rb, rb, Pr, ALU.subtract, [rt, pt_], [rt])
        for hg in range(2):
            Ra, Rb = (Ra, Rb) if hg == 0 else (Ra1, Rb1)
            for j in range(16):
                gp = 16 * hg + j
                for ri, src in ((0, Ra), (1, Rb)):
                    S.add('dve', lambda e, ri=ri, gp=gp, src=src, j=j: e.tensor_tensor_scan(
                        Pq[:, ri, gp, 1:129], RHO[:, L, gp:gp + 1].broadcast_to([128, 128]), src[:, j, :],
                        Pq[:, ri, gp, 0:1], ALU.mult, ALU.add), ['R2', 'R5', 'PQ', 'RHO'], ['PQ'])
        r128 = Pq[:, :, :, 128]
        w128 = W128[:, L, :].rearrange("p (r g) -> p r g", r=2)
        sm = smalls[:, 0:64].rearrange("p (r g) -> p r g", r=2)
        qc = qcar[:, L, :].rearrange("p (r g) -> p r g", r=2)
        RS = ['PQ', 'W128', 'smalls']
        TT('dve', sm[:, 0, :], r128[:, 0, :], w128[:, 0, :], ALU.mult, RS, ['smalls'])
        TT('dve', sm[:, 1, :], r128[:, 1, :], w128[:, 1, :], ALU.mult, RS, ['smalls'])
        TT('dve', qc[:, 0, :], sm[:, 0, :], sm[:, 1, :], ALU.subtract, RS, ['qcar'])
        TT('dve', sm[:, 0, :], r128[:, 0, :], w128[:, 1, :], ALU.mult, RS + ['qcar'], ['smalls'])
        TT('dve', sm[:, 1, :], r128[:, 1, :], w128[:, 0, :], ALU.mult, RS + ['qcar'], ['smalls'])
        TT('dve', qc[:, 1, :], sm[:, 0, :], sm[:, 1, :], ALU.add, RS, ['qcar'])
        Wc_v, Wc_t = load_tab(tabW_d[L, 2, :, :], 4096)
        Ws_v, Ws_t = load_tab(tabW_d[L, 3, :, :], 4096)
        Qb = R2.rearrange("p (r g c) -> p r g c", r=2, g=32)
        R5f = R5.bitcast(F32)
        tA = R5f[:, 0:2048].rearrange("p (g c) -> p g c", g=16)[:, :, 0:127]
        tB = R5f[:, 2048:4096].rearrange("p (g c) -> p g c", g=16)[:, :, 0:127]
        CP('act', Qb[:, :, :, 0], Pq[:, :, :, 0], ['PQ'], ['R2'])
        for hg in range(2):
            gs = slice(16 * hg, 16 * hg + 16)
            rr = Pq[:, 0, gs, 1:128]
            rim = Pq[:, 1, gs, 1:128]
            wc = g3(Wc_v)[:, gs, 1:128]
            ws = g3(Ws_v)[:, gs, 1:128]
            TT('dve', tA, rim, ws, ALU.mult, ['PQ', Ws_t, 'R5'], ['R5'])
            TT('dve', tB, rr, wc, ALU.mult, ['PQ', Wc_t, 'R5'], ['R5'])
            TT('dve', Qb[:, 0, gs, 1:128], tB, tA, ALU.subtract, ['R5'], ['R2'])
            TT('dve', tA, rr, ws, ALU.mult, ['PQ', Ws_t, 'R5'], ['R5'])
            TT('dve', tB, rim, wc, ALU.mult, ['PQ', Wc_t, 'R5'], ['R5'])
            TT('dve', Qb[:, 1, gs, 1:128], tA, tB, ALU.add, ['R5'], ['R2'])
        for q in range(4):
            tvs = []
            for g2 in range(2):
                i = next_wb()
                view = wbuf[i][:, 0:2048].rearrange("p (r g a) -> p r g a", r=2, g=8)
                DMA('sp', view, tabG2_d[L, g2, :, :].rearrange("p (r g a) -> p r g a", r=2, g=32)[:, :, q * 8:(q + 1) * 8, :],
                    [f'wb{i}', 'tabG2'], [f'wb{i}'])
                tvs.append((view, f'wb{i}'))
            for half8 in range(2):
                pt, ptn = next_pp()
                for j in range(8):
                    gl = half8 * 8 + j
                    g = q * 16 + gl
                    gp, g2 = g // 2, g % 2
                    gpl = gp - q * 8
                    tv, ttag = tvs[g2]
                    for ri in range(2):
                        MM(pt[:, j * 128:(j + 1) * 128], Qb[:, ri, gp, :], tv[:, ri, gpl, :], ri == 0, ri == 1,
                           ['R2', ttag], [ptn])
                g0 = q * 16 + half8 * 8
                yv = Ytm[:, :, g0:g0 + 8, :].rearrange("p l g h -> p g l h")
                TT('dve', yv, yv, pt[:].rearrange("p (g l h) -> p g l h", g=8, l=8), ALU.add, [ptn, f'R1/y{g0}'], [f'R1/y{g0}'])
        pre_glu = prefetch_w(w_glu_d, L, 0, 1024, 8, nblocks=1)
        ACT(R1, R1, AF.Gelu_apprx_tanh, ['R1', 'PQ'], ['R1'])
        S.mark('s5.gelu_tr')
        gT = PQ[:, 4096:8192].bitcast(BF16).rearrange("p (k t) -> p k t", k=8)
        Yf = R1.rearrange("p (l k c) -> p l k c", l=8, k=8)
        for k in range(8):
            pt, ptn = next_pp()
            ptb = pt[:].bitcast(BF16)
            for l in range(8):
                TR(ptb[:, l * 128:(l + 1) * 128], Yf[:, l, k, :], identb[:], ['R1', 'identb'], [ptn])
            CP(evac_eng(), gT[:, k, :].rearrange("p (c s) -> p s c", s=8),
               ptb[:, 0:1024].rearrange("p (s c) -> p s c", s=8), [ptn], ct('PQ', f'gT{k}', ('gT', L, ph_no[0])))
        S.mark('s5.glu')
        yaT = R2.rearrange("p (k t) -> p k t", k=8)
        sg = sgt[:]

        def glu_consume(m, pt, ptn, ms):
            ACT(sg, pt[:], AF.Sigmoid, [ptn, 'pers'], ['sgt'],
                bias=pers[:, L, C_BGLU + m:C_BGLU + m + 1])
            TT('dve', yaT[:, m, :], gT[:, m, :], sg, ALU.mult, ['PQ', 'sgt'], ['R2'])
        proj_fm(w_glu_d, L, 0, 1024, 8, gT, 'PQ', glu_consume, pre=pre_glu)
        held['xbc'] = prefetch_w(w_in_d, L, 2048, 1536, 8, nblocks=1)

    def norm_inplace(yT, tag, gain_col0, L, sq=None, sq_tag='PQ'):
        if sq is None:
            sq = PQ[:, 4096:8192].bitcast(BF16).rearrange("p (k t) -> p k t", k=8)
        rms_generic(yT, yT, gain_col0, L, sq, [tag], [tag], [sq_tag])

    def ssd_block(L):
        for i in range(48):
            TS('pool', cdiag[:, i, :], identb[:], pers[:, L, C_CW + i:C_CW + i + 1], None, ALU.mult, None,
               ['identb', 'pers'], ['cdiag'])
        xraw = PQ[:, 0:6168].bitcast(BF16)[:, 0:12 * 1027].rearrange("p (m t) -> p m t", m=12)
        CP('dve', xraw[:, :, 0:3], ctail[:, L, :, :], ['ctail', 'PQ'], ['PQ'])

        def xbc_consume(m, pt, ptn, ms):
            CP(evac_eng(), xraw[:, m, 3:3 + NT], pt[:], [ptn], [f'PQ/xr{m}'])
        proj_fm(w_in_d, L, 2048, 1536, 8, hT, 'hT', xbc_consume, pre=held.pop('xbc', None))
        CP('dve', ctail[:, L, :, :], xraw[:, :, NT:NT + 3], ['PQ'], ['ctail'])
        norm_inplace(R2.rearrange("p (k t) -> p k t", k=8), 'R2', C_GS5, L,
                     sq=R1.rearrange("p (k t) -> p k t", k=8), sq_tag='R1')
        S.mark('ssd.conv')
        xsT = R1.rearrange("p (k t) -> p k t", k=8)
        for m in range(12):
            pt, ptn = next_pp()
            for n in range(2):
                for tap in range(4):
                    MM(pt[:, n * 512:(n + 1) * 512], cdiag[:, m * 4 + tap, :],
                       xraw[:, m, tap + n * 512:tap + n * 512 + 512], tap == 0, tap == 3, [f'PQ/xr{m}', 'cdiag'], [ptn])
            dst = xsT[:, m, :] if m < 8 else BCT[:, m - 8, :]
            ACT(dst, pt[:], AF.Silu, [ptn, 'pers'], ['R1' if m < 8 else 'BCT'],
                bias=pers[:, L, C_CB + m:C_CB + m + 1])
        S.mark('ssd.dt')
        wv, wtag = load_w(w_in_d[L, :, 3584:3600], 8, 16)
        pt, ptn = next_pp()
        for n in range(2):
            for k in range(8):
                MM(pt[0:16, n * 512:(n + 1) * 512], wv[:, k, :], hT[:, k, n * 512:(n + 1) * 512], k == 0, k == 7,
                   [wtag, 'hT'], [ptn])
        o = [0]

        def tq(n, name):
            a_ = PQ[:, o[0]:o[0] + n]
            o[0] += n
            return a_, 'PQ/' + name
        e1 = rs
        dtp = rstd
        dta = rs
        nA, nA_t = tq(1, 'nA')
        S.add('dve', lambda e: e.memset(nA[:, 0:1], 0.0), [], ['PQ'])
        ACT(e1[0:16, :], pt[0:16, :], AF.Exp, [ptn, 'pers'], ['rs'], bias=pers[0:16, L, C_DTB:C_DTB + 1])
        ACT(dtp[0:16, :], e1[0:16, :], AF.Ln, ['rs'], ['rstd'], bias=1.0)
        ACT(nA[0:16, :], pers[0:16, L, C_ALOG:C_ALOG + 1], AF.Exp, ['pers'], [nA_t])
        TS('dve', dta[0:16, :], dtp[0:16, :], nA[0:16, 0:1], -1.0, ALU.mult, ALU.mult, ['rstd', nA_t], ['rs'])
        dtp_tm, dtp_tm_t = tq(128, 'dtp_tm')
        dta_tm, dta_tm_t = tq(128, 'dta_tm')
        pt2, ptn2 = next_pp()
        for c in range(8):
            TR(pt2[:, c * 16:(c + 1) * 16], dtp[0:16, c * 128:(c + 1) * 128], identf[0:16, 0:16], ['rstd', 'cst'], [ptn2])
            TR(pt2[:, 128 + c * 16:128 + (c + 1) * 16], dta[0:16, c * 128:(c + 1) * 128], identf[0:16, 0:16],
               ['rs', 'cst'], [ptn2])
        CP('dve', dtp_tm, pt2[:, 0:128], [ptn2], [dtp_tm_t])
        CP('dve', dta_tm, pt2[:, 128:256], [ptn2], [dta_tm_t])
        acum_tm, acum_t = tq(128, 'acum')
        nacum, nacum_t = tq(128, 'nacum')
        dec_tm, dec_t = tq(128, 'dec')
        dtd_tm, dtd_t = tq(128, 'dtd')
        alast, alast_t = tq(128, 'alast')
        edec, edec_t = tq(128, 'edec')
        pt3, ptn3 = next_pp()
        MM(pt3[:, 0:128], trif, dta_tm, True, True, [dta_tm_t, 'cst'], [ptn3])
        MM(pt3[:, 128:256], cst[:, K_ONE:K_ONE + 128], dta_tm, True, True, [dta_tm_t, 'cst'], [ptn3])
        CP('dve', acum_tm, pt3[:, 0:128], [ptn3], [acum_t])
        TS('dve', nacum, acum_tm, -1.0, None, ALU.mult, None, [acum_t], [nacum_t])
        CP('dve', alast, pt3[:, 128:256], [ptn3], [alast_t])
        TT('dve', dec_tm, alast, acum_tm, ALU.subtract, [alast_t, acum_t], [dec_t])
        ACT(dec_tm, dec_tm, AF.Exp, [dec_t], [dec_t])
        TT('dve', dtd_tm, dec_tm, dtp_tm, ALU.mult, [dec_t, dtp_tm_t], [dtd_t])
        ACT(edec, alast, AF.Exp, [alast_t], [edec_t])
        dtri_l, dtri_tl = [], []
        for g in range(2):
            a_, t_ = tq(1024, f'dtri{g}')
            dtri_l.append(a_); dtri_tl.append(t_)
        bfb = PQ[:, o[0]:8448].bitcast(BF16)
        ob = [0]

        def tb(n, name):
            a_ = bfb[:, ob[0]:ob[0] + n]
            ob[0] += n
            return a_, 'PQ/' + name
        xdt, xdt_t = tb(1024, 'xdt')
        xdtd, xdtd_t = tb(1024, 'xdtd')
        Btm, Btm_t = tb(256, 'Btm')
        stb, stb_t = tb(1024, 'stb')
        LTl = [tb(1024, f'LT{g}') for g in range(2)]
        EEl = [tb(1024, f'EE{g}') for g in range(2)]
        CBl = [tb(128, f'CBm{g}') for g in range(2)]
        pre_z = prefetch_w(w_in_d, L, 1024, 1024, 8, nblocks=2)
        S.mark('ssd.chunks')
        ybT = R5.rearrange("p (k t) -> p k t", k=8)
        ones_f = cst[:, K_ONE:K_ONE + 128]
        v8 = lambda a_: a_.rearrange("p (h l) -> p h l", h=8)
        for c in range(8):
            cs_ = slice(c * 128, (c + 1) * 128)
            ptx, ptxn = next_pp()
            ptxb = ptx[:].bitcast(BF16)
            for k in range(8):
                TR(ptxb[:, k * 128:(k + 1) * 128], xsT[:, k, cs_], identb[:], ['R1', 'identb'], [ptxn])
            for g in range(2):
                TR(ptxb[:, 1024 + g * 128:1024 + (g + 1) * 128], BCT[:, g, cs_], identb[:], ['BCT', 'identb'], [ptxn])
            bch = lambda a_: a_.unsqueeze(2).broadcast_to([128, 16, 64])
            for g in range(2):
                for h8 in range(8):
                    hcol = c * 16 + g * 8 + h8
                    S.add('act', lambda e, g=g, h8=h8, hcol=hcol: e.activation(
                        dtri_l[g][:, h8 * 128:(h8 + 1) * 128], trif, AF.Copy, scale=dta_tm[:, hcol:hcol + 1]),
                        [dta_tm_t, 'cst'], [dtri_tl[g]])
            TT('dve', xdt.rearrange("p (h q) -> p h q", h=16), ptxb[:, 0:1024].rearrange("p (h q) -> p h q", h=16),
               bch(dtp_tm[:, c * 16:(c + 1) * 16]), ALU.mult, [ptxn, dtp_tm_t], [xdt_t])
            TT('dve', xdtd.rearrange("p (h q) -> p h q", h=16), xdt.rearrange("p (h q) -> p h q", h=16),
               bch(dec_tm[:, c * 16:(c + 1) * 16]), ALU.mult, [xdt_t, dec_t], [xdtd_t])
            CP('act', Btm, ptxb[:, 1024:1280], [ptxn], [Btm_t])
            CP('act', stb, sstate[:, L, :], ['sstate'], [stb_t])
            pls = []
            pc, pcn = next_pp()
            for g in range(2):
                pl, pln = next_pp()
                pls.append((pl, pln))
                for hb in range(2):
                    MM(pl[:, hb * 512:(hb + 1) * 512], ones_f, dtri_l[g][:, hb * 512:(hb + 1) * 512], True, True,
                       [dtri_tl[g], 'cst'], [pln])
                MM(pc[:, g * 128:(g + 1) * 128], BCT[:, g, cs_], BCT[:, 2 + g, cs_], True, True, ['BCT'], [pcn])
            for g in range(2):
                ACT(EEl[g][0], pls[g][0][:], AF.Exp, [pls[g][1]], [EEl[g][1]])
            for g in range(2):
                pl, pln = pls[g]
                TT('dve', v8(pl[:]), v8(pl[:]),
                   acum_tm[:, c * 16 + g * 8:c * 16 + g * 8 + 8].unsqueeze(2).broadcast_to([128, 8, 128]), ALU.min,
                   [pln, acum_t], [pln])
                TT('dve', v8(pl[:]), v8(pl[:]),
                   nacum[:, c * 16 + g * 8:c * 16 + g * 8 + 8].unsqueeze(2).broadcast_to([128, 8, 128]), ALU.add,
                   [pln, nacum_t], [pln])
                TT('dve', CBl[g][0], pc[:, g * 128:(g + 1) * 128], trif, ALU.mult, [pcn, 'cst'], [CBl[g][1]])
            for g in range(2):
                ACT(LTl[g][0], pls[g][0][:], AF.Exp, [pls[g][1]], [LTl[g][1]])
            for g in range(2):
                TT('dve', v8(EEl[g][0]), v8(EEl[g][0]), BCT[:, 2 + g, cs_].unsqueeze(1).broadcast_to([128, 8, 128]), ALU.mult,
                   [EEl[g][1], 'BCT'], [EEl[g][1]])
                TT('dve', v8(LTl[g][0]), v8(LTl[g][0]), CBl[g][0].unsqueeze(1).broadcast_to([128, 8, 128]), ALU.mult,
                   [LTl[g][1], CBl[g][1]], [LTl[g][1]])
            for g in range(2):
                py, pyn = next_pp()
                MT, MT_t = LTl[g]
                CE, CE_t = EEl[g]
                for kk in range(4):
                    k = g * 4 + kk
                    for hh in range(2):
                        h = 2 * k + hh
                        h8 = h % 8
                        o_ = py[64 * hh:64 * hh + 64, kk * 128:(kk + 1) * 128]
                        MM(o_, xdt[:, h * 64:(h + 1) * 64], MT[:, h8 * 128:(h8 + 1) * 128], True, False, [xdt_t, MT_t], [pyn])
                        MM(o_, stb[:, h * 64:(h + 1) * 64], CE[:, h8 * 128:(h8 + 1) * 128], False, True, [stb_t, CE_t], [pyn])
                for kk in range(4):
                    k = g * 4 + kk
                    STT(ybT[:, k, cs_], xsT[:, k, cs_], pers[:, L, C_SD + k:C_SD + k + 1], py[:, kk * 128:(kk + 1) * 128],
                        ALU.mult, ALU.add, ['R1', 'pers', pyn], ['R5'])
            ps_, psn = next_pp()
            for g in range(2):
                MM(ps_[:, g * 512:(g + 1) * 512], Btm[:, g * 128:(g + 1) * 128], xdtd[:, g * 512:(g + 1) * 512], True, True,
                   [Btm_t, xdtd_t], [psn])
            sv = sstate[:, L, :].rearrange("p (h q) -> p h q", h=16)
            TT('pool', sv, sv, edec[:, c * 16:(c + 1) * 16].unsqueeze(2).broadcast_to([128, 16, 64]),
               ALU.mult, ['sstate', edec_t], ['sstate'])
            TT('dve', sstate[:, L, :], sstate[:, L, :], ps_[:, :], ALU.add, ['sstate', psn], ['sstate'])
        zs = sgt[:]

        def z_consume(m, pt, ptn, ms):
            ACT(zs, pt[:], AF.Silu, [ptn], ['sgt'])
            TT('dve', ybT[:, m, :], ybT[:, m, :], zs, ALU.mult, ['sgt', 'R5'], ['R5'])
        proj_fm(w_in_d, L, 1024, 1024, 8, hT, 'hT', z_consume, pre=pre_z)

    negones = sb("negones", [128, 128])
    S.add('dve', lambda e: e.memset(negones[:], -1.0), [], ['negones'])

    def out_proj(L):
        def consume(m, pt, ptn, ms):
            TT('dve', x[:, m, :], x[:, m, :], pt[:], ALU.add, [ptn, 'x'], ['x'])
        ybT_ = R5.rearrange("p (k t) -> p k t", k=8)
        yaT_ = R2.rearrange("p (k t) -> p k t", k=8)
        norm_inplace(ybT_, 'R5', C_GSSD, L)
        proj_fm(w_out_d, L, 0, 1024, 8, yaT_, 'R2', consume, blk=512, row0=0)
        proj_fm(w_out_d, L, 0, 1024, 8, ybT_, 'R5', consume, blk=512, row0=1024)

    def proj_fm_dep2(*a, **k):
        return proj_fm(*a, **k)

    def ffn(L):
        pre_g = [load_w(w_gate_d[L, :, 0:128], 8, 128), load_w(w_up_d[L, :, 0:128], 8, 128)]
        rmsnorm_to(hT, C_GFFN, L)
        aT = BIG[:, 0:22 * NT].rearrange("p (k t) -> p k t", k=22)
        sg = sgt[:]
        for m in range(22):
            if m == 0:
                (wg, wgt), (wu, wut) = pre_g
            else:
                wg, wgt = load_w(w_gate_d[L, :, m * 128:(m + 1) * 128], 8, 128)
                wu, wut = load_w(w_up_d[L, :, m * 128:(m + 1) * 128], 8, 128)
            pg, pgn = next_pp()
            pu, pun = next_pp()
            for n in range(2):
                for k in range(8):
                    MM(pg[:, n * 512:(n + 1) * 512], wg[:, k, :], hT[:, k, n * 512:(n + 1) * 512], k == 0, k == 7,
                       [wgt, 'hT'], [pgn])
            for n in range(2):
                for k in range(8):
                    MM(pu[:, n * 512:(n + 1) * 512], wu[:, k, :], hT[:, k, n * 512:(n + 1) * 512], k == 0, k == 7,
                       [wut, 'hT'], [pun])
            ACT(sg, pg[:], AF.Silu, [pgn], ['sgt'])
            TT('dve', aT[:, m, :], sg, pu[:], ALU.mult, ['sgt', pun], ['R1', 'R2', 'R5'])

        def consume(m, pt, ptn, ms):
            TT('dve', x[:, m, :], x[:, m, :], pt[:], ALU.add, [ptn, 'x'], ['x'])
        proj_fm(w_down_d, L, 0, 1024, 22, aT, ['R1', 'R2', 'R5'], consume, blk=128)

    for hf in range(NH):
        if not (hf == 0 and held.pop('x0', False)):
            for k in range(8):
                DMA('sp', x[:, k, :], xT_d[k * 128:(k + 1) * 128, hf * NT:(hf + 1) * NT], ['x'], ['x'])
        for L in range(DEPTH):
            if STAGE < 2:
                break
            S.mark(f'h{hf}L{L}.norm')
            wvs_u = [load_w(w_in_d[L, :, nb * 512:(nb + 1) * 512], 8, 512) for nb in range(2)]
            rmsnorm_to(hT, C_GMIX, L)
            if STAGE == 2:
                for k in range(8):
                    dump(hT[:, k, :], k * 1024, 1024, ['hT'])
                break
            S.mark(f'h{hf}L{L}.s5')
            s5_block(L, wvs_u)
            if STAGE == 3:
                for k in range(8):
                    dump(R2[:, k * 1024:(k + 1) * 1024], k * 1024, 1024, ['R2'])
                break
            S.mark(f'h{hf}L{L}.ssd')
            ssd_block(L)
            if STAGE == 4:
                for k in range(8):
                    dump(R5[:, k * 1024:(k + 1) * 1024], k * 1024, 1024, ['R5'])
                break
            S.mark(f'h{hf}L{L}.outproj')
            out_proj(L)
            if STAGE == 5:
                break
            S.mark(f'h{hf}L{L}.ffn')
            ffn(L)
            if STAGE == 6:
                break
        if STAGE < 99:
            for k in range(8):
                final.append(DMA('sp', out_d[k * 128:(k + 1) * 128, hf * NT:(hf + 1) * NT], x[:, k, :], ['x'], ['out']))
            break
        S.mark(f'h{hf}.final')
        if hf < NH - 1:
            ostage = BIG[:, 8192:24576].bitcast(F32).rearrange("p (k t) -> p k t", k=8)
            rms_generic(x, ostage, C_GFIN, 0, R1.rearrange("p (k t) -> p k t", k=8), ['x'], ['R2'], ['R1'], extra_w=['R5'])
            for k in range(8):
                final.append(DMA('sp', out_d[k * 128:(k + 1) * 128, hf * NT:(hf + 1) * NT], ostage[:, k, :],
                                 [f'R2/o{k}n0', f'R2/o{k}n1', 'R5'], ['out']))
        else:
            rms_generic(x, x, C_GFIN, 0, R1.rearrange("p (k t) -> p k t", k=8), ['x'], ['x'], ['R1'])
            for k in range(8):
                final.append(DMA('sp', out_d[k * 128:(k + 1) * 128, hf * NT:(hf + 1) * NT], x[:, k, :],
                                 [f'x/o{k}n0', f'x/o{k}n1'], ['out']))
    S.emit(final)


def _s5_lay(a):
    return a.reshape(32, 2, 64).transpose(1, 2, 0).reshape(128, 32)


def _consts():
    c = np.zeros((128, NCST), np.float32)
    i = np.arange(128)
    c[:, K_ID:K_ID + 128] = np.eye(128)
    c[:, K_TRI:K_TRI + 128] = (i[:, None] <= i[None, :])
    c[:, K_NEG:K_NEG + 128] = np.where(i[None, :] < i[:, None], -30000.0, 0.0)
    c[:, K_BLK:K_BLK + 128] = ((i[None, :] // 16) >= (i[:, None] // 16))
    c[:, K_ONE:K_ONE + 128] = 1.0
    return c


def _small_params(inp):
    sp = np.zeros((DEPTH, 128, NSP), np.float32)
    fm = lambda v: np.asarray(v, np.float32).reshape(8, 128).T
    for L in range(DEPTH):
        sp[L, :, C_LRE:C_LRE + 32] = _s5_lay(inp['s5_lam_re'][L])
        sp[L, :, C_LIM:C_LIM + 32] = _s5_lay(inp['s5_lam_im'][L])
        sp[L, :, C_LST:C_LST + 32] = _s5_lay(np.broadcast_to(inp['s5_log_step'][L][:, None], (64, 64)))
        for nm, c0 in (('s5_b_re', C_BRE), ('s5_b_im', C_BIM)):
            sp[L, :, c0:c0 + 512] = inp[nm][L].reshape(32, 2, 64, 16).transpose(1, 2, 0, 3).reshape(128, 512)
        for nm, c0 in (('s5_c_re', C_CRE), ('s5_c_im', C_CIM)):
            sp[L, :, c0:c0 + 512] = inp[nm][L].reshape(32, 2, 16, 64).transpose(1, 3, 0, 2).reshape(128, 512)
        P = C_P0
        sp[L, :, P + C_DCOL:P + C_DCOL + 64] = np.tile(inp['s5_d'][L].reshape(64, 16).T, (8, 1))
        sp[L, :, P + C_GMIX:P + C_GMIX + 8] = fm(inp['norm_mix'][L])
        sp[L, :, P + C_GS5:P + C_GS5 + 8] = fm(inp['s5_norm'][L])
        sp[L, :, P + C_GSSD:P + C_GSSD + 8] = fm(inp['ssd_norm'][L])
        sp[L, :, P + C_GFFN:P + C_GFFN + 8] = fm(inp['norm_ffn'][L])
        sp[L, :, P + C_GFIN:P + C_GFIN + 8] = fm(inp['norm_final'])
        sp[L, :, P + C_BGLU:P + C_BGLU + 8] = fm(inp['s5_b_glu'][L])
        sp[L, :, P + C_CW:P + C_CW + 48] = inp['ssd_conv_w'][L].T.reshape(12, 128, 4).transpose(1, 0, 2).reshape(128, 48)
        sp[L, :, P + C_CB:P + C_CB + 12] = inp['ssd_conv_b'][L].reshape(12, 128).T
        sp[L, 0:16, P + C_DTB] = inp['ssd_dt_bias'][L]
        sp[L, 0:16, P + C_ALOG] = inp['ssd_a_log'][L]
        sp[L, :, P + C_SD:P + C_SD + 8] = np.repeat(inp['ssd_d'][L], 64).reshape(8, 128).T
    return sp


def make_in_maps(inp, cores):
    inp = {k: np.asarray(v) for k, v in inp.items()}
    sp = _small_params(inp)
    cst = _consts()
    shared = dict(w_in=np.ascontiguousarray(inp['w_in'], np.float32), w_glu=np.ascontiguousarray(inp['s5_w_glu'], np.float32),
                  w_out=np.ascontiguousarray(inp['w_out'], np.float32), w_gate=np.ascontiguousarray(inp['w_gate'], np.float32),
                  w_up=np.ascontiguousarray(inp['w_up'], np.float32), w_down=np.ascontiguousarray(inp['w_down'], np.float32),
                  sp=sp, cst=cst)
    maps = []
    for b in cores:
        m = dict(shared)
        m['xT'] = np.ascontiguousarray(inp['x'][b].T, np.float32)
        maps.append(m)
    return maps


_NC = [None]


def kernel(**inputs):
    if _NC[0] is None:
        _NC[0] = build_nc()
    nc = _NC[0]
    maps = make_in_maps(inputs, list(range(8)))
    res = run_bass_kernel_spmd(nc, maps, core_ids=list(range(8)))
    out = np.stack([np.asarray(r["outT"]).T for r in res.results], 0)
    return np.ascontiguousarray(out, np.float32)
```

```python
import math
import os
from contextlib import ExitStack
import numpy as np
import ml_dtypes
import concourse.bass as bass
import concourse.mybir as mybir
from concourse.bass_utils import run_bass_kernel_spmd

F32 = mybir.dt.float32
BF16 = mybir.dt.bfloat16
AF = mybir.ActivationFunctionType
ALU = mybir.AluOpType

NT = 1024
NH = 2
DEPTH = 2
EPS = 1e-6
C_LRE, C_LIM, C_LST, C_BRE, C_BIM, C_CRE, C_CIM = 0, 32, 64, 96, 608, 1120, 1632
C_P0 = 2144
C_DCOL, C_GMIX, C_GS5, C_GSSD, C_GFFN, C_GFIN, C_BGLU, C_CW, C_CB, C_DTB, C_ALOG, C_SD = (
    0, 64, 72, 80, 88, 96, 104, 112, 160, 172, 173, 174)
NPERS = 184
NSP = C_P0 + NPERS
K_ID, K_TRI, K_NEG, K_BLK, K_ONE = 0, 128, 256, 384, 512
NCST = 640


class Sched:
    def __init__(self, nc, es):
        self.nc = nc
        self.ops = {e: [] for e in ('pe', 'act', 'dve', 'pool', 'sp')}
        self.cnt = {e: 0 for e in self.ops}
        self.sems = {e: [es.enter_context(nc.semaphore(f"s_{e}{i}")) for i in range(2)] for e in self.ops}
        self.dsems = {q: [es.enter_context(nc.semaphore(f"s_dma_{q}{i}")) for i in range(8)] for q in ('sp', 'pool')}
        self.dval = {q: [0] * 8 for q in ('sp', 'pool')}
        self.dnext = {q: 0 for q in ('sp', 'pool')}
        self.waited = {}
        self.lastw = {}
        self.readers = {}
        self.children = {}
        self.marks = []

    def _deps(self, b, is_write):
        out = []
        if '/' in b:
            par = b.split('/')[0]
            self.children.setdefault(par, set()).add(b)
            tags = [b, par]
        else:
            tags = [b] + list(self.children.get(b, ()))
        for t in tags:
            if t in self.lastw:
                out.append(self.lastw[t])
            if is_write:
                out.extend(self.readers.get(t, ()))
        return out

    def add(self, eng, fn, r=(), w=(), dma=False):
        deps = []
        for b in r:
            deps.extend(self._deps(b, False))
        for b in w:
            deps.extend(self._deps(b, True))
        if dma:
            i = self.dnext[eng]
            self.dnext[eng] = (i + 1) % 8
            if self.dval[eng][i] > 0:
                deps.append(('dma', eng, i, self.dval[eng][i]))
            self.dval[eng][i] += 16
            tok = ('dma', eng, i, self.dval[eng][i])
        else:
            k = self.cnt[eng]
            self.cnt[eng] += 1
            tok = ('op', eng, k)
        need = []
        for d in deps:
            if d[0] == 'op' and d[1] == eng and not dma and eng == 'pe':
                continue
            need.append(d)
        self.ops[eng].append([need, fn, tok])
        for b in w:
            self.lastw[b] = tok
            self.readers[b] = []
        for b in r:
            self.readers.setdefault(b, []).append(tok)
        return tok

    def mark(self, name):
        self.marks.append((name, dict(self.cnt)))

    def emit(self, final_tokens):
        nc = self.nc
        if os.environ.get('KMARKS'):
            import json
            json.dump(self.marks, open(os.environ['KMARKS'], 'w'))
        targets = {e: set() for e in self.ops}
        for e, lst in self.ops.items():
            for need, fn, tok in lst:
                best = {}
                for d in need:
                    if d[0] == 'op':
                        best[d[1]] = max(best.get(d[1], -1), d[2])
                for pe_, k in best.items():
                    targets[pe_].add(k)
        for d in final_tokens:
            if d[0] == 'op':
                targets[d[1]].add(d[2])
        for e in self.ops:
            if e != 'pe':
                targets[e] = set(range(self.cnt[e]))
        rank = {}
        for e, ks in targets.items():
            for r_, k in enumerate(sorted(ks)):
                rank[(e, k)] = r_

        def semval(d):
            if d[0] == 'dma':
                return self.dsems[d[1]][d[2]], d[3]
            r_ = rank[(d[1], d[2])]
            return self.sems[d[1]][r_ % 2], r_ // 2 + 1
        plans = {}
        for e, lst in self.ops.items():
            waited = {}
            plan = []
            for need, fn, tok in lst:
                best = {}
                for d in need:
                    if d[0] == 'op':
                        key = ('op', d[1])
                        if key not in best or best[key][2] < d[2]:
                            best[key] = d
                    else:
                        key = ('dma', d[1], d[2])
                        if key not in best or best[key][3] < d[3]:
                            best[key] = d
                waits = []
                for d in best.values():
                    sem, val = semval(d)
                    if waited.get(id(sem), 0) >= val:
                        continue
                    waited[id(sem)] = val
                    waits.append((sem, val))
                if tok[0] == 'dma':
                    inc = (self.dsems[tok[1]][tok[2]], 16)
                elif (tok[1], tok[2]) in rank:
                    r_ = rank[(tok[1], tok[2])]
                    inc = (self.sems[tok[1]][r_ % 2], 1)
                else:
                    inc = None
                plan.append((waits, fn, inc))
            plans[e] = plan
        fin = [semval(d) for d in final_tokens]
        with nc.Block() as block:
            def runner(name):
                def body(e):
                    for waits, fn, inc in plans[name]:
                        for (s_, v) in waits:
                            e.wait_ge(s_, v)
                        ins = fn(e)
                        if inc is not None:
                            ins.then_inc(inc[0], inc[1])
                    if name == 'pool':
                        for (s_, v) in fin:
                            e.wait_ge(s_, v)
                return body
            block.tensor(runner('pe'))
            block.scalar(runner('act'))
            block.vector(runner('dve'))
            block.gpsimd(runner('pool'))
            block.sync(runner('sp'))


import os
STAGE = int(os.environ.get('KSTAGE', '99'))
SUB = int(os.environ.get('KSUB', '99'))


def build_nc(dbg_spec=None):
    if dbg_spec is None and STAGE < 99:
        dbg_spec = ('dbg', [128, 32768])
    nc = bass.Bass("TRN2", target_bir_lowering=False)
    es = ExitStack()
    with es:
        _build(nc, es, dbg_spec)
    return nc


def _build(nc, es, dbg_spec):
    def dram(name, shape, dt=F32, kind="ExternalInput"):
        return nc.dram_tensor(name, shape, dt, kind=kind).ap()

    xT_d = dram("xT", [1024, NH * NT])
    out_d = dram("outT", [1024, NH * NT], kind="ExternalOutput")
    w_in_d = dram("w_in", [DEPTH, 1024, 3600])
    w_glu_d = dram("w_glu", [DEPTH, 1024, 1024])
    w_out_d = dram("w_out", [DEPTH, 2048, 1024])
    w_gate_d = dram("w_gate", [DEPTH, 1024, 2816])
    w_up_d = dram("w_up", [DEPTH, 1024, 2816])
    w_down_d = dram("w_down", [DEPTH, 2816, 1024])
    sp_d = dram("sp", [DEPTH, 128, NSP])
    cst_d = dram("cst", [128, NCST])
    tabM_d = dram("tabM", [DEPTH, 128, 8192], BF16, kind="Internal")
    tabG1T_d = dram("tabG1T", [DEPTH, 128, 8192], BF16, kind="Internal")
    tabG2_d = dram("tabG2", [DEPTH, 2, 128, 8192], BF16, kind="Internal")
    tabW_d = dram("tabW", [DEPTH, 4, 128, 4096], BF16, kind="Internal")
    dbg_d = None
    if dbg_spec is not None:
        dbg_d = dram("dbg", list(dbg_spec[1]), F32, kind="ExternalOutput")

    def sb(name, shape, dt=F32):
        return es.enter_context(nc.sbuf_tensor(name, shape, dt))

    S = Sched(nc, es)
    x = sb("x", [128, 8, NT])
    hT = sb("hT", [128, 8, NT], BF16)
    BIG = sb("BIG", [128, 24576], BF16)
    PQ = sb("PQ", [128, 8448])
    BCT = sb("BCT", [128, 4, NT], BF16)
    rs = sb("rs", [128, NT])
    rstd = sb("rstd", [128, NT])
    NWB = 4
    wbuf = [sb(f"wbuf{i}", [128, 4096], BF16) for i in range(NWB)]
    cst = sb("cst_sb", [128, NCST])
    identb = sb("identb", [128, 128], BF16)
    onesb = sb("onesb", [128, 128], BF16)
    pers = sb("pers", [128, DEPTH, NPERS])
    T1 = sb("T1", [128, DEPTH, 64])
    T2 = sb("T2", [128, DEPTH, 64])
    qcar = sb("qcar", [128, DEPTH, 64])
    RHO = sb("RHO", [128, DEPTH, 32])
    W128 = sb("W128", [128, DEPTH, 64])
    sstate = sb("sstate", [128, DEPTH, 1024])
    ctail = sb("ctail", [128, DEPTH, 12, 3], BF16)
    cdiag = sb("cdiag", [128, 48, 128], BF16)
    smalls = sb("smalls", [128, 64])
    sgt = sb("sgt", [128, NT], BF16)
    pp = [es.enter_context(nc.psum_tensor(f"pp{i}", [128, 1024], F32)) for i in range(4)]
    ppn = [f"pp{i}" for i in range(4)]

    R1 = BIG[:, 0:8192]
    R2 = BIG[:, 8192:16384]
    R5 = BIG[:, 16384:24576]
    identf = cst[:, K_ID:K_ID + 128]
    trif = cst[:, K_TRI:K_TRI + 128]
    negf = cst[:, K_NEG:K_NEG + 128]
    blkf = cst[:, K_BLK:K_BLK + 128]

    def TT(eng, out, a, b, op, r, w):
        S.add(eng, lambda e: e.tensor_tensor(out, a, b, op), r, w)

    def TS(eng, out, a, s1, s2, op0, op1, r, w):
        if op1 is None:
            S.add(eng, lambda e: e.tensor_scalar(out, a, s1, None, op0), r, w)
        else:
            S.add(eng, lambda e: e.tensor_scalar(out, a, s1, s2, op0, op1), r, w)

    def STT(out, a, sc, b, op0, op1, r, w):
        S.add('dve', lambda e: e.scalar_tensor_tensor(out, a, sc, b, op0, op1), r, w)

    def ACT(out, a, func, r, w, bias=None, scale=None):
        kw = {}
        if bias is not None:
            kw['bias'] = bias
        if scale is not None:
            kw['scale'] = scale
        S.add('act', lambda e: e.activation(out, a, func, **kw), r, w)

    def CP(eng, out, a, r, w):
        if eng == 'act':
            S.add('act', lambda e: e.copy(out, a), r, w)
        else:
            S.add(eng, lambda e: e.tensor_copy(out, a), r, w)

    def MM(out, lhsT, rhs, start, stop, r, w):
        S.add('pe', lambda e: e.matmul(out, lhsT, rhs, start=start, stop=stop), r, w)

    def TR(out, in_, ident, r, w):
        S.add('pe', lambda e: e.transpose(out, in_, ident), r, w)

    def DMA(eng, out, in_, r, w):
        return S.add(eng, lambda e: e.dma_start(out=out, in_=in_), r, w, dma=True)

    wb_i = [0]

    def next_wb():
        i = wb_i[0] % NWB
        wb_i[0] += 1
        return i

    DMA('sp', cst[:], cst_d[:, :], [], ['cst'])
    CP('dve', identb[:], identf, ['cst'], ['identb'])
    CP('dve', onesb[:], cst[:, K_ONE:K_ONE + 128], ['cst'], ['onesb'])
    for L in range(DEPTH):
        DMA('sp', pers[:, L, :], sp_d[L, :, C_P0:C_P0 + NPERS], [], ['pers'])
    S.add('dve', lambda e: e.memset(qcar[:], 0.0), [], ['qcar'])
    S.add('dve', lambda e: e.memset(sstate[:], 0.0), [], ['sstate'])
    S.add('dve', lambda e: e.memset(ctail[:], 0.0), [], ['ctail'])

    held = {}

    def gen_tables(L):
        prm = PQ[:, 0:C_P0]
        DMA('sp', prm, sp_d[L, :, 0:C_P0], ['PQ'], ['PQ'])
        base = [C_P0]

        def tmp(n):
            a = PQ[:, base[0]:base[0] + n]
            base[0] += n
            return a
        R, W = ['PQ', 'cst'], ['PQ']
        lr = prm[:, C_LRE:C_LRE + 32]
        li = prm[:, C_LIM:C_LIM + 32]
        step = tmp(32); th = tmp(32); aa = tmp(32)
        ACT(step, prm[:, C_LST:C_LST + 32], AF.Exp, R, W)
        TT('dve', th, li, step, ALU.mult, R, W)
        TT('dve', aa, lr, step, ALU.mult, R, W)
        mag = tmp(32); sn = tmp(32); cs = tmp(32)
        ACT(mag, aa, AF.Exp, R, W, scale=1.0 / 16)
        ACT(sn, th, AF.Sin, R, W, scale=1.0 / 16)
        hp = tmp(1)
        S.add('dve', lambda e: e.memset(hp, math.pi / 2), R, W)
        ACT(cs, th, AF.Sin, R, W, scale=1.0 / 16, bias=hp)
        zr = tmp(32); zi = tmp(32); t0 = tmp(32); t1 = tmp(32)
        TT('dve', zr, mag, cs, ALU.mult, R, W)
        TT('dve', zi, mag, sn, ALU.mult, R, W)
        for _ in range(4):
            TT('dve', t0, zr, zr, ALU.mult, R, W)
            TT('dve', t1, zi, zi, ALU.mult, R, W)
            TT('dve', zi, zr, zi, ALU.mult, R, W)
            TS('dve', zi, zi, 2.0, None, ALU.mult, None, R, W)
            TT('dve', zr, t0, t1, ALU.subtract, R, W)
        ar, ai = zr, zi
        ir = tmp(32); ii = tmp(32)
        TT('dve', t0, ar, ar, ALU.mult, R, W)
        TT('dve', t1, ai, ai, ALU.mult, R, W)
        TT('dve', t0, t0, t1, ALU.add, R, W)
        S.add('dve', lambda e: e.reciprocal(t0, t0), R, W)
        TT('dve', ir, ar, t0, ALU.mult, R, W)
        TT('dve', ii, ai, t0, ALU.mult, R, W)
        TS('dve', ii, ii, -1.0, None, ALU.mult, None, R, W)
        PWr = tmp(9 * 32); PWi = tmp(9 * 32); NPr = tmp(8 * 32); NPi = tmp(8 * 32)

        def powers(Pr, Pi, n, br_, bi_):
            S.add('dve', lambda e: e.memset(Pr[:, 0:32], 1.0), R, W)
            S.add('dve', lambda e: e.memset(Pi[:, 0:32], 0.0), R, W)
            for j in range(1, n):
                pr, pi = Pr[:, (j - 1) * 32:j * 32], Pi[:, (j - 1) * 32:j * 32]
                qr, qi = Pr[:, j * 32:(j + 1) * 32], Pi[:, j * 32:(j + 1) * 32]
                TT('dve', t0, pr, br_, ALU.mult, R, W)
                TT('dve', t1, pi, bi_, ALU.mult, R, W)
                TT('dve', qr, t0, t1, ALU.subtract, R, W)
                TT('dve', t0, pr, bi_, ALU.mult, R, W)
                TT('dve', t1, pi, br_, ALU.mult, R, W)
                TT('dve', qi, t0, t1, ALU.add, R, W)
        powers(PWr, PWi, 9, ar, ai)
        powers(NPr, NPi, 8, ir, ii)
        if SUB == 1:
            return
        nr = tmp(32); den = tmp(32); cr = tmp(32); ci = tmp(32)
        TS('dve', nr, ar, -1.0, None, ALU.add, None, R, W)
        TT('dve', t0, lr, lr, ALU.mult, R, W)
        TT('dve', t1, li, li, ALU.mult, R, W)
        TT('dve', den, t0, t1, ALU.add, R, W)
        S.add('dve', lambda e: e.reciprocal(den, den), R, W)
        TT('dve', t0, nr, lr, ALU.mult, R, W)
        TT('dve', t1, ai, li, ALU.mult, R, W)
        TT('dve', t0, t0, t1, ALU.add, R, W)
        TT('dve', cr, t0, den, ALU.mult, R, W)
        TT('dve', t0, ai, lr, ALU.mult, R, W)
        TT('dve', t1, nr, li, ALU.mult, R, W)
        TT('dve', t0, t0, t1, ALU.subtract, R, W)
        TT('dve', ci, t0, den, ALU.mult, R, W)
        v3 = lambda a: a.rearrange("p (g h) -> p g h", h=16)
        bc = lambda a: a.unsqueeze(2).broadcast_to([128, 32, 16])
        bre = v3(prm[:, C_BRE:C_BRE + 512]); bim = v3(prm[:, C_BIM:C_BIM + 512])
        cre = v3(prm[:, C_CRE:C_CRE + 512]); cim = v3(prm[:, C_CIM:C_CIM + 512])
        bbr = v3(tmp(512)); bbi = v3(tmp(512)); u0 = v3(tmp(512)); u1 = v3(tmp(512))
        TT('dve', u0, bre, bc(cr), ALU.mult, R, W)
        TT('dve', u1, bim, bc(ci), ALU.mult, R, W)
        TT('dve', bbr, u0, u1, ALU.subtract, R, W)
        TT('dve', u0, bim, bc(cr), ALU.mult, R, W)
        TT('dve', u1, bre, bc(ci), ALU.mult, R, W)
        TT('dve', bbi, u0, u1, ALU.add, R, W)
        G1 = R1.rearrange("p (r g s h) -> p r g s h", r=2, g=32, s=8)
        G2 = R2.rearrange("p (r g s h) -> p r g s h", r=2, g=32, s=8)
        t5w = R5.bitcast(F32)
        w0 = t5w[:, 0:2048].rearrange("p (g s h) -> p g s h", g=32, s=4)
        w1 = t5w[:, 2048:4096].rearrange("p (g s h) -> p g s h", g=32, s=4)
        bs = lambda a_: a_.unsqueeze(2).broadcast_to([128, 32, 4, 16])
        ps_ = lambda tab, s0: tab.rearrange("p (s g) -> p g s", g=32)[:, :, s0:s0 + 4].unsqueeze(3).broadcast_to([128, 32, 4, 16])
        RW5 = R + ['R5']
        for s0 in (0, 4):
            nr_, ni_ = ps_(NPr[:, 0:256], s0), ps_(NPi[:, 0:256], s0)
            pr_, pi_ = ps_(PWr[:, 0:256], s0), ps_(PWi[:, 0:256], s0)
            TT('dve', w0, bs(bbr), nr_, ALU.mult, RW5, ['R5'])
            TT('dve', w1, bs(bbi), ni_, ALU.mult, RW5, ['R5'])
            TT('dve', G1[:, 0, :, s0:s0 + 4, :], w0, w1, ALU.subtract, RW5, ['R1'])
            TT('dve', w0, bs(bbr), ni_, ALU.mult, RW5 + ['R1'], ['R5'])
            TT('dve', w1, bs(bbi), nr_, ALU.mult, RW5 + ['R1'], ['R5'])
            TT('dve', G1[:, 1, :, s0:s0 + 4, :], w0, w1, ALU.add, RW5, ['R1'])
            TT('dve', w0, bs(cre), pr_, ALU.mult, RW5 + ['R1'], ['R5'])
            TT('dve', w1, bs(cim), pi_, ALU.mult, RW5 + ['R1'], ['R5'])
            TT('dve', G2[:, 0, :, s0:s0 + 4, :], w0, w1, ALU.subtract, RW5, ['R2'])
            TT('dve', w0, bs(cre), pi_, ALU.mult, RW5 + ['R2'], ['R5'])
            TT('dve', w1, bs(cim), pr_, ALU.mult, RW5 + ['R2'], ['R5'])
            TT('dve', w0, w0, w1, ALU.add, RW5, ['R5'])
            TS('dve', G2[:, 1, :, s0:s0 + 4, :], w0, -1.0, None, ALU.mult, None, RW5, ['R2'])
        if SUB == 2:
            return
        G1f = R1.rearrange("p (r g a) -> p r g a", r=2, g=32)
        G2f = R2.rearrange("p (r g a) -> p r g a", r=2, g=32)
        G2z = [hT[:].rearrange("p k t -> p (k t)"), x[:].rearrange("p k t -> p (k t)").bitcast(BF16)[:, 0:8192]]
        G2ztag = ['hT', 'x']
        for g2 in range(2):
            S.add('dve', lambda e, g2=g2: e.memset(G2z[g2], 0.0), [], [G2ztag[g2]])
            CP('dve', G2z[g2][64 * g2:64 * g2 + 64, :], R2[64 * g2:64 * g2 + 64, :], ['R2'], [G2ztag[g2]])
            DMA('sp', tabG2_d[L, g2, :, :], G2z[g2], [G2ztag[g2]], ['tabG2'])
        G2zf = [g.rearrange("p (r g a) -> p r g a", r=2, g=32) for g in G2z]
        if SUB == 3:
            return
        Mst = R5.rearrange("p (g a) -> p g a", g=64)
        dcol = pers[:, L, C_DCOL:C_DCOL + 64]
        dd = PQ[:, 6144:6144 + 1024].rearrange("p (g a) -> p g a", g=8)
        mt = PQ[:, 7168:7168 + 1024].rearrange("p (g a) -> p g a", g=8)
        for b8 in range(8):
            pt = pp[b8 % 2]
            ptn = ppn[b8 % 2]
            for j in range(8):
                g = b8 * 8 + j
                gp, g2 = g // 2, g % 2
                for ri in range(2):
                    MM(pt[:, j * 128:(j + 1) * 128], G1f[:, ri, gp, :], G2zf[g2][:, ri, gp, :], ri == 0, ri == 1,
                       ['R1', G2ztag[g2]], [ptn])
            TT('dve', dd, identf.unsqueeze(1).broadcast_to([128, 8, 128]),
               dcol[:, b8 * 8:(b8 + 1) * 8].unsqueeze(2).broadcast_to([128, 8, 128]), ALU.mult, ['cst', 'pers'], ['PQd'])
            TT('dve', mt, pt[:].rearrange("p (g a) -> p g a", g=8),
               blkf.unsqueeze(1).broadcast_to([128, 8, 128]), ALU.mult, [ptn, 'cst'], ['PQm'])
            TT('dve', Mst[:, b8 * 8:(b8 + 1) * 8, :], mt, dd, ALU.add, ['PQm', 'PQd'], ['R5'])
        DMA('sp', tabM_d[L, :, :], R5, ['R5'], ['tabM'])
        if L == DEPTH - 1 and STAGE >= 99:
            for k in range(8):
                DMA('sp', x[:, k, :], xT_d[k * 128:(k + 1) * 128, 0:NT], ['x'], ['x'])
            held['x0'] = True
        if SUB == 4:
            return
        G1Tst = R5.rearrange("p (r g a) -> p r g a", r=2, g=32)
        for b4 in range(4):
            pt = pp[2 + b4 % 2]
            ptn = ppn[2 + b4 % 2]
            ptb = pt[:].bitcast(BF16)
            for j in range(16):
                idx = b4 * 16 + j
                ri, gp = idx // 32, idx % 32
                TR(ptb[:, j * 128:(j + 1) * 128], G1f[:, ri, gp, :], identb[:], ['R1', 'identb'], [ptn])
            CP('act', R5[:, b4 * 2048:(b4 + 1) * 2048], ptb, [ptn], ['R5'])
        DMA('sp', tabG1T_d[L, :, :], R5, ['R5'], ['tabG1T'])
        irho = tmp(32); Vr = tmp(32); Vi = tmp(32); v0 = tmp(32); v1 = tmp(32)
        RW = ['PQ', 'RHO']
        ACT(RHO[:, L, :], aa, AF.Exp, R, RW, scale=8.0)
        S.add('dve', lambda e: e.reciprocal(irho, RHO[:, L, :]), RW, W)
        TT('dve', Vr, PWr[:, 256:288], irho, ALU.mult, R, W)
        TT('dve', Vi, PWi[:, 256:288], irho, ALU.mult, R, W)
        Wc = R1.bitcast(F32).rearrange("p (g c) -> p g c", g=32)
        Ws = R2.bitcast(F32).rearrange("p (g c) -> p g c", g=32)
        t5 = R5.bitcast(F32)
        S.add('dve', lambda e: e.memset(Wc[:, :, 0:1], 1.0), ['R1'], ['R1'])
        S.add('dve', lambda e: e.memset(Ws[:, :, 0:1], 0.0), ['R2'], ['R2'])
        k = 1
        while k < 128:
            ta = t5[:, 0:32 * k].rearrange("p (g c) -> p g c", g=32)
            tb_ = t5[:, 2048:2048 + 32 * k].rearrange("p (g c) -> p g c", g=32)
            vrb = Vr.unsqueeze(2).broadcast_to([128, 32, k])
            vib = Vi.unsqueeze(2).broadcast_to([128, 32, k])
            RR_ = ['R1', 'R2', 'R5', 'PQ']
            TT('dve', ta, Wc[:, :, 0:k], vrb, ALU.mult, RR_, ['R5'])
            TT('dve', tb_, Ws[:, :, 0:k], vib, ALU.mult, RR_, ['R5'])
            TT('dve', Wc[:, :, k:2 * k], ta, tb_, ALU.subtract, RR_, ['R1'])
            TT('dve', ta, Wc[:, :, 0:k], vib, ALU.mult, RR_, ['R5'])
            TT('dve', tb_, Ws[:, :, 0:k], vrb, ALU.mult, RR_, ['R5'])
            TT('dve', Ws[:, :, k:2 * k], ta, tb_, ALU.add, RR_, ['R2'])
            TT('dve', v0, Vr, Vr, ALU.mult, R, W)
            TT('dve', v1, Vi, Vi, ALU.mult, R, W)
            TT('dve', Vi, Vr, Vi, ALU.mult, R, W)
            TS('dve', Vi, Vi, 2.0, None, ALU.mult, None, R, W)
            TT('dve', Vr, v0, v1, ALU.subtract, R, W)
            k *= 2
        CP('dve', W128[:, L, 0:32], Vr, R, ['W128'])
        CP('dve', W128[:, L, 32:64], Vi, R, ['W128'])
        st5 = R5.rearrange("p (t g c) -> p t g c", t=2, g=32)
        rb = RHO[:, L, :].unsqueeze(2).broadcast_to([128, 32, 128])
        TT('dve', st5[:, 0], Wc, rb, ALU.mult, ['R1', 'RHO'], ['R5'])
        TT('dve', st5[:, 1], Ws, rb, ALU.mult, ['R2', 'RHO'], ['R5'])
        DMA('sp', tabW_d[L, 0:2, :, :].rearrange("t p n -> p t n"), R5.rearrange("p (t n) -> p t n", t=2), ['R5'], ['tabW'])
        CP('dve', st5[:, 0], Wc, ['R1'], ['R5'])
        CP('act', st5[:, 1], Ws, ['R2'], ['R5'])
        DMA('sp', tabW_d[L, 2:4, :, :].rearrange("t p n -> p t n"), R5.rearrange("p (t n) -> p t n", t=2), ['R5'], ['tabW'])

    final = []

    dbgst = sb("dbgst", [128, 256]) if dbg_d is not None else None

    def dump(tile_ap, col0, ncols, tags):
        for c in range(0, ncols, 256):
            n = min(256, ncols - c)
            CP('dve', dbgst[:, 0:n], tile_ap[:, c:c + n], tags, ['dbgst'])
            final.append(DMA('sp', dbg_d[:, col0 + c:col0 + c + n], dbgst[:, 0:n], ['dbgst'], ['dbg']))

    if STAGE >= 1:
        for L in range(DEPTH if STAGE > 1 else 1):
            S.mark(f'tables{L}')
            gen_tables(L)
    if STAGE == 1 and SUB != 5:
        DMA('sp', R1, tabM_d[0, :, :], ['tabM', 'R1'], ['R1'])
        dump(R1, 0, 8192, ['R1'])
        DMA('sp', R2, tabG1T_d[0, :, :], ['tabG1T', 'R2'], ['R2'])
        dump(R2, 8192, 8192, ['R2'])
        DMA('sp', R5, tabG2_d[0, 0, :, :], ['tabG2', 'R5'], ['R5'])
        dump(R5, 16384, 8192, ['R5'])
        dump(T1[:, 0, :], 24576, 64, ['T12'])
        dump(T2[:, 0, :], 24640, 64, ['T12'])

    epsT = sb("epsT", [128, 1])
    S.add('dve', lambda e: e.memset(epsT[:], EPS), [], ['epsT'])

    def rms_generic(src, dst, gain_col0, L, sq, src_tags, dst_tags, sq_tags, extra_w=()):
        hs = [slice(0, 512), slice(512, 1024)]
        sqp = sq_tags[0].split('/')[0]
        dstp = dst_tags[0].split('/')[0]
        inplace = (dst_tags == src_tags)
        first = [True]

        def sqw(k, nh):
            t = [f'{sqp}/sq{k}n{nh}']
            if first[0]:
                first[0] = False
                t = [sqp] + t
            return t
        for nh in range(2):
            for k in range(8):
                if k % 2 == 0:
                    ACT(sq[:, k, hs[nh]], src[:, k, hs[nh]], AF.Square, src_tags, sqw(k, nh))
                else:
                    TT('dve', sq[:, k, hs[nh]], src[:, k, hs[nh]], src[:, k, hs[nh]], ALU.mult, src_tags, sqw(k, nh))
        for nh in range(2):
            for k in range(8):
                MM(pp[0][:, hs[nh]], onesb[:], sq[:, k, hs[nh]], k == 0, k == 7, [f'{sqp}/sq{k}n{nh}', 'onesb'], [f'pp0/n{nh}'])
        for nh in range(2):
            ACT(rs[:, hs[nh]], pp[0][:, hs[nh]], AF.Ln, [f'pp0/n{nh}', 'epsT'], [f'rs/n{nh}'], bias=epsT[:], scale=1.0 / 1024)
        for nh in range(2):
            ACT(rstd[:, hs[nh]], rs[:, hs[nh]], AF.Exp, [f'rs/n{nh}'], [f'rstd/n{nh}'], scale=-0.5)
        firstd = [True]
        for nh in range(2):
            for k in range(8):
                g = pers[:, L, gain_col0 + k:gain_col0 + k + 1]
                wt = [f'{dstp}/o{k}n{nh}']
                if firstd[0]:
                    firstd[0] = False
                    wt = [dstp] + wt
                STT(dst[:, k, hs[nh]], src[:, k, hs[nh]], g, rstd[:, hs[nh]], ALU.mult, ALU.mult,
                    src_tags + ['pers', f'rstd/n{nh}'], wt + list(extra_w))

    def rmsnorm_to(dst_bf, gain_col0, L, src=x, tagw='hT'):
        rms_generic(src, dst_bf, gain_col0, L, R1.rearrange("p (k t) -> p k t", k=8), ['x'], [tagw], ['R1'])

    def load_w(src_ap, kt, ncols, eng='pool'):
        i = next_wb()
        view = wbuf[i][:, 0:kt * ncols].rearrange("p (k c) -> p k c", k=kt)
        DMA(eng, view, src_ap.rearrange("(k p) c -> p k c", p=128), [f'wb{i}'], [f'wb{i}'])
        return view, f'wb{i}'

    def load_tab(src_ap, ncols):
        i = next_wb()
        view = wbuf[i][:, 0:ncols]
        DMA('sp', view, src_ap, [f'wb{i}', 'tabM', 'tabG1T', 'tabW'], [f'wb{i}'])
        return view, f'wb{i}'

    pp_rr = [0]

    def next_pp():
        i = pp_rr[0] % 4
        pp_rr[0] += 1
        return pp[i], ppn[i]

    ev_rr = [0]

    def evac_eng():
        ev_rr[0] += 1
        return 'act' if ev_rr[0] % 2 else 'dve'

    def prefetch_w(w_d, L, col0, ncols_total, kt, blk=512, row0=0, nblocks=1):
        out = []
        c = 0
        while c < ncols_total and len(out) < nblocks:
            nb = min(blk, ncols_total - c)
            out.append(load_w(w_d[L, row0:row0 + kt * 128, col0 + c:col0 + c + nb], kt, nb))
            c += nb
        return out

    def proj_fm(w_d, L, col0, ncols_total, kt, rhs_tile, rhs_tag, consume, blk=512, row0=0, pre=None):
        c = 0
        pre = list(pre or [])
        while c < ncols_total:
            nb = min(blk, ncols_total - c)
            if pre:
                wv, wtag = pre.pop(0)
            else:
                wv, wtag = load_w(w_d[L, row0:row0 + kt * 128, col0 + c:col0 + c + nb], kt, nb)
            for m0 in range(0, nb, 128):
                ms = min(128, nb - m0)
                pt, ptn = next_pp()
                for n in range(2):
                    for k in range(kt):
                        MM(pt[0:ms, n * 512:(n + 1) * 512], wv[:, k, m0:m0 + ms],
                           rhs_tile[:, k, n * 512:(n + 1) * 512], k == 0, k == kt - 1,
                           [wtag] + (rhs_tag if isinstance(rhs_tag, list) else [rhs_tag]), [ptn])
                consume((c + m0) // 128, pt, ptn, ms)
            c += nb

    claimed = {}
    ph_no = [0]

    def ct(parent, name, phase):
        key = (parent, phase)
        if key not in claimed:
            claimed[key] = True
            return [parent, f'{parent}/{name}']
        return [f'{parent}/{name}']

    def s5_block(L, wvs):
        ph_no[0] += 1
        for i in range(48):
            TS('pool', cdiag[:, i, :], identb[:], pers[:, L, C_CW + i:C_CW + i + 1], None, ALU.mult, None,
               ['identb', 'pers'], ['cdiag'])
        Utm2 = R1.rearrange("p (g s h) -> p g s h", g=64, s=8)
        hs = hT[:].rearrange("p k (c s) -> p k s c", s=8)
        for s in range(8):
            pt, ptn = next_pp()
            for nb in range(2):
                wv, wtag = wvs[nb]
                for k in range(8):
                    MM(pt[:, nb * 512:(nb + 1) * 512], hs[:, k, s, :], wv[:, k, :], k == 0, k == 7,
                       ['hT', wtag], [ptn])
            CP(evac_eng(), Utm2[:, :, s, :], pt[:].rearrange("p (g h) -> p g h", h=16), [ptn], ct('R1', f'u{s}', ('u', L, ph_no[0])))
        S.mark('s5.tr')
        for b4 in range(4):
            pt, ptn = next_pp()
            ptb = pt[:].bitcast(BF16)
            for j in range(16):
                g = b4 * 16 + j
                TR(ptb[:, j * 128:(j + 1) * 128], R1[:, g * 128:(g + 1) * 128], identb[:], ['R1', 'identb'], [ptn])
            CP(evac_eng(), R2[:, b4 * 2048:(b4 + 1) * 2048], ptb, [ptn], ct('R2', f'ug{b4}', ('ug', L, ph_no[0])))
        Ug = R2.rearrange("p (g c) -> p g c", g=64)
        S.mark('s5.yintra')
        Ytm = R1.rearrange("p (l g h) -> p l g h", l=8, g=64)
        for q in range(4):
            tv, ttag = load_tab(tabM_d[L, :, q * 2048:(q + 1) * 2048], 2048)
            for half8 in range(2):
                pt, ptn = next_pp()
                for j in range(8):
                    gl = half8 * 8 + j
                    g = q * 16 + gl
                    MM(pt[:, j * 128:(j + 1) * 128], Ug[:, g, :], tv[:, gl * 128:(gl + 1) * 128], True, True,
                       ['R2', ttag], [ptn])
                g0 = q * 16 + half8 * 8
                CP(evac_eng(), Ytm[:, :, g0:g0 + 8, :].rearrange("p l g h -> p g l h"),
                   pt[:].rearrange("p (g l h) -> p g l h", g=8, l=8), [ptn], ct('R1', f'y{g0}', ('yi', L, ph_no[0])))
        S.mark('s5.P')
        Pq = PQ[:, 0:2 * 32 * 129].rearrange("p (r g c) -> p r g c", r=2, g=32)
        CP('dve', Pq[:, :, :, 0], qcar[:, L, :].rearrange("p (r g) -> p r g", r=2), ['qcar'], ['PQ'])
        for ri in range(2):
            for q in range(2):
                tv, ttag = load_tab(tabG1T_d[L, :, ri * 4096 + q * 2048:ri * 4096 + (q + 1) * 2048], 2048)
                for b in range(4):
                    pt, ptn = next_pp()
                    for j in range(4):
                        gpl = b * 4 + j
                        gp = q * 16 + gpl
                        for g2 in range(2):
                            g = gp * 2 + g2
                            MM(pt[64 * g2:64 * g2 + 64, j * 128:(j + 1) * 128],
                               tv[:, gpl * 128 + g2 * 64:gpl * 128 + g2 * 64 + 64], Ug[:, g, :], True, True,
                               ['R2', ttag], [ptn])
                    gp0 = q * 16 + b * 4
                    CP(evac_eng(), Pq[:, ri, gp0:gp0 + 4, 1:129],
                       pt[:, 0:512].rearrange("p (g c) -> p g c", g=4), [ptn], [f'PQ/P{ri}_{gp0}'])
        S.mark('s5.scan')
        Tc_v, Tc_t = load_tab(tabW_d[L, 0, :, :], 4096)
        Ts_v, Ts_t = load_tab(tabW_d[L, 1, :, :], 4096)
        g3 = lambda a_: a_.rearrange("p (g c) -> p g c", g=32)
        R2f = R2.bitcast(F32)
        Ra = R2f[:, 0:2048].rearrange("p (g c) -> p g c", g=16)
        Rb = R2f[:, 2048:4096].rearrange("p (g c) -> p g c", g=16)
        R5f_ = R5.bitcast(F32)
        Ra1 = R5f_[:, 0:2048].rearrange("p (g c) -> p g c", g=16)
        Rb1 = R5f_[:, 2048:4096].rearrange("p (g c) -> p g c", g=16)
        for hg in (1, 0):
            gs = slice(16 * hg, 16 * hg + 16)
            Pr = Pq[:, 0, gs, 1:129]
            Pi = Pq[:, 1, gs, 1:129]
            Tc = g3(Tc_v)[:, gs, :]
            Ts = g3(Ts_v)[:, gs, :]
            en = 'dve'
            ra, rb, rt, pt_ = (Ra1, Rb1, 'R5', 'PQ') if hg == 1 else (Ra, Rb, 'R2', 'PQ')
            TT(en, ra, Pr, Tc, ALU.mult, [pt_, Tc_t], [rt])
            TT(en, rb, Pi, Ts, ALU.mult, [pt_, Ts_t], [rt])
            TT(en, ra, ra, rb, ALU.add, [rt], [rt])
            TT(en, rb, Pi, Tc, ALU.mult, [pt_, Tc_t, rt], [rt])
            TT(en, Pr, Pr, Ts, ALU.mult, [pt_, Ts_t], [pt_])
            TT(en, rb, rb, Pr, ALU.subtract, [rt, pt_], [rt])
        for hg in range(2):
            Ra, Rb = (Ra, Rb) if hg == 0 else (Ra1, Rb1)
            for j in range(16):
                gp = 16 * hg + j
                for ri, src in ((0, Ra), (1, Rb)):
                    S.add('dve', lambda e, ri=ri, gp=gp, src=src, j=j: e.tensor_tensor_scan(
                        Pq[:, ri, gp, 1:129], RHO[:, L, gp:gp + 1].broadcast_to([128, 128]), src[:, j, :],
                        Pq[:, ri, gp, 0:1], ALU.mult, ALU.add), ['R2', 'R5', 'PQ', 'RHO'], ['PQ'])
        r128 = Pq[:, :, :, 128]
        w128 = W128[:, L, :].rearrange("p (r g) -> p r g", r=2)
        sm = smalls[:, 0:64].rearrange("p (r g) -> p r g", r=2)
        qc = qcar[:, L, :].rearrange("p (r g) -> p r g", r=2)
        RS = ['PQ', 'W128', 'smalls']
        TT('dve', sm[:, 0, :], r128[:, 0, :], w128[:, 0, :], ALU.mult, RS, ['smalls'])
        TT('dve', sm[:, 1, :], r128[:, 1, :], w128[:, 1, :], ALU.mult, RS, ['smalls'])
        TT('dve', qc[:, 0, :], sm[:, 0, :], sm[:, 1, :], ALU.subtract, RS, ['qcar'])
        TT('dve', sm[:, 0, :], r128[:, 0, :], w128[:, 1, :], ALU.mult, RS + ['qcar'], ['smalls'])
        TT('dve', sm[:, 1, :], r128[:, 1, :], w128[:, 0, :], ALU.mult, RS + ['qcar'], ['smalls'])
        TT('dve', qc[:, 1, :], sm[:, 0, :], sm[:, 1, :], ALU.add, RS, ['qcar'])
        Wc_v, Wc_t = load_tab(tabW_d[L, 2, :, :], 4096)
        Ws_v, Ws_t = load_tab(tabW_d[L, 3, :, :], 4096)
        Qb = R2.rearrange("p (r g c) -> p r g c", r=2, g=32)
        R5f = R5.bitcast(F32)
        tA = R5f[:, 0:2048].rearrange("p (g c) -> p g c", g=16)[:, :, 0:127]
        tB = R5f[:, 2048:4096].rearrange("p (g c) -> p g c", g=16)[:, :, 0:127]
        CP('act', Qb[:, :, :, 0], Pq[:, :, :, 0], ['PQ'], ['R2'])
        for hg in range(2):
            gs = slice(16 * hg, 16 * hg + 16)
            rr = Pq[:, 0, gs, 1:128]
            rim = Pq[:, 1, gs, 1:128]
            wc = g3(Wc_v)[:, gs, 1:128]
            ws = g3(Ws_v)[:, gs, 1:128]
            TT('dve', tA, rim, ws, ALU.mult, ['PQ', Ws_t, 'R5'], ['R5'])
            TT('dve', tB, rr, wc, ALU.mult, ['PQ', Wc_t, 'R5'], ['R5'])
            TT('dve', Qb[:, 0, gs, 1:128], tB, tA, ALU.subtract, ['R5'], ['R2'])
            TT('dve', tA, rr, ws, ALU.mult, ['PQ', Ws_t, 'R5'], ['R5'])
            TT('dve', tB, rim, wc, ALU.mult, ['PQ', Wc_t, 'R5'], ['R5'])
            TT('dve', Qb[:, 1, gs, 1:128], tA, tB, ALU.add, ['R5'], ['R2'])
        for q in range(4):
            tvs = []
            for g2 in range(2):
                i = next_wb()
                view = wbuf[i][:, 0:2048].rearrange("p (r g a) -> p r g a", r=2, g=8)
                DMA('sp', view, tabG2_d[L, g2, :, :].rearrange("p (r g a) -> p r g a", r=2, g=32)[:, :, q * 8:(q + 1) * 8, :],
                    [f'wb{i}', 'tabG2'], [f'wb{i}'])
                tvs.append((view, f'wb{i}'))
            for half8 in range(2):
                pt, ptn = next_pp()
                for j in range(8):
                    gl = half8 * 8 + j
                    g = q * 16 + gl
                    gp, g2 = g // 2, g % 2
                    gpl = gp - q * 8
                    tv, ttag = tvs[g2]
                    for ri in range(2):
                        MM(pt[:, j * 128:(j + 1) * 128], Qb[:, ri, gp, :], tv[:, ri, gpl, :], ri == 0, ri == 1,
                           ['R2', ttag], [ptn])
                g0 = q * 16 + half8 * 8
                yv = Ytm[:, :, g0:g0 + 8, :].rearrange("p l g h -> p g l h")
                TT('dve', yv, yv, pt[:].rearrange("p (g l h) -> p g l h", g=8, l=8), ALU.add, [ptn, f'R1/y{g0}'], [f'R1/y{g0}'])
        pre_glu = prefetch_w(w_glu_d, L, 0, 1024, 8, nblocks=1)
        ACT(R1, R1, AF.Gelu_apprx_tanh, ['R1', 'PQ'], ['R1'])
        S.mark('s5.gelu_tr')
        gT = PQ[:, 4096:8192].bitcast(BF16).rearrange("p (k t) -> p k t", k=8)
        Yf = R1.rearrange("p (l k c) -> p l k c", l=8, k=8)
        for k in range(8):
            pt, ptn = next_pp()
            ptb = pt[:].bitcast(BF16)
            for l in range(8):
                TR(ptb[:, l * 128:(l + 1) * 128], Yf[:, l, k, :], identb[:], ['R1', 'identb'], [ptn])
            CP(evac_eng(), gT[:, k, :].rearrange("p (c s) -> p s c", s=8),
               ptb[:, 0:1024].rearrange("p (s c) -> p s c", s=8), [ptn], ct('PQ', f'gT{k}', ('gT', L, ph_no[0])))
        S.mark('s5.glu')
        yaT = R2.rearrange("p (k t) -> p k t", k=8)
        sg = sgt[:]

        def glu_consume(m, pt, ptn, ms):
            ACT(sg, pt[:], AF.Sigmoid, [ptn, 'pers'], ['sgt'],
                bias=pers[:, L, C_BGLU + m:C_BGLU + m + 1])
            TT('dve', yaT[:, m, :], gT[:, m, :], sg, ALU.mult, ['PQ', 'sgt'], ['R2'])
        proj_fm(w_glu_d, L, 0, 1024, 8, gT, 'PQ', glu_consume, pre=pre_glu)
        held['xbc'] = prefetch_w(w_in_d, L, 2048, 1536, 8, nblocks=1)

    def norm_inplace(yT, tag, gain_col0, L, sq=None, sq_tag='PQ'):
        if sq is None:
            sq = PQ[:, 4096:8192].bitcast(BF16).rearrange("p (k t) -> p k t", k=8)
        rms_generic(yT, yT, gain_col0, L, sq, [tag], [tag], [sq_tag])

    def ssd_block(L):
        xraw = PQ[:, 0:6168].bitcast(BF16)[:, 0:12 * 1027].rearrange("p (m t) -> p m t", m=12)
        CP('dve', xraw[:, :, 0:3], ctail[:, L, :, :], ['ctail', 'PQ'], ['PQ'])

        def xbc_consume(m, pt, ptn, ms):
            CP(evac_eng(), xraw[:, m, 3:3 + NT], pt[:], [ptn], [f'PQ/xr{m}'])
        proj_fm(w_in_d, L, 2048, 1536, 8, hT, 'hT', xbc_consume, pre=held.pop('xbc', None))
        CP('dve', ctail[:, L, :, :], xraw[:, :, NT:NT + 3], ['PQ'], ['ctail'])
        norm_inplace(R2.rearrange("p (k t) -> p k t", k=8), 'R2', C_GS5, L,
                     sq=R1.rearrange("p (k t) -> p k t", k=8), sq_tag='R1')
        S.mark('ssd.conv')
        xsT = R1.rearrange("p (k t) -> p k t", k=8)
        for m in range(12):
            pt, ptn = next_pp()
            for n in range(2):
                for tap in range(4):
                    MM(pt[:, n * 512:(n + 1) * 512], cdiag[:, m * 4 + tap, :],
                       xraw[:, m, tap + n * 512:tap + n * 512 + 512], tap == 0, tap == 3, [f'PQ/xr{m}', 'cdiag'], [ptn])
            dst = xsT[:, m, :] if m < 8 else BCT[:, m - 8, :]
            ACT(dst, pt[:], AF.Silu, [ptn, 'pers'], ['R1' if m < 8 else 'BCT'],
                bias=pers[:, L, C_CB + m:C_CB + m + 1])
        S.mark('ssd.dt')
        wv, wtag = load_w(w_in_d[L, :, 3584:3600], 8, 16)
        pt, ptn = next_pp()
        for n in range(2):
            for k in range(8):
                MM(pt[0:16, n * 512:(n + 1) * 512], wv[:, k, :], hT[:, k, n * 512:(n + 1) * 512], k == 0, k == 7,
                   [wtag, 'hT'], [ptn])
        o = [0]

        def tq(n, name):
            a_ = PQ[:, o[0]:o[0] + n]
            o[0] += n
            return a_, 'PQ/' + name
        e1 = rs
        dtp = rstd
        dta = rs
        nA, nA_t = tq(1, 'nA')
        S.add('dve', lambda e: e.memset(nA[:, 0:1], 0.0), [], ['PQ'])
        ACT(e1[0:16, :], pt[0:16, :], AF.Exp, [ptn, 'pers'], ['rs'], bias=pers[0:16, L, C_DTB:C_DTB + 1])
        ACT(dtp[0:16, :], e1[0:16, :], AF.Ln, ['rs'], ['rstd'], bias=1.0)
        ACT(nA[0:16, :], pers[0:16, L, C_ALOG:C_ALOG + 1], AF.Exp, ['pers'], [nA_t])
        TS('dve', dta[0:16, :], dtp[0:16, :], nA[0:16, 0:1], -1.0, ALU.mult, ALU.mult, ['rstd', nA_t], ['rs'])
        dtp_tm, dtp_tm_t = tq(128, 'dtp_tm')
        dta_tm, dta_tm_t = tq(128, 'dta_tm')
        pt2, ptn2 = next_pp()
        for c in range(8):
            TR(pt2[:, c * 16:(c + 1) * 16], dtp[0:16, c * 128:(c + 1) * 128], identf[0:16, 0:16], ['rstd', 'cst'], [ptn2])
            TR(pt2[:, 128 + c * 16:128 + (c + 1) * 16], dta[0:16, c * 128:(c + 1) * 128], identf[0:16, 0:16],
               ['rs', 'cst'], [ptn2])
        CP('dve', dtp_tm, pt2[:, 0:128], [ptn2], [dtp_tm_t])
        CP('dve', dta_tm, pt2[:, 128:256], [ptn2], [dta_tm_t])
        acum_tm, acum_t = tq(128, 'acum')
        nacum, nacum_t = tq(128, 'nacum')
        dec_tm, dec_t = tq(128, 'dec')
        dtd_tm, dtd_t = tq(128, 'dtd')
        alast, alast_t = tq(128, 'alast')
        edec, edec_t = tq(128, 'edec')
        pt3, ptn3 = next_pp()
        MM(pt3[:, 0:128], trif, dta_tm, True, True, [dta_tm_t, 'cst'], [ptn3])
        MM(pt3[:, 128:256], cst[:, K_ONE:K_ONE + 128], dta_tm, True, True, [dta_tm_t, 'cst'], [ptn3])
        CP('dve', acum_tm, pt3[:, 0:128], [ptn3], [acum_t])
        TS('dve', nacum, acum_tm, -1.0, None, ALU.mult, None, [acum_t], [nacum_t])
        CP('dve', alast, pt3[:, 128:256], [ptn3], [alast_t])
        TT('dve', dec_tm, alast, acum_tm, ALU.subtract, [alast_t, acum_t], [dec_t])
        ACT(dec_tm, dec_tm, AF.Exp, [dec_t], [dec_t])
        TT('dve', dtd_tm, dec_tm, dtp_tm, ALU.mult, [dec_t, dtp_tm_t], [dtd_t])
        ACT(edec, alast, AF.Exp, [alast_t], [edec_t])
        dtri_l, dtri_tl = [], []
        for g in range(2):
            a_, t_ = tq(1024, f'dtri{g}')
            dtri_l.append(a_); dtri_tl.append(t_)
        bfb = PQ[:, o[0]:8448].bitcast(BF16)
        ob = [0]

        def tb(n, name):
            a_ = bfb[:, ob[0]:ob[0] + n]
            ob[0] += n
            return a_, 'PQ/' + name
        xdt, xdt_t = tb(1024, 'xdt')
        xdtd, xdtd_t = tb(1024, 'xdtd')
        Btm, Btm_t = tb(256, 'Btm')
        stb, stb_t = tb(1024, 'stb')
        LTl = [tb(1024, f'LT{g}') for g in range(2)]
        EEl = [tb(1024, f'EE{g}') for g in range(2)]
        CBl = [tb(128, f'CBm{g}') for g in range(2)]
        pre_z = prefetch_w(w_in_d, L, 1024, 1024, 8, nblocks=2)
        S.mark('ssd.chunks')
        ybT = R5.rearrange("p (k t) -> p k t", k=8)
        ones_f = cst[:, K_ONE:K_ONE + 128]
        v8 = lambda a_: a_.rearrange("p (h l) -> p h l", h=8)
        for c in range(8):
            cs_ = slice(c * 128, (c + 1) * 128)
            ptx, ptxn = next_pp()
            ptxb = ptx[:].bitcast(BF16)
            for k in range(8):
                TR(ptxb[:, k * 128:(k + 1) * 128], xsT[:, k, cs_], identb[:], ['R1', 'identb'], [ptxn])
            for g in range(2):
                TR(ptxb[:, 1024 + g * 128:1024 + (g + 1) * 128], BCT[:, g, cs_], identb[:], ['BCT', 'identb'], [ptxn])
            bch = lambda a_: a_.unsqueeze(2).broadcast_to([128, 16, 64])
            for g in range(2):
                for h8 in range(8):
                    hcol = c * 16 + g * 8 + h8
                    S.add('act', lambda e, g=g, h8=h8, hcol=hcol: e.activation(
                        dtri_l[g][:, h8 * 128:(h8 + 1) * 128], trif, AF.Copy, scale=dta_tm[:, hcol:hcol + 1]),
                        [dta_tm_t, 'cst'], [dtri_tl[g]])
            TT('dve', xdt.rearrange("p (h q) -> p h q", h=16), ptxb[:, 0:1024].rearrange("p (h q) -> p h q", h=16),
               bch(dtp_tm[:, c * 16:(c + 1) * 16]), ALU.mult, [ptxn, dtp_tm_t], [xdt_t])
            TT('dve', xdtd.rearrange("p (h q) -> p h q", h=16), xdt.rearrange("p (h q) -> p h q", h=16),
               bch(dec_tm[:, c * 16:(c + 1) * 16]), ALU.mult, [xdt_t, dec_t], [xdtd_t])
            CP('act', Btm, ptxb[:, 1024:1280], [ptxn], [Btm_t])
            CP('act', stb, sstate[:, L, :], ['sstate'], [stb_t])
            pls = []
            pc, pcn = next_pp()
            for g in range(2):
                pl, pln = next_pp()
                pls.append((pl, pln))
                for hb in range(2):
                    MM(pl[:, hb * 512:(hb + 1) * 512], ones_f, dtri_l[g][:, hb * 512:(hb + 1) * 512], True, True,
                       [dtri_tl[g], 'cst'], [pln])
                MM(pc[:, g * 128:(g + 1) * 128], BCT[:, g, cs_], BCT[:, 2 + g, cs_], True, True, ['BCT'], [pcn])
            for g in range(2):
                ACT(EEl[g][0], pls[g][0][:], AF.Exp, [pls[g][1]], [EEl[g][1]])
            for g in range(2):
                pl, pln = pls[g]
                TT('dve', v8(pl[:]), v8(pl[:]),
                   acum_tm[:, c * 16 + g * 8:c * 16 + g * 8 + 8].unsqueeze(2).broadcast_to([128, 8, 128]), ALU.min,
                   [pln, acum_t], [pln])
                TT('dve', v8(pl[:]), v8(pl[:]),
                   nacum[:, c * 16 + g * 8:c * 16 + g * 8 + 8].unsqueeze(2).broadcast_to([128, 8, 128]), ALU.add,
                   [pln, nacum_t], [pln])
                TT('dve', CBl[g][0], pc[:, g * 128:(g + 1) * 128], trif, ALU.mult, [pcn, 'cst'], [CBl[g][1]])
            for g in range(2):
                ACT(LTl[g][0], pls[g][0][:], AF.Exp, [pls[g][1]], [LTl[g][1]])
            for g in range(2):
                TT('dve', v8(EEl[g][0]), v8(EEl[g][0]), BCT[:, 2 + g, cs_].unsqueeze(1).broadcast_to([128, 8, 128]), ALU.mult,
                   [EEl[g][1], 'BCT'], [EEl[g][1]])
                TT('dve', v8(LTl[g][0]), v8(LTl[g][0]), CBl[g][0].unsqueeze(1).broadcast_to([128, 8, 128]), ALU.mult,
                   [LTl[g][1], CBl[g][1]], [LTl[g][1]])
            for g in range(2):
                py, pyn = next_pp()
                MT, MT_t = LTl[g]
                CE, CE_t = EEl[g]
                for kk in range(4):
                    k = g * 4 + kk
                    for hh in range(2):
                        h = 2 * k + hh
                        h8 = h % 8
                        o_ = py[64 * hh:64 * hh + 64, kk * 128:(kk + 1) * 128]
                        MM(o_, xdt[:, h * 64:(h + 1) * 64], MT[:, h8 * 128:(h8 + 1) * 128], True, False, [xdt_t, MT_t], [pyn])
                        MM(o_, stb[:, h * 64:(h + 1) * 64], CE[:, h8 * 128:(h8 + 1) * 128], False, True, [stb_t, CE_t], [pyn])
                for kk in range(4):
                    k = g * 4 + kk
                    STT(ybT[:, k, cs_], xsT[:, k, cs_], pers[:, L, C_SD + k:C_SD + k + 1], py[:, kk * 128:(kk + 1) * 128],
                        ALU.mult, ALU.add, ['R1', 'pers', pyn], ['R5'])
            ps_, psn = next_pp()
            for g in range(2):
                MM(ps_[:, g * 512:(g + 1) * 512], Btm[:, g * 128:(g + 1) * 128], xdtd[:, g * 512:(g + 1) * 512], True, True,
                   [Btm_t, xdtd_t], [psn])
            sv = sstate[:, L, :].rearrange("p (h q) -> p h q", h=16)
            TT('pool', sv, sv, edec[:, c * 16:(c + 1) * 16].unsqueeze(2).broadcast_to([128, 16, 64]),
               ALU.mult, ['sstate', edec_t], ['sstate'])
            TT('dve', sstate[:, L, :], sstate[:, L, :], ps_[:, :], ALU.add, ['sstate', psn], ['sstate'])
        zs = sgt[:]

        def z_consume(m, pt, ptn, ms):
            ACT(zs, pt[:], AF.Silu, [ptn], ['sgt'])
            TT('dve', ybT[:, m, :], ybT[:, m, :], zs, ALU.mult, ['sgt', 'R5'], ['R5'])
        proj_fm(w_in_d, L, 1024, 1024, 8, hT, 'hT', z_consume, pre=pre_z)

    negones = sb("negones", [128, 128])
    S.add('dve', lambda e: e.memset(negones[:], -1.0), [], ['negones'])

    def out_proj(L):
        def consume(m, pt, ptn, ms):
            TT('dve', x[:, m, :], x[:, m, :], pt[:], ALU.add, [ptn, 'x'], ['x'])
        ybT_ = R5.rearrange("p (k t) -> p k t", k=8)
        yaT_ = R2.rearrange("p (k t) -> p k t", k=8)
        norm_inplace(ybT_, 'R5', C_GSSD, L)
        proj_fm(w_out_d, L, 0, 1024, 8, yaT_, 'R2', consume, blk=512, row0=0)
        proj_fm(w_out_d, L, 0, 1024, 8, ybT_, 'R5', consume, blk=512, row0=1024)

    def proj_fm_dep2(*a, **k):
        return proj_fm(*a, **k)

    def ffn(L):
        pre_g = [load_w(w_gate_d[L, :, 0:128], 8, 128), load_w(w_up_d[L, :, 0:128], 8, 128)]
        rmsnorm_to(hT, C_GFFN, L)
        aT = BIG[:, 0:22 * NT].rearrange("p (k t) -> p k t", k=22)
        sg = sgt[:]
        for m in range(22):
            if m == 0:
                (wg, wgt), (wu, wut) = pre_g
            else:
                wg, wgt = load_w(w_gate_d[L, :, m * 128:(m + 1) * 128], 8, 128)
                wu, wut = load_w(w_up_d[L, :, m * 128:(m + 1) * 128], 8, 128)
            pg, pgn = next_pp()
            pu, pun = next_pp()
            for n in range(2):
                for k in range(8):
                    MM(pg[:, n * 512:(n + 1) * 512], wg[:, k, :], hT[:, k, n * 512:(n + 1) * 512], k == 0, k == 7,
                       [wgt, 'hT'], [pgn])
            for n in range(2):
                for k in range(8):
                    MM(pu[:, n * 512:(n + 1) * 512], wu[:, k, :], hT[:, k, n * 512:(n + 1) * 512], k == 0, k == 7,
                       [wut, 'hT'], [pun])
            ACT(sg, pg[:], AF.Silu, [pgn], ['sgt'])
            TT('dve', aT[:, m, :], sg, pu[:], ALU.mult, ['sgt', pun], ['R1', 'R2', 'R5'])

        def consume(m, pt, ptn, ms):
            TT('dve', x[:, m, :], x[:, m, :], pt[:], ALU.add, [ptn, 'x'], ['x'])
        proj_fm(w_down_d, L, 0, 1024, 22, aT, ['R1', 'R2', 'R5'], consume, blk=128)

    for hf in range(NH):
        if not (hf == 0 and held.pop('x0', False)):
            for k in range(8):
                DMA('sp', x[:, k, :], xT_d[k * 128:(k + 1) * 128, hf * NT:(hf + 1) * NT], ['x'], ['x'])
        for L in range(DEPTH):
            if STAGE < 2:
                break
            S.mark(f'h{hf}L{L}.norm')
            wvs_u = [load_w(w_in_d[L, :, nb * 512:(nb + 1) * 512], 8, 512) for nb in range(2)]
            rmsnorm_to(hT, C_GMIX, L)
            if STAGE == 2:
                for k in range(8):
                    dump(hT[:, k, :], k * 1024, 1024, ['hT'])
                break
            S.mark(f'h{hf}L{L}.s5')
            s5_block(L, wvs_u)
            if STAGE == 3:
                for k in range(8):
                    dump(R2[:, k * 1024:(k + 1) * 1024], k * 1024, 1024, ['R2'])
                break
            S.mark(f'h{hf}L{L}.ssd')
            ssd_block(L)
            if STAGE == 4:
                for k in range(8):
                    dump(R5[:, k * 1024:(k + 1) * 1024], k * 1024, 1024, ['R5'])
                break
            S.mark(f'h{hf}L{L}.outproj')
            out_proj(L)
            if STAGE == 5:
                break
            S.mark(f'h{hf}L{L}.ffn')
            ffn(L)
            if STAGE == 6:
                break
        if STAGE < 99:
            for k in range(8):
                final.append(DMA('sp', out_d[k * 128:(k + 1) * 128, hf * NT:(hf + 1) * NT], x[:, k, :], ['x'], ['out']))
            break
        S.mark(f'h{hf}.final')
        if hf < NH - 1:
            ostage = BIG[:, 8192:24576].bitcast(F32).rearrange("p (k t) -> p k t", k=8)
            rms_generic(x, ostage, C_GFIN, 0, R1.rearrange("p (k t) -> p k t", k=8), ['x'], ['R2'], ['R1'], extra_w=['R5'])
            for k in range(8):
                final.append(DMA('sp', out_d[k * 128:(k + 1) * 128, hf * NT:(hf + 1) * NT], ostage[:, k, :],
                                 [f'R2/o{k}n0', f'R2/o{k}n1', 'R5'], ['out']))
        else:
            rms_generic(x, x, C_GFIN, 0, R1.rearrange("p (k t) -> p k t", k=8), ['x'], ['x'], ['R1'])
            for k in range(8):
                final.append(DMA('sp', out_d[k * 128:(k + 1) * 128, hf * NT:(hf + 1) * NT], x[:, k, :],
                                 [f'x/o{k}n0', f'x/o{k}n1'], ['out']))
    S.emit(final)


def _s5_lay(a):
    return a.reshape(32, 2, 64).transpose(1, 2, 0).reshape(128, 32)


def _consts():
    c = np.zeros((128, NCST), np.float32)
    i = np.arange(128)
    c[:, K_ID:K_ID + 128] = np.eye(128)
    c[:, K_TRI:K_TRI + 128] = (i[:, None] <= i[None, :])
    c[:, K_NEG:K_NEG + 128] = np.where(i[None, :] < i[:, None], -30000.0, 0.0)
    c[:, K_BLK:K_BLK + 128] = ((i[None, :] // 16) >= (i[:, None] // 16))
    c[:, K_ONE:K_ONE + 128] = 1.0
    return c


def _small_params(inp):
    sp = np.zeros((DEPTH, 128, NSP), np.float32)
    fm = lambda v: np.asarray(v, np.float32).reshape(8, 128).T
    for L in range(DEPTH):
        sp[L, :, C_LRE:C_LRE + 32] = _s5_lay(inp['s5_lam_re'][L])
        sp[L, :, C_LIM:C_LIM + 32] = _s5_lay(inp['s5_lam_im'][L])
        sp[L, :, C_LST:C_LST + 32] = _s5_lay(np.broadcast_to(inp['s5_log_step'][L][:, None], (64, 64)))
        for nm, c0 in (('s5_b_re', C_BRE), ('s5_b_im', C_BIM)):
            sp[L, :, c0:c0 + 512] = inp[nm][L].reshape(32, 2, 64, 16).transpose(1, 2, 0, 3).reshape(128, 512)
        for nm, c0 in (('s5_c_re', C_CRE), ('s5_c_im', C_CIM)):
            sp[L, :, c0:c0 + 512] = inp[nm][L].reshape(32, 2, 16, 64).transpose(1, 3, 0, 2).reshape(128, 512)
        P = C_P0
        sp[L, :, P + C_DCOL:P + C_DCOL + 64] = np.tile(inp['s5_d'][L].reshape(64, 16).T, (8, 1))
        sp[L, :, P + C_GMIX:P + C_GMIX + 8] = fm(inp['norm_mix'][L])
        sp[L, :, P + C_GS5:P + C_GS5 + 8] = fm(inp['s5_norm'][L])
        sp[L, :, P + C_GSSD:P + C_GSSD + 8] = fm(inp['ssd_norm'][L])
        sp[L, :, P + C_GFFN:P + C_GFFN + 8] = fm(inp['norm_ffn'][L])
        sp[L, :, P + C_GFIN:P + C_GFIN + 8] = fm(inp['norm_final'])
        sp[L, :, P + C_BGLU:P + C_BGLU + 8] = fm(inp['s5_b_glu'][L])
        sp[L, :, P + C_CW:P + C_CW + 48] = inp['ssd_conv_w'][L].T.reshape(12, 128, 4).transpose(1, 0, 2).reshape(128, 48)
        sp[L, :, P + C_CB:P + C_CB + 12] = inp['ssd_conv_b'][L].reshape(12, 128).T
        sp[L, 0:16, P + C_DTB] = inp['ssd_dt_bias'][L]
        sp[L, 0:16, P + C_ALOG] = inp['ssd_a_log'][L]
        sp[L, :, P + C_SD:P + C_SD + 8] = np.repeat(inp['ssd_d'][L], 64).reshape(8, 128).T
    return sp


def make_in_maps(inp, cores):
    inp = {k: np.asarray(v) for k, v in inp.items()}
    sp = _small_params(inp)
    cst = _consts()
    shared = dict(w_in=np.ascontiguousarray(inp['w_in'], np.float32), w_glu=np.ascontiguousarray(inp['s5_w_glu'], np.float32),
                  w_out=np.ascontiguousarray(inp['w_out'], np.float32), w_gate=np.ascontiguousarray(inp['w_gate'], np.float32),
                  w_up=np.ascontiguousarray(inp['w_up'], np.float32), w_down=np.ascontiguousarray(inp['w_down'], np.float32),
                  sp=sp, cst=cst)
    maps = []
    for b in cores:
        m = dict(shared)
        m['xT'] = np.ascontiguousarray(inp['x'][b].T, np.float32)
        maps.append(m)
    return maps


_NC = [None]


def kernel(**inputs):
    if _NC[0] is None:
        _NC[0] = build_nc()
    nc = _NC[0]
    maps = make_in_maps(inputs, list(range(8)))
    res = run_bass_kernel_spmd(nc, maps, core_ids=list(range(8)))
    out = np.stack([np.asarray(r["outT"]).T for r in res.results], 0)
    return np.ascontiguousarray(out, np.float32)
```
